# Optimizing a Trainium2 kernel written in Bass

```python
import math
import jax, jax.numpy as jnp
from jax import lax
import numpy as np

D_MODEL = 2048
BATCH = 2
SEQ = 4096
DEPTH = 2

GRID_W = 64
CTX_LEN = 256
N_MIXERS = 2
S5_GROUP_CH = 16
S5_GROUPS = D_MODEL // S5_GROUP_CH
S5_STATE = 64
S5_DT_MIN = 1e-3
S5_DT_MAX = 1e-1
SGU_CHUNK = 128
SGU_DIM = 2 * D_MODEL
SGU_HEADS = 16
SGU_HEAD_DIM = SGU_DIM // SGU_HEADS
FFN_DIM = 5632
CONV_K = 3
N_MOD = 6
RMS_EPS = 1e-6
LN_EPS = 1e-5

kernel_name = 'hybrid_s5_sgu_convffn_prefix_dit'


def _rmsnorm(x, g):
    x32 = x.astype(jnp.float32)
    y = x32 * lax.rsqrt(jnp.mean(x32 * x32, axis=-1, keepdims=True) + RMS_EPS)
    return (y * g.astype(jnp.float32)).astype(x.dtype)


def _layernorm(x, g, b):
    x32 = x.astype(jnp.float32)
    mu = jnp.mean(x32, axis=-1, keepdims=True)
    xc = x32 - mu
    y = xc * lax.rsqrt(jnp.mean(xc * xc, axis=-1, keepdims=True) + LN_EPS)
    return (y * g.astype(jnp.float32) + b.astype(jnp.float32)).astype(x.dtype)


def _modulate(h, shift, scale):
    return h * (1 + scale) + shift


def _s5_discretize(a_re, a_im, b_re, b_im, log_step):
    a_re = a_re.astype(jnp.float32); a_im = a_im.astype(jnp.float32)
    b_re = b_re.astype(jnp.float32); b_im = b_im.astype(jnp.float32)
    dt = jnp.exp(log_step.astype(jnp.float32))[:, None]
    mag = jnp.exp(a_re * dt)
    lb_re = mag * jnp.cos(a_im * dt)
    lb_im = mag * jnp.sin(a_im * dt)
    den = a_re * a_re + a_im * a_im
    k_re = ((lb_re - 1.0) * a_re + lb_im * a_im) / den
    k_im = (lb_im * a_re - (lb_re - 1.0) * a_im) / den
    bb_re = k_re[..., None] * b_re - k_im[..., None] * b_im
    bb_im = k_re[..., None] * b_im + k_im[..., None] * b_re
    return lb_re, lb_im, bb_re, bb_im


def _lin_rec_combine(e1, e2):
    a1r, a1i, b1r, b1i = e1
    a2r, a2i, b2r, b2i = e2
    return (a1r * a2r - a1i * a2i,
            a1r * a2i + a1i * a2r,
            a2r * b1r - a2i * b1i + b2r,
            a2r * b1i + a2i * b1r + b2i)


def _s5_scan(u, lb_re, lb_im, bb_re, bb_im, s0, reverse):
    x_re = jnp.einsum('blgh,gph->blgp', u, bb_re)
    x_im = jnp.einsum('blgh,gph->blgp', u, bb_im)
    if s0 is not None:
        s0_re, s0_im = s0
        t0 = -1 if reverse else 0
        x_re = x_re.at[:, t0].add(lb_re * s0_re - lb_im * s0_im)
        x_im = x_im.at[:, t0].add(lb_re * s0_im + lb_im * s0_re)
    n = u.shape[1]
    a_re = jnp.broadcast_to(lb_re, (1, n) + lb_re.shape)
    a_im = jnp.broadcast_to(lb_im, (1, n) + lb_im.shape)
    _, _, s_re, s_im = lax.associative_scan(_lin_rec_combine, (a_re, a_im, x_re, x_im), reverse=reverse, axis=1)
    return s_re, s_im


def _s5_readout(s_re, s_im, c_re, c_im):
    return jnp.einsum('blgp,ghp->blgh', s_re, c_re) - jnp.einsum('blgp,ghp->blgh', s_im, c_im)


def _half_glu(y, w_glu, b_glu, w_out, dtype):
    g = jax.nn.gelu(y).astype(dtype)
    return (g * jax.nn.sigmoid(g @ w_glu + b_glu)) @ w_out


def _s5_mixer(h_lat, h_ctx, w_in, a_re, a_im, b_re, b_im, c_re, c_im, log_step, d_skip, w_glu, b_glu, w_out, ctx_out):
    dtype = h_lat.dtype
    def to_groups(h):
        u = (h @ w_in).astype(jnp.float32)
        return u.reshape(u.shape[0], u.shape[1], S5_GROUPS, S5_GROUP_CH)
    u_lat = to_groups(h_lat)
    u_ctx = to_groups(h_ctx)
    d_g = d_skip.astype(jnp.float32).reshape(S5_GROUPS, S5_GROUP_CH)
    y_lat = d_g * u_lat
    y_ctx = d_g * u_ctx if ctx_out else None
    for d, reverse in enumerate((False, True)):
        c_r = c_re[d].astype(jnp.float32)
        c_i = c_im[d].astype(jnp.float32)
        lb_re, lb_im, bb_re, bb_im = _s5_discretize(a_re[d], a_im[d], b_re[d], b_im[d], log_step[d])
        cs_re, cs_im = _s5_scan(u_ctx, lb_re, lb_im, bb_re, bb_im, None, reverse)
        t_end = 0 if reverse else -1
        ls_re, ls_im = _s5_scan(u_lat, lb_re, lb_im, bb_re, bb_im, (cs_re[:, t_end], cs_im[:, t_end]), reverse)
        y_lat = y_lat + _s5_readout(ls_re, ls_im, c_r, c_i)
        if ctx_out:
            y_ctx = y_ctx + _s5_readout(cs_re, cs_im, c_r, c_i)
    bsz, n = h_lat.shape[0], h_lat.shape[1]
    out_lat = _half_glu(y_lat.reshape(bsz, n, D_MODEL), w_glu, b_glu, w_out, dtype)
    out_ctx = None
    if ctx_out:
        out_ctx = _half_glu(y_ctx.reshape(bsz, h_ctx.shape[1], D_MODEL), w_glu, b_glu, w_out, dtype)
    return out_lat, out_ctx


def _sgu_mixer(h, w_in, ln_g, ln_b, w_s, b_s, w_out):
    bsz, n, _ = h.shape
    z = jax.nn.gelu(h @ w_in)
    u, v = jnp.split(z, 2, axis=-1)
    v = _layernorm(v, ln_g, ln_b).reshape(bsz, n // SGU_CHUNK, SGU_CHUNK, SGU_HEADS, SGU_HEAD_DIM)
    sv = jnp.einsum('gqk,bnkgc->bnqgc', w_s, v) + b_s.T[:, :, None]
    return (u * sv.reshape(bsz, n, SGU_DIM)) @ w_out


def _dwconv3x3(z, w, b):
    r, wd = z.shape[1], z.shape[2]
    zp = jnp.pad(z, ((0, 0), (1, 1), (1, 1), (0, 0)))
    w = w.astype(z.dtype)
    out = b.astype(z.dtype) + zp[:, 1:1 + r, 1:1 + wd] * w[1, 1]
    for di in range(CONV_K):
        for dj in range(CONV_K):
            if di == 1 and dj == 1:
                continue
            out = out + zp[:, di:di + r, dj:dj + wd] * w[di, dj]
    return out


def _conv_ffn(h, rows, cols, w_up, conv_w, conv_b, w_down):
    bsz = h.shape[0]
    z = h @ w_up
    ch = z.shape[-1]
    z = _dwconv3x3(z.reshape(bsz, rows, cols, ch), conv_w, conv_b).reshape(bsz, rows * cols, ch)
    gate, val = jnp.split(z, 2, axis=-1)
    return (jax.nn.silu(gate) * val) @ w_down


def setup_inputs(seed: int = 0) -> dict:
    key = jax.random.key(seed)
    ks = iter(jax.random.split(key, 64))
    f32 = jnp.float32
    def nrm(shape, s):
        return jax.random.normal(next(ks), shape, f32) * s
    D, G, P, H = D_MODEL, S5_GROUPS, S5_STATE, S5_GROUP_CH
    F2 = 2 * FFN_DIM
    n_a = len(range(0, DEPTH, N_MIXERS))
    n_b = len(range(1, DEPTH, N_MIXERS))
    n_idx = jnp.arange(P, dtype=f32)
    return {
        'x': nrm((BATCH, SEQ, D), 1.0),
        'c': nrm((BATCH, D), 1.0),
        'ctx': nrm((BATCH, CTX_LEN, D), 1.0),
        'c_ctx': nrm((D,), 1.0),
        'mix_norm_g': 1.0 + nrm((DEPTH, D), 0.02),
        'ffn_norm_g': 1.0 + nrm((DEPTH, D), 0.02),
        'w_mod': nrm((DEPTH, D, N_MOD * D), 0.5 * D ** -0.5),
        'b_mod': nrm((DEPTH, N_MOD * D), 0.01),
        'ffn_w_up': nrm((DEPTH, D, F2), D ** -0.5),
        'ffn_conv_w': nrm((DEPTH, CONV_K, CONV_K, F2), 1.0 / CONV_K),
        'ffn_conv_b': nrm((DEPTH, F2), 0.01),
        'ffn_w_down': nrm((DEPTH, FFN_DIM, D), FFN_DIM ** -0.5),
        's5_w_in': nrm((n_a, D, D), D ** -0.5),
        's5_a_re': -0.5 * jnp.exp(nrm((n_a, 2, G, P), 0.02)),
        's5_a_im': math.pi * n_idx + nrm((n_a, 2, G, P), 0.01),
        's5_b_re': nrm((n_a, 2, G, P, H), (2 * H) ** -0.5),
        's5_b_im': nrm((n_a, 2, G, P, H), (2 * H) ** -0.5),
        's5_c_re': nrm((n_a, 2, G, H, P), P ** -0.5),
        's5_c_im': nrm((n_a, 2, G, H, P), P ** -0.5),
        's5_log_step': jax.random.uniform(next(ks), (n_a, 2, G), f32, math.log(S5_DT_MIN), math.log(S5_DT_MAX)),
        's5_d': nrm((n_a, D), 1.0),
        's5_w_glu': nrm((n_a, D, D), D ** -0.5),
        's5_b_glu': nrm((n_a, D), 0.01),
        's5_w_out': nrm((n_a, D, D), D ** -0.5),
        'sgu_w_in': nrm((n_b, D, 2 * SGU_DIM), D ** -0.5),
        'sgu_ln_g': 1.0 + nrm((n_b, SGU_DIM), 0.02),
        'sgu_ln_b': nrm((n_b, SGU_DIM), 0.01),
        'sgu_w_s': nrm((n_b, SGU_HEADS, SGU_CHUNK, SGU_CHUNK), SGU_CHUNK ** -0.5),
        'sgu_b_s': 1.0 + nrm((n_b, SGU_HEADS, SGU_CHUNK), 0.02),
        'sgu_w_out': nrm((n_b, SGU_DIM, D), SGU_DIM ** -0.5),
        'final_norm_g': 1.0 + nrm((D,), 0.02),
    }


def reference(x, c, ctx, c_ctx, mix_norm_g, ffn_norm_g, w_mod, b_mod, ffn_w_up, ffn_conv_w, ffn_conv_b, ffn_w_down,
              s5_w_in, s5_a_re, s5_a_im, s5_b_re, s5_b_im, s5_c_re, s5_c_im, s5_log_step, s5_d, s5_w_glu, s5_b_glu, s5_w_out,
              sgu_w_in, sgu_ln_g, sgu_ln_b, sgu_w_s, sgu_b_s, sgu_w_out, final_norm_g):
    n_lat = x.shape[1]
    rows = n_lat // GRID_W
    n_ctx = ctx.shape[1]
    ctx_needed_after = [any(j % N_MIXERS == 0 for j in range(i + 1, DEPTH)) for i in range(DEPTH)]
    s = ctx
    for i in range(DEPTH):
        keep_ctx = ctx_needed_after[i]
        is_s5 = i % N_MIXERS == 0
        k = i // N_MIXERS
        m_lat = jnp.split((jax.nn.silu(c) @ w_mod[i] + b_mod[i])[:, None, :], N_MOD, axis=-1)
        hl = _modulate(_rmsnorm(x, mix_norm_g[i]), m_lat[0], m_lat[1])
        if is_s5 or keep_ctx:
            m_ctx = jnp.split(jax.nn.silu(c_ctx) @ w_mod[i] + b_mod[i], N_MOD, axis=-1)
            hc = _modulate(_rmsnorm(s, mix_norm_g[i]), m_ctx[0], m_ctx[1])
        if is_s5:
            ml, mc = _s5_mixer(hl, hc, s5_w_in[k], s5_a_re[k], s5_a_im[k], s5_b_re[k], s5_b_im[k], s5_c_re[k], s5_c_im[k],
                               s5_log_step[k], s5_d[k], s5_w_glu[k], s5_b_glu[k], s5_w_out[k], keep_ctx)
        else:
            ml = _sgu_mixer(hl, sgu_w_in[k], sgu_ln_g[k], sgu_ln_b[k], sgu_w_s[k], sgu_b_s[k], sgu_w_out[k])
            mc = _sgu_mixer(hc, sgu_w_in[k], sgu_ln_g[k], sgu_ln_b[k], sgu_w_s[k], sgu_b_s[k], sgu_w_out[k]) if keep_ctx else None
        x = x + m_lat[2] * ml
        hl = _modulate(_rmsnorm(x, ffn_norm_g[i]), m_lat[3], m_lat[4])
        x = x + m_lat[5] * _conv_ffn(hl, rows, GRID_W, ffn_w_up[i], ffn_conv_w[i], ffn_conv_b[i], ffn_w_down[i])
        if keep_ctx:
            s = s + m_ctx[2] * mc
            hc = _modulate(_rmsnorm(s, ffn_norm_g[i]), m_ctx[3], m_ctx[4])
            s = s + m_ctx[5] * _conv_ffn(hc, 1, n_ctx, ffn_w_up[i], ffn_conv_w[i], ffn_conv_b[i], ffn_w_down[i])
    return _rmsnorm(x, final_norm_g)
```

```python
import contextlib
import numpy as np
import concourse.bass as bass
import concourse.mybir as mybir
from concourse.bass_utils import run_bass_kernel_spmd

F32 = mybir.dt.float32
BF16 = mybir.dt.bfloat16
ALU = mybir.AluOpType
AF = mybir.ActivationFunctionType
NCORES = 8
D = 2048


class Buf:
    __slots__ = ("name", "last_writer", "readers")

    def __init__(self, name=""):
        self.name = name
        self.last_writer = None
        self.readers = []


class Op:
    __slots__ = ("eng", "fn", "deps", "dma", "sig", "needed", "inc")

    def __init__(self, eng, fn, dma, inc):
        self.eng, self.fn, self.dma, self.inc = eng, fn, dma, inc
        self.deps, self.sig, self.needed = [], None, False


ENGS = ("pe", "dve", "act", "pool", "sp")
N_DMA_SEMS = 8


class Sched:
    def __init__(self, nc):
        self.nc = nc
        self.ops = []

    def add(self, eng, fn, reads=(), writes=(), dma=False):
        op = Op(eng, fn, dma, 16 if dma else 1)
        deps = set()
        for r in reads:
            if r.last_writer is not None:
                deps.add(r.last_writer)
        for w in writes:
            if w.last_writer is not None:
                deps.add(w.last_writer)
            deps.update(w.readers)
        for r in reads:
            r.readers.append(op)
        for w in writes:
            w.last_writer = op
            w.readers = []
        op.deps = [d for d in deps if not (d.eng == "pe" and eng == "pe")]
        for d in op.deps:
            d.needed = True
        if dma:
            op.needed = True
        self.ops.append(op)
        return op

    def emit(self, final_ops):
        nc = self.nc
        for o in final_ops:
            o.needed = True
        with contextlib.ExitStack() as st:
            comp_sem = {e: st.enter_context(nc.semaphore(f"s_{e}")) for e in ENGS}
            dma_sems = {e: [st.enter_context(nc.semaphore(f"d_{e}{i}")) for i in range(N_DMA_SEMS)]
                        for e in ("sp", "act", "pool")}
            comp_cnt = {e: 0 for e in ENGS}
            dma_cnt = {e: 0 for e in ENGS}
            sem_total = {}
            per_eng = {e: [] for e in ENGS}
            for op in self.ops:
                per_eng[op.eng].append(op)
                if op.dma:
                    n = dma_cnt[op.eng]
                    dma_cnt[op.eng] += 1
                    sem = dma_sems[op.eng][n % N_DMA_SEMS]
                    before = sem_total.get(id(sem), 0)
                    sem_total[id(sem)] = before + op.inc
                    op.sig = (sem, before + op.inc, before)
                elif op.needed:
                    comp_cnt[op.eng] += 1
                    op.sig = (comp_sem[op.eng], comp_cnt[op.eng], None)
            final = list(final_ops)

            def run_engine(ename, eng):
                seen = {}

                def wait(sem, val):
                    if seen.get(id(sem), 0) >= val:
                        return
                    eng.wait_ge(sem, val)
                    seen[id(sem)] = val

                for op in per_eng[ename]:
                    for d in op.deps:
                        wait(d.sig[0], d.sig[1])
                    if op.dma and op.sig[2] > 0:
                        wait(op.sig[0], op.sig[2])
                    ins = op.fn(eng)
                    if op.sig is not None:
                        ins.then_inc(op.sig[0], op.inc if op.dma else 1)
                if ename == "sp":
                    for o in final:
                        wait(o.sig[0], o.sig[1])

            with nc.Block() as block:
                block.tensor(lambda e: run_engine("pe", e))
                block.vector(lambda e: run_engine("dve", e))
                block.scalar(lambda e: run_engine("act", e))
                block.gpsimd(lambda e: run_engine("pool", e))
                block.sync(lambda e: run_engine("sp", e))


class KB:
    def __init__(self):
        self.nc = bass.Bass("TRN2", target_bir_lowering=False)
        self.S = Sched(self.nc)
        self.st = contextlib.ExitStack()
        self.finals = []
        self.banks = []
        for i in range(8):
            t = self.st.enter_context(self.nc.psum_tensor(f"ps{i}", [128, 512], F32))
            self.banks.append((t, Buf(f"ps{i}")))
        self.bi = 0
        self.rr = 0

    def bank(self):
        b = self.banks[self.bi % 8]
        self.bi += 1
        return b

    def sb(self, name, shape, dt=F32):
        return self.st.enter_context(self.nc.sbuf_tensor(name, list(shape), dt))

    def din(self, name, shape, dt=F32):
        return self.nc.dram_tensor(name, list(shape), dt, kind="ExternalInput").ap()

    def dout(self, name, shape, dt=F32):
        return self.nc.dram_tensor(name, list(shape), dt, kind="ExternalOutput").ap()

    def load(self, dst_ap, src_ap, buf, eng=None, reads=()):
        if eng is None:
            eng = ("sp", "pool")[self.rr % 2]
            self.rr += 1
        return self.S.add(eng, lambda e: e.dma_start(out=dst_ap, in_=src_ap), reads=reads, writes=[buf], dma=True)

    def store(self, dst_ap, src_ap, buf, eng="sp"):
        op = self.S.add(eng, lambda e: e.dma_start(out=dst_ap, in_=src_ap), reads=[buf], dma=True)
        self.finals.append(op)
        return op

    def mm(self, out_ap, out_buf, pairs, reads):
        def fn(e):
            n = len(pairs)
            ins = None
            for i, (l, r) in enumerate(pairs):
                ins = e.matmul(out_ap, l, r, start=(i == 0), stop=(i == n - 1))
            return ins
        return self.S.add("pe", fn, reads=reads, writes=[out_buf])

    def finish(self):
        self.S.emit(self.finals)
        self.st.close()
        return self.nc


def run(nc, in_maps):
    res = run_bass_kernel_spmd(nc, in_maps, core_ids=list(range(NCORES)))
    return res.results


def bc(v, p=128):
    return np.ascontiguousarray(np.broadcast_to(np.asarray(v, np.float32).reshape(1, -1), (p, v.size)))


def norm_mod_T(k, xt_ap, rows, A, B, bA, bB, bx, ident, bident, tag, bufs, want_hl=False):
    S = k.S
    junk, bjunk = bufs["junk"]
    ss, bss = bufs["ss"]
    rstd, brstd = bufs["rstd"]
    t1, bt1 = bufs["t1"]
    hl, bhl = bufs["hl"]
    hlT, bhlT = bufs["hlT"]
    S.add("act", lambda e: e.activation(out=junk[:rows, :], in_=xt_ap, func=AF.Square, accum_out=ss[:rows, 0:1]),
          reads=[bx], writes=[bjunk, bss])
    S.add("dve", lambda e: e.tensor_scalar(out=rstd[:rows, 0:1], in0=ss[:rows, 0:1], scalar1=1.0 / D, scalar2=1e-6,
                                           op0=ALU.mult, op1=ALU.add), reads=[bss], writes=[brstd])
    S.add("act", lambda e: e.sqrt(out=rstd[:rows, 0:1], in_=rstd[:rows, 0:1]), reads=[brstd], writes=[brstd])
    S.add("dve", lambda e: e.reciprocal(out=rstd[:rows, 0:1], in_=rstd[:rows, 0:1]), reads=[brstd], writes=[brstd])
    S.add("dve", lambda e: e.scalar_tensor_tensor(out=t1[:rows, :], in0=xt_ap, scalar=rstd[:rows, 0:1], in1=A[:rows, :],
                                                  op0=ALU.mult, op1=ALU.mult), reads=[bx, brstd, bA], writes=[bt1])
    S.add("dve", lambda e: e.tensor_tensor(out=hl[:rows, :], in0=t1[:rows, :], in1=B[:rows, :], op=ALU.add),
          reads=[bt1, bB], writes=[bhl])
    for half in range(2):
        pt, bpt = k.bank()
        ptb = pt[:].bitcast(BF16)

        def fn(e, half=half, ptb=ptb):
            ins = None
            for j in range(8):
                kc = half * 8 + j
                ins = e.transpose(ptb[:, j * 128:j * 128 + rows], hl[:rows, kc * 128:(kc + 1) * 128], ident[:rows, :rows])
            return ins
        S.add("pe", fn, reads=[bhl, bident], writes=[bpt])
        S.add("act", lambda e, half=half, ptb=ptb: e.copy(
            out=hlT[:, half * 8:(half + 1) * 8, :rows],
            in_=ptb.rearrange("p (j t) -> p j t", t=128)[:, :, :rows]), reads=[bpt], writes=[bhlT])


def build_mod(k, ccT, bcc, wmod_d, bmodB_d, ncols, nrows_m, outs):
    S = k.S
    sc = k.sb("sc", [128, 16, nrows_m]); bsc = Buf()
    scb = k.sb("scb", [128, 16, nrows_m, 128]); bscb = Buf()
    S.add("act", lambda e: e.activation(out=sc[:], in_=ccT[:], func=AF.Silu), reads=[bcc], writes=[bsc])
    S.add("dve", lambda e: e.tensor_copy(out=scb[:], in_=sc[:].unsqueeze(3).to_broadcast([128, 16, nrows_m, 128])),
          reads=[bsc], writes=[bscb])
    NW = 128
    wst = [(k.sb(f"wst{i}", [128, 16, NW]), Buf()) for i in range(2)]
    bmt = [(k.sb(f"bmt{i}", [128, NW]), Buf()) for i in range(2)]
    for nb in range(ncols // NW):
        w, bw = wst[nb % 2]
        bm, bbm = bmt[nb % 2]
        k.load(w[:], wmod_d[:, nb * NW:(nb + 1) * NW].rearrange("(kc p) n -> p kc n", p=128), bw)
        k.load(bm[:], bmodB_d[:, nb * NW:(nb + 1) * NW], bbm)
        for m in range(nrows_m):
            pt, bpt = k.bank()
            k.mm(pt[:, :NW], bpt, [(scb[:, kc, m, :], w[:, kc, :]) for kc in range(16)], reads=[bscb, bw])
            dst, bdst = outs[m](nb * NW, NW)
            S.add("dve", lambda e, pt=pt, dst=dst, bm=bm: e.tensor_tensor(out=dst, in0=pt[:, :NW], in1=bm[:], op=ALU.add),
                  reads=[bpt, bbm], writes=[bdst])


def build_L1():
    k = KB()
    S = k.S
    T = 1088
    xs = k.din("xs", [T, D]); ccT_d = k.din("ccT", [128, 16, 2]); gB_d = k.din("gB", [128, D])
    wmod_d = k.din("wmod", [D, 4096]); bmodB_d = k.din("bmodB", [128, 4096]); win_d = k.din("w_in", [D, D])
    ident_d = k.din("ident", [128, 128])
    u_d = k.dout("u", [T, D])
    ident = k.sb("ident_sb", [128, 128], BF16); bident = Buf()
    k.load(ident[:], ident_d, bident, eng="pool")
    ccT = k.sb("ccT_sb", [128, 16, 2]); bcc = Buf()
    k.load(ccT[:], ccT_d, bcc)
    gB = k.sb("gB_sb", [128, D]); bgB = Buf()
    k.load(gB[:], gB_d, bgB)
    AB = [[(k.sb(f"AB{m}{j}", [128, D]), Buf()) for j in range(2)] for m in range(2)]
    outs = [(lambda c0, n, m=m: (AB[m][c0 // D][0][:, c0 % D:c0 % D + n], AB[m][c0 // D][1])) for m in range(2)]
    build_mod(k, ccT, bcc, wmod_d, bmodB_d, 4096, 2, outs)
    for m in range(2):
        A, bA = AB[m][1]
        S.add("dve", lambda e, A=A: e.scalar_tensor_tensor(out=A[:], in0=A[:], scalar=1.0, in1=gB[:], op0=ALU.add, op1=ALU.mult),
              reads=[bA, bgB], writes=[bA])
    win = k.sb("win", [128, 16, D], BF16); bwin = Buf()
    k.load(win[:], win_d.rearrange("(kc p) n -> p kc n", p=128), bwin, eng="pool")
    xt = [(k.sb(f"xt{i}", [128, D]), Buf()) for i in range(2)]
    ut = [(k.sb(f"ut{i}", [128, D]), Buf()) for i in range(2)]
    junk_ = (k.sb("junk", [128, D], BF16), Buf())
    t1_ = (k.sb("t1", [128, D]), Buf())
    nb_ = [dict(junk=junk_, ss=(k.sb(f"ss{i}", [128, 1]), Buf()),
                rstd=(k.sb(f"rstd{i}", [128, 1]), Buf()), t1=t1_,
                hl=(k.sb(f"hl{i}", [128, D], BF16), Buf()), hlT=(k.sb(f"hlT{i}", [128, 16, 128], BF16), Buf()))
           for i in range(2)]
    for i in range(9):
        rows = 128 if i < 8 else 64
        m = 0 if i < 8 else 1
        x, bx = xt[i % 2]
        k.load(x[:rows, :], xs[i * 128:i * 128 + rows, :], bx, eng="sp")
        bufs = nb_[i % 2]
        norm_mod_T(k, x[:rows, :], rows, AB[m][1][0], AB[m][0][0], AB[m][1][1], AB[m][0][1], bx, ident, bident, "l1", bufs)
        hlT, bhlT = bufs["hlT"]
        u, bu = ut[i % 2]
        for nb in range(4):
            pt, bpt = k.bank()
            k.mm(pt[:rows, :], bpt, [(hlT[:, kc, :rows], win[:, kc, nb * 512:(nb + 1) * 512]) for kc in range(16)],
                 reads=[bhlT, bwin])
            eng = "act" if nb % 2 == 0 else "dve"
            if eng == "act":
                S.add("act", lambda e, pt=pt, u=u, nb=nb, rows=rows: e.copy(out=u[:rows, nb * 512:(nb + 1) * 512], in_=pt[:rows, :]),
                      reads=[bpt], writes=[bu])
            else:
                S.add("dve", lambda e, pt=pt, u=u, nb=nb, rows=rows: e.tensor_copy(out=u[:rows, nb * 512:(nb + 1) * 512], in_=pt[:rows, :]),
                      reads=[bpt], writes=[bu])
        k.store(u_d[i * 128:i * 128 + rows, :], u[:rows, :], bu)
    return k.finish()


def launch_L1(inp):
    nc = build_L1()
    maps = []
    ident = np.eye(128, dtype=np.float32)
    for c in range(NCORES):
        b, q = c // 4, c % 4
        xs = np.concatenate([inp["x"][b, q * 1024:(q + 1) * 1024], inp["ctx"][b, q * 64:(q + 1) * 64]], 0)
        cc = np.stack([inp["c"][b], inp["c_ctx"]], 0)
        ccT = np.ascontiguousarray(cc.reshape(2, 16, 128).transpose(2, 1, 0))
        maps.append(dict(xs=np.ascontiguousarray(xs), ccT=ccT, gB=bc(inp["mix_norm_g"][0]),
                         wmod=np.ascontiguousarray(inp["w_mod"][0][:, :4096]), bmodB=bc(inp["b_mod"][0][:4096]),
                         w_in=np.ascontiguousarray(inp["s5_w_in"][0]), ident=ident))
    res = run(nc, maps)
    u_lat = np.zeros((2, 4096, D), np.float32)
    u_ctx = np.zeros((2, 256, D), np.float32)
    for c in range(NCORES):
        b, q = c // 4, c % 4
        u_lat[b, q * 1024:(q + 1) * 1024] = res[c]["u"][:1024]
        u_ctx[b, q * 64:(q + 1) * 64] = res[c]["u"][1024:]
    return u_lat, u_ctx


SEQ_T = 4352
SEG = 1088


def build_L2():
    k = KB()
    S = k.S
    V = lambda name, shape, dt=F32: (k.sb(name, shape, dt), Buf(name))
    uTd = [k.din("uT_f", [256, 2 * SEQ_T]), k.din("uT_r", [256, 2 * SEQ_T])]
    yTd = [k.dout("yT_f", [256, 2 * SEQ_T]), k.dout("yT_r", [256, 2 * SEQ_T])]
    pd = {n: k.din(n, [128, 2, 8]) for n in ("are", "aim", "lst")}
    pd4 = {n: k.din(n, [128, 2, 8, 16]) for n in ("bre", "bim", "cre", "cim")}
    ident_d = k.din("ident", [128, 128])
    ident, bident = V("ident_sb", [128, 128])
    k.load(ident[:], ident_d, bident)
    P = {}
    for n, d_ in pd.items():
        P[n] = V(n + "_sb", [128, 16])
        k.load(P[n][0][:], d_.rearrange("p d r -> p (d r)"), P[n][1])
    P4 = {}
    for n, d_ in pd4.items():
        P4[n] = V(n + "_sb", [128, 16, 16])
        k.load(P4[n][0][:], d_.rearrange("p d r h -> p (d r) h"), P4[n][1])

    cnt = [0]

    def tmp(shape=(128, 16)):
        cnt[0] += 1
        return V(f"tmp{cnt[0]}", list(shape))

    def tt(out, a, b, op, eng="dve"):
        S.add(eng, lambda e: e.tensor_tensor(out=out[0][:], in0=a[0][:], in1=b[0][:], op=op), reads=[a[1], b[1]], writes=[out[1]])
        return out

    def ts(out, a, s1, op0, s2=None, op1=None):
        if op1 is None:
            S.add("dve", lambda e: e.tensor_single_scalar(out=out[0][:], in_=a[0][:], scalar=s1, op=op0), reads=[a[1]], writes=[out[1]])
        else:
            S.add("dve", lambda e: e.tensor_scalar(out=out[0][:], in0=a[0][:], scalar1=s1, scalar2=s2, op0=op0, op1=op1),
                  reads=[a[1]], writes=[out[1]])
        return out

    def stt(out, a, s, b, op0, op1):
        S.add("dve", lambda e: e.scalar_tensor_tensor(out=out[0][:], in0=a[0][:], scalar=s, in1=b[0][:], op0=op0, op1=op1),
              reads=[a[1], b[1]], writes=[out[1]])
        return out

    def act(out, a, func):
        S.add("act", lambda e: e.activation(out=out[0][:], in_=a[0][:], func=func), reads=[a[1]], writes=[out[1]])
        return out

    are, aim, lst = P["are"], P["aim"], P["lst"]
    dt = act(tmp(), lst, AF.Exp)
    adt = tt(tmp(), are, dt, ALU.mult)
    mag = act(tmp(), adt, AF.Exp)
    th = tt(tmp(), aim, dt, ALU.mult)
    y = ts(tmp(), th, 1.0 / 32.0, ALU.mult)
    y2 = tt(tmp(), y, y, ALU.mult)
    p = ts(tmp(), y2, 1.0 / 362880.0, ALU.mult)
    for c_ in (-1.0 / 5040.0, 1.0 / 120.0, -1.0 / 6.0):
        p = stt(tmp(), p, c_, y2, ALU.add, ALU.mult)
    s = stt(tmp(), p, 1.0, y, ALU.add, ALU.mult)
    q = ts(tmp(), y2, -1.0 / 3628800.0, ALU.mult)
    for c_ in (1.0 / 40320.0, -1.0 / 720.0, 1.0 / 24.0, -0.5):
        q = stt(tmp(), q, c_, y2, ALU.add, ALU.mult)
    c = ts(tmp(), q, 1.0, ALU.add)
    for _ in range(5):
        s_n = stt(tmp(), s, 2.0, c, ALU.mult, ALU.mult)
        t_ = stt(tmp(), s, -2.0, s, ALU.mult, ALU.mult)
        c = ts(tmp(), t_, 1.0, ALU.add)
        s = s_n
    cth, sth = c, s
    lr = tt(tmp(), mag, cth, ALU.mult)
    li = tt(tmp(), mag, sth, ALU.mult)
    den = tt(tmp(), tt(tmp(), are, are, ALU.mult), tt(tmp(), aim, aim, ALU.mult), ALU.add)
    rden = tmp()
    S.add("dve", lambda e: e.reciprocal(out=rden[0][:], in_=den[0][:]), reads=[den[1]], writes=[rden[1]])
    lm1 = ts(tmp(), lr, -1.0, ALU.add)
    kre = tt(tmp(), tt(tmp(), tt(tmp(), lm1, are, ALU.mult), tt(tmp(), li, aim, ALU.mult), ALU.add), rden, ALU.mult)
    kim = tt(tmp(), tt(tmp(), tt(tmp(), li, are, ALU.mult), tt(tmp(), lm1, aim, ALU.mult), ALU.subtract), rden, ALU.mult)

    def bmul(name, a, b4):
        o = V(name, [128, 16, 16])
        S.add("dve", lambda e: e.tensor_tensor(out=o[0][:], in0=b4[0][:], in1=a[0][:].unsqueeze(2).to_broadcast([128, 16, 16]), op=ALU.mult),
              reads=[a[1], b4[1]], writes=[o[1]])
        return o
    bbr = tt(V("bbr", [128, 16, 16]), bmul("m1", kre, P4["bre"]), bmul("m2", kim, P4["bim"]), ALU.subtract)
    bbi = tt(V("bbi", [128, 16, 16]), bmul("m3", kre, P4["bim"]), bmul("m4", kim, P4["bre"]), ALU.add)
    ncim = ts(V("ncim", [128, 16, 16]), P4["cim"], -1.0, ALU.mult)

    def blockify(name, src, dt_):
        o = V(name, [128, 16, 32], dt_)
        S.add("dve", lambda e: e.memset(o[0][:], 0.0), writes=[o[1]])
        S.add("dve", lambda e: e.tensor_copy(out=o[0][0:64, :, 0:16], in_=src[0][0:64, :, :]), reads=[src[1]], writes=[o[1]])
        S.add("dve", lambda e: e.tensor_copy(out=o[0][64:128, :, 16:32], in_=src[0][64:128, :, :]), reads=[src[1]], writes=[o[1]])
        return o
    CTre = blockify("CTre", P4["cre"], BF16)
    CTim = blockify("CTim", ncim, BF16)
    BLre = blockify("BLre", bbr, F32)
    BLim = blockify("BLim", bbi, F32)
    BbT = [V("BbTre", [32, 16, 128], BF16), V("BbTim", [32, 16, 128], BF16)]
    for ci, BL in enumerate((BLre, BLim)):
        for g4 in range(4):
            pt, bpt = k.bank()

            def fn(e, BL=BL, g4=g4, pt=pt):
                ins = None
                for j in range(4):
                    ins = e.transpose(pt[0:32, j * 128:(j + 1) * 128], BL[0][:, g4 * 4 + j, :], ident[:, :])
                return ins
            S.add("pe", fn, reads=[BL[1], bident], writes=[bpt])
            S.add("act", lambda e, pt=pt, g4=g4, ci=ci: e.copy(out=BbT[ci][0][:, g4 * 4:(g4 + 1) * 4, :],
                                                             in_=pt[0:32, :].rearrange("p (j t) -> p j t", t=128)),
                  reads=[bpt], writes=[BbT[ci][1]])

    CM = [V("cma", [128, 16, 64]), V("cmb", [128, 16, 64])]

    def cmul_bc(o_re, o_im, a_re, a_im, p_re, p_im, L, n):
        W_ = o_re[0].shape[2]
        t1 = (CM[0][0][:, :, 0:n], CM[0][1]); t2 = (CM[1][0][:, :, 0:n], CM[1][1])
        pb_re = lambda: p_re[0][:].unsqueeze(2).to_broadcast([128, 16, n])
        pb_im = lambda: p_im[0][:].unsqueeze(2).to_broadcast([128, 16, n])
        S.add("dve", lambda e: e.tensor_tensor(out=t1[0][:], in0=a_re[0][:, :, 0:n], in1=pb_re(), op=ALU.mult), reads=[a_re[1], p_re[1]], writes=[t1[1]])
        S.add("dve", lambda e: e.tensor_tensor(out=t2[0][:], in0=a_im[0][:, :, 0:n], in1=pb_im(), op=ALU.mult), reads=[a_im[1], p_im[1]], writes=[t2[1]])
        S.add("dve", lambda e: e.tensor_tensor(out=o_re[0][:, :, L:L + n], in0=t1[0][:], in1=t2[0][:], op=ALU.subtract), reads=[t1[1], t2[1]], writes=[o_re[1]])
        S.add("dve", lambda e: e.tensor_tensor(out=t1[0][:], in0=a_re[0][:, :, 0:n], in1=pb_im(), op=ALU.mult), reads=[a_re[1], p_im[1], o_re[1]], writes=[t1[1]])
        S.add("dve", lambda e: e.tensor_tensor(out=t2[0][:], in0=a_im[0][:, :, 0:n], in1=pb_re(), op=ALU.mult), reads=[a_im[1], p_re[1], o_re[1]], writes=[t2[1]])
        S.add("dve", lambda e: e.tensor_tensor(out=o_im[0][:, :, L:L + n], in0=t1[0][:], in1=t2[0][:], op=ALU.add), reads=[t1[1], t2[1]], writes=[o_im[1]])

    def csq(p_re, p_im):
        a = tt(tmp(), p_re, p_re, ALU.mult); b = tt(tmp(), p_im, p_im, ALU.mult)
        n_re = tt(tmp(), a, b, ALU.subtract)
        n_im = stt(tmp(), p_re, 2.0, p_im, ALU.mult, ALU.mult)
        return n_re, n_im

    def power_table(name, p_re, p_im, n):
        T_re = V(name + "re", [128, 16, n]); T_im = V(name + "im", [128, 16, n])
        S.add("dve", lambda e: e.memset(T_re[0][:, :, 0:1], 1.0), writes=[T_re[1]])
        S.add("dve", lambda e: e.memset(T_im[0][:, :, 0:1], 0.0), writes=[T_im[1]])
        L = 1
        while L < n:
            m = min(L, n - L)
            cmul_bc(T_re, T_im, T_re, T_im, p_re, p_im, L, m)
            p_re, p_im = csq(p_re, p_im)
            L *= 2
        return T_re, T_im, p_re, p_im
    E64re, E64im, q_re, q_im = power_table("E64", cth, sth, 64)
    Fre, Fim, _, _ = power_table("F68", q_re, q_im, 68)

    Ere, Eim = V("Ere", [128, 68, 64]), V("Eim", [128, 68, 64])
    Et1, Et2 = V("Et1", [128, 34, 64]), V("Et2", [128, 34, 64])
    Rt = V("Rt", [128, SEG])
    Wsets = [{n: V(n + str(i), [128, SEG]) for n in ("xre", "xim", "A1", "A2", "T1")} for i in range(2)]
    Ssets = [(V(f"sre{i}", [128, SEG], BF16), V(f"sim{i}", [128, SEG], BF16)) for i in range(2)]
    UPF = 2
    usb = [V(f"usb{i}", [32, SEG], BF16) for i in range(UPF + 1)]
    iters = [(d, pr, b, sgi) for d in range(2) for pr in range(8) for b in range(2) for sgi in range(SEQ_T // SEG)]

    def issue_u(n_it):
        d_, pr_, b_, sg_ = iters[n_it]
        u__, bu__ = usb[n_it % (UPF + 1)]
        k.load(u__[:], uTd[d_][pr_ * 32:(pr_ + 1) * 32, b_ * SEQ_T + sg_ * SEG:b_ * SEQ_T + (sg_ + 1) * SEG], bu__, eng="pool")

    yt = [V(f"yt{i}", [32, SEG]) for i in range(2)]
    carries = [V(f"carry{i}", [128, 2]) for i in range(2)]
    it = 0
    for d in range(2):
        for pr in range(8):
            dp = d * 8 + pr
            for hf in range(2):
                Fb = lambda T, hf=hf, dp=dp: T[0][:, dp, hf * 34:(hf + 1) * 34].unsqueeze(2).to_broadcast([128, 34, 64])
                Eb = lambda T, dp=dp: T[0][:, dp, :].unsqueeze(1).to_broadcast([128, 34, 64])
                Eo_re = Ere[0][:, hf * 34:(hf + 1) * 34, :]
                Eo_im = Eim[0][:, hf * 34:(hf + 1) * 34, :]
                S.add("pool", lambda e, Fb=Fb, Eb=Eb: e.tensor_tensor(out=Et1[0][:], in0=Fb(Fre), in1=Eb(E64re), op=ALU.mult), reads=[Fre[1], E64re[1]], writes=[Et1[1]])
                S.add("pool", lambda e, Fb=Fb, Eb=Eb: e.tensor_tensor(out=Et2[0][:], in0=Fb(Fim), in1=Eb(E64im), op=ALU.mult), reads=[Fim[1], E64im[1]], writes=[Et2[1]])
                S.add("pool", lambda e, Eo_re=Eo_re: e.tensor_tensor(out=Eo_re, in0=Et1[0][:], in1=Et2[0][:], op=ALU.subtract), reads=[Et1[1], Et2[1]], writes=[Ere[1]])
                S.add("pool", lambda e, Fb=Fb, Eb=Eb: e.tensor_tensor(out=Et1[0][:], in0=Fb(Fre), in1=Eb(E64im), op=ALU.mult), reads=[Fre[1], E64im[1], Ere[1]], writes=[Et1[1]])
                S.add("pool", lambda e, Fb=Fb, Eb=Eb: e.tensor_tensor(out=Et2[0][:], in0=Fb(Fim), in1=Eb(E64re), op=ALU.mult), reads=[Fim[1], E64re[1], Ere[1]], writes=[Et2[1]])
                S.add("pool", lambda e, Eo_im=Eo_im: e.tensor_tensor(out=Eo_im, in0=Et1[0][:], in1=Et2[0][:], op=ALU.add), reads=[Et1[1], Et2[1]], writes=[Eim[1]])
            S.add("pool", lambda e, dp=dp: e.tensor_copy(out=Rt[0][:], in_=mag[0][:, dp:dp + 1].to_broadcast([128, SEG])), reads=[mag[1]], writes=[Rt[1]])
            Ef_re = Ere[0][:].rearrange("p k t -> p (k t)")
            Ef_im = Eim[0][:].rearrange("p k t -> p (k t)")
            chunks = [(c0, min(512, SEG - c0)) for c0 in range(0, SEG, 512)]

            def tte(eng, out, a_, bap, bbuf, op):
                S.add(eng, lambda e: e.tensor_tensor(out=out[0][:], in0=a_[0][:], in1=bap, op=op), reads=[a_[1], bbuf], writes=[out[1]])

            def stage_A(cx):
                W = cx["W"]; u_, bu_ = cx["u"]
                if cx["it"] == 0:
                    for pf_ in range(UPF):
                        issue_u(pf_)
                if cx["it"] + UPF < len(iters):
                    issue_u(cx["it"] + UPF)
                for ci, (dst, bdst) in enumerate((W["xre"], W["xim"])):
                    for (c0, n) in chunks:
                        pt, bpt = k.bank()
                        k.mm(pt[:, :n], bpt, [(BbT[ci][0][:, dp, :], u_[:, c0:c0 + n])], reads=[BbT[ci][1], bu_])
                        S.add("act", lambda e, pt=pt, dst=dst, c0=c0, n=n: e.copy(out=dst[:, c0:c0 + n], in_=pt[:, :n]), reads=[bpt], writes=[bdst])
                Er, Ei = cx["Er"], cx["Ei"]
                xre, xim, A1, A2, T1 = W["xre"], W["xim"], W["A1"], W["A2"], W["T1"]
                tte("dve", A1, xre, Er, Ere[1], ALU.mult)
                tte("dve", T1, xim, Ei, Eim[1], ALU.mult)
                tt(A1, A1, T1, ALU.add)
                tte("dve", A2, xim, Er, Ere[1], ALU.mult)
                tte("dve", T1, xre, Ei, Eim[1], ALU.mult)
                tt(A2, A2, T1, ALU.subtract)
                pcarry = cx["pcarry"]; carry = cx["carry"]
                for ci, (src, dst) in enumerate(((A1, xre), (A2, xim))):
                    if cx["sgi"] == 0:
                        S.add("dve", lambda e, src=src, dst=dst: e.tensor_tensor_scan(out=dst[0][:], data0=Rt[0][:], data1=src[0][:], initial=0.0,
                                                                                     op0=ALU.mult, op1=ALU.add), reads=[Rt[1], src[1]], writes=[dst[1]])
                    else:
                        S.add("dve", lambda e, src=src, dst=dst, ci=ci, pcarry=pcarry: e.tensor_tensor_scan(out=dst[0][:], data0=Rt[0][:], data1=src[0][:],
                                                                                            initial=pcarry[0][:, ci:ci + 1], op0=ALU.mult, op1=ALU.add),
                              reads=[Rt[1], src[1], pcarry[1]], writes=[dst[1]])
                if cx["sgi"] < SEQ_T // SEG - 1:
                    S.add("act", lambda e, carry=carry, xre=xre: e.copy(out=carry[0][:, 0:1], in_=xre[0][:, SEG - 1:SEG]), reads=[xre[1]], writes=[carry[1]])
                    S.add("act", lambda e, carry=carry, xim=xim: e.copy(out=carry[0][:, 1:2], in_=xim[0][:, SEG - 1:SEG]), reads=[xim[1]], writes=[carry[1]])

            def stage_B(cx):
                W = cx["W"]; sre, sim = cx["S"]; y_, by_ = cx["y"]
                Er, Ei = cx["Er"], cx["Ei"]
                xre, xim, A1, T1 = W["xre"], W["xim"], W["A1"], W["T1"]
                tte("dve", T1, xre, Er, Ere[1], ALU.mult)
                tte("dve", A1, xim, Ei, Eim[1], ALU.mult)
                tt(sre, T1, A1, ALU.subtract)
                tte("dve", T1, xre, Ei, Eim[1], ALU.mult)
                tte("dve", A1, xim, Er, Ere[1], ALU.mult)
                tt(sim, T1, A1, ALU.add)
                for (c0, n) in chunks:
                    pt, bpt = k.bank()
                    k.mm(pt[0:32, :n], bpt, [(CTre[0][:, dp, :], sre[0][:, c0:c0 + n]), (CTim[0][:, dp, :], sim[0][:, c0:c0 + n])],
                         reads=[CTre[1], CTim[1], sre[1], sim[1]])
                    S.add("act", lambda e, pt=pt, y_=y_, c0=c0, n=n: e.copy(out=y_[:, c0:c0 + n], in_=pt[0:32, :n]), reads=[bpt], writes=[by_])
                k.store(yTd[d][pr * 32:(pr + 1) * 32, cx["b"] * SEQ_T + cx["t0"]:cx["b"] * SEQ_T + cx["t0"] + SEG], y_[:], by_)

            cxs = []
            for b in range(2):
                for sgi in range(SEQ_T // SEG):
                    t0 = sgi * SEG
                    cxs.append(dict(it=it, b=b, sgi=sgi, t0=t0, W=Wsets[it % 2], S=Ssets[it % 2], carry=carries[it % 2], pcarry=carries[(it - 1) % 2],
                                    u=usb[it % (UPF + 1)], y=yt[it % 2], Er=Ef_re[:, t0:t0 + SEG], Ei=Ef_im[:, t0:t0 + SEG]))
                    it += 1
            stage_A(cxs[0])
            for i_ in range(len(cxs)):
                if i_ + 1 < len(cxs):
                    stage_A(cxs[i_ + 1])
                stage_B(cxs[i_])
    return k.finish()


def s5_param_layout(inp, c):
    G0 = 16 * c
    out = {}
    for n, key in (("are", "s5_a_re"), ("aim", "s5_a_im")):
        a = inp[key][0][:, G0:G0 + 16, :]
        out[n] = np.ascontiguousarray(a.reshape(2, 8, 2, 64).transpose(2, 3, 0, 1).reshape(128, 2, 8))
    ls = inp["s5_log_step"][0][:, G0:G0 + 16]
    ls = np.broadcast_to(ls.reshape(2, 8, 2, 1), (2, 8, 2, 64))
    out["lst"] = np.ascontiguousarray(ls.transpose(2, 3, 0, 1).reshape(128, 2, 8))
    for n, key in (("bre", "s5_b_re"), ("bim", "s5_b_im")):
        a = inp[key][0][:, G0:G0 + 16]
        out[n] = np.ascontiguousarray(a.reshape(2, 8, 2, 64, 16).transpose(2, 3, 0, 1, 4).reshape(128, 2, 8, 16))
    for n, key in (("cre", "s5_c_re"), ("cim", "s5_c_im")):
        a = inp[key][0][:, G0:G0 + 16]
        out[n] = np.ascontiguousarray(a.reshape(2, 8, 2, 16, 64).transpose(2, 4, 0, 1, 3).reshape(128, 2, 8, 16))
    return out


def launch_L2(inp, u_lat, u_ctx):
    nc = build_L2()
    seq_f = np.concatenate([u_ctx, u_lat], 1)
    seq_r = np.concatenate([u_ctx[:, ::-1], u_lat[:, ::-1]], 1)
    ident = np.eye(128, dtype=np.float32)
    maps = []
    for c in range(NCORES):
        m = s5_param_layout(inp, c)
        m["uT_f"] = np.ascontiguousarray(seq_f[:, :, c * 256:(c + 1) * 256].transpose(2, 0, 1).reshape(256, 2 * SEQ_T))
        m["uT_r"] = np.ascontiguousarray(seq_r[:, :, c * 256:(c + 1) * 256].transpose(2, 0, 1).reshape(256, 2 * SEQ_T))
        m["ident"] = ident
        maps.append(m)
    res = run(nc, maps)
    y_f = np.zeros((2, 4096, D), np.float32)
    y_r = np.zeros((2, 4096, D), np.float32)
    for c in range(NCORES):
        yf = res[c]["yT_f"].reshape(256, 2, SEQ_T)[:, :, 256:]
        yr = res[c]["yT_r"].reshape(256, 2, SEQ_T)[:, :, 256:][:, :, ::-1]
        y_f[:, :, c * 256:(c + 1) * 256] = yf.transpose(1, 2, 0)
        y_r[:, :, c * 256:(c + 1) * 256] = yr.transpose(1, 2, 0)
    return y_f, y_r


def V_(k, name, shape, dt=F32):
    return (k.sb(name, shape, dt), Buf(name))


def gelu_tanh(k, out, bout, xin, bx, tmpa, tmpb, rows=128, accum=None, eng2="pool"):
    S = k.S
    ta, bta = tmpa
    tb, btb = tmpb
    S.add("act", lambda e: e.activation(out=ta, in_=xin, func=AF.Square), reads=[bx], writes=[bta])
    S.add("dve", lambda e: e.tensor_scalar(out=ta, in0=ta, scalar1=0.044715, scalar2=1.0, op0=ALU.mult, op1=ALU.add), reads=[bta], writes=[bta])
    S.add("dve", lambda e: e.tensor_tensor(out=ta, in0=ta, in1=xin, op=ALU.mult), reads=[bta, bx], writes=[bta])
    S.add("act", lambda e: e.activation(out=tb, in_=ta, func=AF.Sigmoid, scale=1.5957691216057308), reads=[bta], writes=[btb])
    if accum is None:
        S.add("dve", lambda e: e.tensor_tensor(out=out, in0=xin, in1=tb, op=ALU.mult), reads=[bx, btb], writes=[bout])
    else:
        acc_ap, bacc = accum
        S.add("dve", lambda e: e.scalar_tensor_tensor(out=out, in0=xin, scalar=1.0, in1=tb, op0=ALU.mult, op1=ALU.mult, accum_out=acc_ap),
              reads=[bx, btb], writes=[bout, bacc])


def transpose16(k, src, bsrc, dst, bdst, ident, bident, rows=128):
    S = k.S
    for half in range(2):
        pt, bpt = k.bank()
        ptb = pt[:].bitcast(BF16)

        def fn(e, half=half, ptb=ptb):
            ins = None
            for j in range(8):
                kc = half * 8 + j
                ins = e.transpose(ptb[:, j * 128:j * 128 + rows], src[:rows, kc * 128:(kc + 1) * 128], ident[:rows, :rows])
            return ins
        S.add("pe", fn, reads=[bsrc, bident], writes=[bpt])
        S.add("act", lambda e, half=half, ptb=ptb: e.copy(out=dst[:, half * 8:(half + 1) * 8, :rows],
                                                         in_=ptb.rearrange("p (j t) -> p j t", t=128)[:, :, :rows]), reads=[bpt], writes=[bdst])


def load_w16(k, w_sb, bw, w_d, nk=16, eng="pool"):
    k.load(w_sb[:, 0:nk, :], w_d.rearrange("(kc p) n -> p kc n", p=128), bw, eng=eng)


def build_L3a():
    k = KB()
    S = k.S
    yf_d = k.din("yf", [1024, D]); yr_d = k.din("yr", [1024, D]); u_d = k.din("u", [1024, D])
    dB_d = k.din("dB", [128, D]); bg_d = k.din("bgluB", [128, D])
    wg_d = k.din("w_glu", [D, D]); wo_d = k.din("w_out", [D, D]); ident_d = k.din("ident", [128, 128])
    ml_d = k.dout("ml", [1024, D])
    ident, bident = V_(k, "ident_sb", [128, 128], BF16); k.load(ident[:], ident_d, bident, eng="pool")
    dB, bdB = V_(k, "dB_sb", [128, D]); k.load(dB[:], dB_d, bdB)
    bg, bbg = V_(k, "bg_sb", [128, D]); k.load(bg[:], bg_d, bbg)
    wg, bwg = V_(k, "wg", [128, 16, D], BF16); load_w16(k, wg, bwg, wg_d)
    wo, bwo = V_(k, "wo", [128, 16, D], BF16); load_w16(k, wo, bwo, wo_d)
    Y = [V_(k, f"Y{i}", [128, D]) for i in range(2)]
    T, bT = V_(k, "T", [128, D])
    T2, bT2 = V_(k, "T2", [128, D])
    g, bg_ = V_(k, "g", [128, D], BF16)
    gT, bgT = V_(k, "gT", [128, 16, 128], BF16)
    zs, bzs = V_(k, "zs", [128, D])
    v, bv = V_(k, "v", [128, D], BF16)
    vT, bvT = V_(k, "vT", [128, 16, 128], BF16)
    mlo, bmlo = zs, bzs
    for i in range(8):
        y, by = Y[i % 2]
        sl = slice(i * 128, (i + 1) * 128)
        k.load(y[:], yf_d[sl, :], by, eng="sp")
        k.load(T[:], yr_d[sl, :], bT, eng="sp")
        S.add("dve", lambda e, y=y: e.tensor_tensor(out=y[:], in0=y[:], in1=T[:], op=ALU.add), reads=[by, bT], writes=[by])
        k.load(T[:], u_d[sl, :], bT, eng="sp")
        S.add("pool", lambda e: e.tensor_tensor(out=T[:], in0=T[:], in1=dB[:], op=ALU.mult), reads=[bT, bdB], writes=[bT])
        S.add("dve", lambda e, y=y: e.tensor_tensor(out=y[:], in0=y[:], in1=T[:], op=ALU.add), reads=[by, bT], writes=[by])
        gelu_tanh(k, g[:], bg_, y[:], by, (T[:], bT), (T2[:], bT2))
        transpose16(k, g, bg_, gT, bgT, ident, bident)
        for nb in range(4):
            pt, bpt = k.bank()
            cs = slice(nb * 512, (nb + 1) * 512)
            k.mm(pt[:], bpt, [(gT[:, kc, :], wg[:, kc, cs]) for kc in range(16)], reads=[bgT, bwg])
            S.add("dve", lambda e, pt=pt, cs=cs: e.tensor_tensor(out=zs[:, cs], in0=pt[:], in1=bg[:, cs], op=ALU.add), reads=[bpt, bbg], writes=[bzs])
        S.add("act", lambda e: e.activation(out=zs[:], in_=zs[:], func=AF.Sigmoid), reads=[bzs], writes=[bzs])
        S.add("dve", lambda e: e.tensor_tensor(out=v[:], in0=g[:], in1=zs[:], op=ALU.mult), reads=[bg_, bzs], writes=[bv])
        transpose16(k, v, bv, vT, bvT, ident, bident)
        for nb in range(4):
            pt, bpt = k.bank()
            cs = slice(nb * 512, (nb + 1) * 512)
            k.mm(pt[:], bpt, [(vT[:, kc, :], wo[:, kc, cs]) for kc in range(16)], reads=[bvT, bwo])
            S.add("act", lambda e, pt=pt, cs=cs: e.copy(out=mlo[:, cs], in_=pt[:]), reads=[bpt], writes=[bmlo])
        k.store(ml_d[sl, :], mlo[:], bmlo)
    return k.finish()


def launch_L3a(inp, y_f, y_r, u_lat):
    nc = build_L3a()
    ident = np.eye(128, dtype=np.float32)
    maps = []
    for c in range(NCORES):
        b, q = c // 4, c % 4
        sl = slice(q * 1024, (q + 1) * 1024)
        maps.append(dict(yf=np.ascontiguousarray(y_f[b, sl]), yr=np.ascontiguousarray(y_r[b, sl]), u=np.ascontiguousarray(u_lat[b, sl]),
                         dB=bc(inp["s5_d"][0]), bgluB=bc(inp["s5_b_glu"][0]), w_glu=inp["s5_w_glu"][0], w_out=inp["s5_w_out"][0], ident=ident))
    res = run(nc, maps)
    ml = np.zeros((2, 4096, D), np.float32)
    for c in range(NCORES):
        ml[c // 4, (c % 4) * 1024:(c % 4 + 1) * 1024] = res[c]["ml"]
    return ml


def build_resnorm():
    k = KB()
    S = k.S
    x_d = k.din("x", [1024, D]); ml_d = k.din("ml", [1024, D]); ccT_d = k.din("ccT", [128, 16, 1])
    wmod_d = k.din("wmod", [D, 3 * D]); bmodB_d = k.din("bmodB", [128, 3 * D]); gF_d = k.din("gF", [128, D])
    x1_d = k.dout("x1", [1024, D]); hl_d = k.dout("hl", [1024, D])
    ccT, bcc = V_(k, "ccT_sb", [128, 16, 1]); k.load(ccT[:], ccT_d, bcc)
    gF, bgF = V_(k, "gF_sb", [128, D]); k.load(gF[:], gF_d, bgF)
    M = [V_(k, f"M{j}", [128, D]) for j in range(3)]
    outs = [lambda c0, n: (M[c0 // D][0][:, c0 % D:c0 % D + n], M[c0 // D][1])]
    build_mod(k, ccT, bcc, wmod_d, bmodB_d, 3 * D, 1, outs)
    A, bA = M[2]
    B, bB = M[1]
    S.add("dve", lambda e: e.scalar_tensor_tensor(out=A[:], in0=A[:], scalar=1.0, in1=gF[:], op0=ALU.add, op1=ALU.mult), reads=[bA, bgF], writes=[bA])
    X = [V_(k, f"X{i}", [128, D]) for i in range(2)]
    Ml = [V_(k, f"Ml{i}", [128, D]) for i in range(2)]
    H = [V_(k, f"H{i}", [128, D]) for i in range(2)]
    junk, bjunk = V_(k, "junk", [128, D], BF16)
    SS = [V_(k, f"ss{i}", [128, 1]) for i in range(2)]
    for i in range(8):
        sl = slice(i * 128, (i + 1) * 128)
        x, bx = X[i % 2]; m, bm = Ml[i % 2]; h, bh = H[i % 2]; ss, bss = SS[i % 2]
        k.load(x[:], x_d[sl, :], bx, eng="sp")
        k.load(m[:], ml_d[sl, :], bm, eng="pool")
        S.add("pool", lambda e, m=m: e.tensor_tensor(out=m[:], in0=m[:], in1=M[0][0][:], op=ALU.mult), reads=[bm, M[0][1]], writes=[bm])
        S.add("dve", lambda e, x=x, m=m: e.tensor_tensor(out=x[:], in0=x[:], in1=m[:], op=ALU.add), reads=[bx, bm], writes=[bx])
        k.store(x1_d[sl, :], x[:], bx)
        S.add("act", lambda e, x=x, ss=ss: e.activation(out=junk[:], in_=x[:], func=AF.Square, accum_out=ss[:, 0:1]), reads=[bx], writes=[bjunk, bss])
        S.add("dve", lambda e, ss=ss: e.tensor_scalar(out=ss[:], in0=ss[:], scalar1=1.0 / D, scalar2=1e-6, op0=ALU.mult, op1=ALU.add), reads=[bss], writes=[bss])
        S.add("act", lambda e, ss=ss: e.sqrt(out=ss[:], in_=ss[:]), reads=[bss], writes=[bss])
        S.add("dve", lambda e, ss=ss: e.reciprocal(out=ss[:], in_=ss[:]), reads=[bss], writes=[bss])
        S.add("dve", lambda e, x=x, h=h, ss=ss: e.scalar_tensor_tensor(out=h[:], in0=x[:], scalar=ss[:, 0:1], in1=A[:], op0=ALU.mult, op1=ALU.mult),
              reads=[bx, bss, bA], writes=[bh])
        S.add("pool", lambda e, h=h: e.tensor_tensor(out=h[:], in0=h[:], in1=B[:], op=ALU.add), reads=[bh, bB], writes=[bh])
        k.store(hl_d[sl, :], h[:], bh)
    return k.finish()


def launch_resnorm(inp, x, ml, layer, gF, mod_cols):
    nc = build_resnorm()
    cols = np.concatenate([np.arange(j * D, (j + 1) * D) for j in mod_cols])
    wmod = np.ascontiguousarray(inp["w_mod"][layer][:, cols])
    bmodB = bc(inp["b_mod"][layer][cols])
    maps = []
    for c in range(NCORES):
        b, q = c // 4, c % 4
        sl = slice(q * 1024, (q + 1) * 1024)
        ccT = np.ascontiguousarray(inp["c"][b].reshape(1, 16, 128).transpose(2, 1, 0))
        maps.append(dict(x=np.ascontiguousarray(x[b, sl]), ml=np.ascontiguousarray(ml[b, sl]), ccT=ccT, wmod=wmod, bmodB=bmodB, gF=bc(gF)))
    res = run(nc, maps)
    x1 = np.zeros((2, 4096, D), np.float32); hl = np.zeros((2, 4096, D), np.float32)
    for c in range(NCORES):
        x1[c // 4, (c % 4) * 1024:(c % 4 + 1) * 1024] = res[c]["x1"]
        hl[c // 4, (c % 4) * 1024:(c % 4 + 1) * 1024] = res[c]["hl"]
    return x1, hl


FF = 5632


def build_ffn(final):
    k = KB()
    S = k.S
    hl_d = k.din("hlT", [D, 1152]); x_d = k.din("x", [1024, D]); ccT_d = k.din("ccT", [128, 16, 1])
    wmod_d = k.din("wmod", [D, D]); bmodB_d = k.din("bmodB", [128, D])
    wup_d = k.din("w_up", [D, 2 * FF]); cw_d = k.din("cw", [128, 88, 9]); cb_d = k.din("cb", [128, 88])
    wdn_d = k.din("w_down", [FF, D])
    xo_d = k.dout("xo", [1024, D])
    if final:
        gfin_d = k.din("gfin", [128, D])
        xs_d = k.nc.dram_tensor("xs_scratch", [1024, D], F32, kind="Internal").ap()
        bxs = Buf("xs")
    ccT, bcc = V_(k, "ccT_sb", [128, 16, 1]); k.load(ccT[:], ccT_d, bcc)
    gate, bgate = V_(k, "gate", [128, D])
    build_mod(k, ccT, bcc, wmod_d, bmodB_d, D, 1, [lambda c0, n: (gate[:, c0:c0 + n], bgate)])
    cw, bcw = V_(k, "cw_sb", [128, 88, 9]); k.load(cw[:], cw_d, bcw)
    cb, bcb = V_(k, "cb_sb", [128, 88]); k.load(cb[:], cb_d, bcb)
    hlT, bhlT = V_(k, "hlT_sb", [128, 16, 1152], BF16)
    k.load(hlT[:], hl_d.rearrange("(kc p) n -> p kc n", p=128), bhlT, eng="pool")
    aT, baT = V_(k, "aT", [128, 44, 512], BF16)
    NWD = 128
    wds = [V_(k, f"wd{i}", [128, 44, NWD], BF16) for i in range(2)]
    PF = 2
    wu = [(k.sb(f"wu{i}", [128, 16, 256], BF16), Buf(), Buf()) for i in range(PF + 1)]

    def issue_wu(n_it):
        j_ = n_it % 44
        w_, bg__, bv__ = wu[n_it % (PF + 1)]
        k.load(w_[:, :, 0:128], wup_d[:, j_ * 128:(j_ + 1) * 128].rearrange("(kc p) n -> p kc n", p=128), bg__, eng="pool")
        k.load(w_[:, :, 128:256], wup_d[:, FF + j_ * 128:FF + (j_ + 1) * 128].rearrange("(kc p) n -> p kc n", p=128), bv__, eng="pool")
    zp = [[V_(k, f"zp{i}{c}", [128, 10, 66]) for c in range(2)] for i in range(2)]
    for i in range(2):
        for c in range(2):
            S.add("pool", lambda e, t=zp[i][c][0]: e.memset(t[:], 0.0), writes=[zp[i][c][1]])
    acc = [[V_(k, f"acc{i}{c}", [128, 8, 64]) for c in range(2)] for i in range(2)]
    sgt = [V_(k, f"sg{i}", [128, 8, 64]) for i in range(2)]
    ptmps = [V_(k, f"ptmp{i}", [128, 8, 64]) for i in range(4)]
    tcnt = [0]
    xc = [V_(k, f"xc{i}", [128, NWD]) for i in range(2)]
    Tt = [V_(k, f"Tt{i}", [128, NWD]) for i in range(2)]
    it = 0
    for h in range(2):
        e0 = h * 8 * 64
        for j in range(44):
            if it == 0:
                for pf in range(PF):
                    issue_wu(pf)
            if it + PF < 88:
                issue_wu(it + PF)
            w, bwg_, bwv_ = wu[it % (PF + 1)]
            for c in range(2):
                z, bz = zp[it % 2][c]
                a_, ba_ = acc[it % 2][c]
                ch = j + 44 * c
                for (c0, n, r0) in ((0, 512, 0), (512, 128, 8)):
                    pt, bpt = k.bank()
                    k.mm(pt[:, :n], bpt, [(w[:, kc, c * 128:(c + 1) * 128], hlT[:, kc, e0 + c0:e0 + c0 + n]) for kc in range(16)],
                         reads=[(bwg_, bwv_)[c], bhlT])
                    S.add("act", lambda e, pt=pt, z=z, n=n, r0=r0: e.copy(out=z[:, r0:r0 + n // 64, 1:65],
                                                                          in_=pt[:, :n].rearrange("p (r c) -> p r c", c=64)), reads=[bpt], writes=[bz])
                if c == 0:
                    S.add("dve", lambda e, z=z, a_=a_, ch=ch: e.tensor_scalar(out=a_[:], in0=z[:, 0:8, 0:64], scalar1=cw[:, ch, 0:1], scalar2=cb[:, ch:ch + 1],
                                                                            op0=ALU.mult, op1=ALU.add), reads=[bz, bcw, bcb], writes=[ba_])
                    for t in range(1, 9):
                        di, dj = divmod(t, 3)
                        S.add("dve", lambda e, z=z, a_=a_, ch=ch, t=t, di=di, dj=dj: e.scalar_tensor_tensor(
                            out=a_[:], in0=z[:, di:di + 8, dj:dj + 64], scalar=cw[:, ch, t:t + 1], in1=a_[:], op0=ALU.mult, op1=ALU.add),
                            reads=[bz, bcw, ba_], writes=[ba_])
                else:
                    S.add("act", lambda e, z=z, a_=a_, ch=ch: e.activation(out=a_[:], in_=z[:, 0:8, 0:64], func=AF.Identity, bias=cb[:, ch:ch + 1],
                                                                         scale=cw[:, ch, 0:1]), reads=[bz, bcw, bcb], writes=[ba_])
                    for t in range(1, 9):
                        di, dj = divmod(t, 3)
                        pt_, bpt_ = ptmps[tcnt[0] % 4]
                        tcnt[0] += 1
                        S.add("act", lambda e, z=z, ch=ch, t=t, di=di, dj=dj, pt_=pt_: e.activation(out=pt_[:], in_=z[:, di:di + 8, dj:dj + 64], func=AF.Copy,
                                                                                                 scale=cw[:, ch, t:t + 1]), reads=[bz, bcw], writes=[bpt_])
                        S.add("pool", lambda e, a_=a_, pt_=pt_: e.tensor_tensor(out=a_[:], in0=a_[:], in1=pt_[:], op=ALU.add), reads=[ba_, bpt_], writes=[ba_])
            ag, bag = acc[it % 2][0]
            av, bav = acc[it % 2][1]
            sg, bsg = sgt[it % 2]
            S.add("act", lambda e, ag=ag, sg=sg: e.activation(out=sg[:], in_=ag[:], func=AF.Silu), reads=[bag], writes=[bsg])
            S.add("dve", lambda e, sg=sg, av=av, j=j: e.tensor_tensor(out=aT[:, j, :].rearrange("p (r c) -> p r c", c=64), in0=sg[:], in1=av[:], op=ALU.mult),
                  reads=[bsg, bav], writes=[baT])
            it += 1
        for nb in range(D // NWD):
            cs = slice(nb * NWD, (nb + 1) * NWD)
            wd, bwd = wds[nb % 2]
            k.load(wd[:], wdn_d[:, cs].rearrange("(j p) n -> p j n", p=128), bwd, eng="pool")
            for tt_ in range(4):
                gi = h * 4 + tt_
                rs = slice(gi * 128, (gi + 1) * 128)
                x, bx = xc[(nb * 4 + tt_) % 2]
                T, bT = Tt[(nb * 4 + tt_) % 2]
                k.load(x[:], x_d[rs, cs], bx, eng="sp")
                pt, bpt = k.bank()
                k.mm(pt[:, :NWD], bpt, [(aT[:, j, tt_ * 128:(tt_ + 1) * 128], wd[:, j, :]) for j in range(44)], reads=[baT, bwd])
                S.add("dve", lambda e, pt=pt, T=T, cs=cs: e.tensor_tensor(out=T[:], in0=pt[:, :NWD], in1=gate[:, cs], op=ALU.mult), reads=[bpt, bgate], writes=[bT])
                S.add("pool", lambda e, T=T, x=x: e.tensor_tensor(out=T[:], in0=T[:], in1=x[:], op=ALU.add), reads=[bT, bx], writes=[bT])
                if final:
                    S.add("sp", lambda e, T=T, rs=rs, cs=cs: e.dma_start(out=xs_d[rs, cs], in_=T[:]), reads=[bT], writes=[bxs], dma=True)
                else:
                    k.store(xo_d[rs, cs], T[:], bT)
    if final:
        gfin, bgfin = V_(k, "gfin_sb", [128, D]); k.load(gfin[:], gfin_d, bgfin)
        XR = [(aT[:].rearrange("p j n -> p (j n)").bitcast(F32)[:, 0:D], baT)] * 2
        bjunk = bhlT
        SS = [V_(k, f"ss{i}", [128, 1]) for i in range(2)]
        for i in range(8):
            rs = slice(i * 128, (i + 1) * 128)
            x, bx = XR[i % 2]; ss, bss = SS[i % 2]
            k.load(x[:], xs_d[rs, :], bx, eng="sp", reads=[bxs])
            S.add("act", lambda e, x=x, ss=ss: e.activation(out=hlT[:, 0:2, 0:1024], in_=x[:].rearrange("p (a b) -> p a b", a=2), func=AF.Square,
                                                           accum_out=ss[:, 0:1]), reads=[bx], writes=[bjunk, bss])
            S.add("dve", lambda e, ss=ss: e.tensor_scalar(out=ss[:], in0=ss[:], scalar1=1.0 / D, scalar2=1e-6, op0=ALU.mult, op1=ALU.add), reads=[bss], writes=[bss])
            S.add("act", lambda e, ss=ss: e.sqrt(out=ss[:], in_=ss[:]), reads=[bss], writes=[bss])
            S.add("dve", lambda e, ss=ss: e.reciprocal(out=ss[:], in_=ss[:]), reads=[bss], writes=[bss])
            S.add("dve", lambda e, x=x, ss=ss: e.scalar_tensor_tensor(out=x[:], in0=x[:], scalar=ss[:, 0:1], in1=gfin[:], op0=ALU.mult, op1=ALU.mult),
                  reads=[bx, bss, bgfin], writes=[bx])
            k.store(xo_d[rs, :], x[:], bx)
    return k.finish()


def launch_ffn(inp, x, hl, layer, final):
    nc = build_ffn(final)
    cols = np.arange(5 * D, 6 * D)
    wmod = np.ascontiguousarray(inp["w_mod"][layer][:, cols]); bmodB = bc(inp["b_mod"][layer][cols])
    cw = np.ascontiguousarray(inp["ffn_conv_w"][layer].reshape(9, 88, 128).transpose(2, 1, 0))
    cb = np.ascontiguousarray(inp["ffn_conv_b"][layer].reshape(88, 128).T)
    maps = []
    for c in range(NCORES):
        b, q = c // 4, c % 4
        ext = np.zeros((1152, D), np.float32)
        lo, hi = q * 1024 - 64, q * 1024 + 1024 + 64
        slo, shi = max(lo, 0), min(hi, 4096)
        ext[slo - lo:shi - lo] = hl[b, slo:shi]
        m = dict(hlT=np.ascontiguousarray(ext.T), x=np.ascontiguousarray(x[b, q * 1024:(q + 1) * 1024]),
                 ccT=np.ascontiguousarray(inp["c"][b].reshape(1, 16, 128).transpose(2, 1, 0)), wmod=wmod, bmodB=bmodB,
                 w_up=inp["ffn_w_up"][layer], cw=cw, cb=cb, w_down=inp["ffn_w_down"][layer])
        if final:
            m["gfin"] = bc(inp["final_norm_g"])
        maps.append(m)
    res = run(nc, maps)
    xo = np.zeros((2, 4096, D), np.float32)
    for c in range(NCORES):
        xo[c // 4, (c % 4) * 1024:(c % 4 + 1) * 1024] = res[c]["xo"]
    return xo


def rms_mod(k, x, bx, A, bA, B, bB, out, bout, junk, ss, t1):
    S = k.S
    S.add("act", lambda e: e.activation(out=junk[0][:], in_=x, func=AF.Square, accum_out=ss[0][:, 0:1]), reads=[bx], writes=[junk[1], ss[1]])
    S.add("dve", lambda e: e.tensor_scalar(out=ss[0][:], in0=ss[0][:], scalar1=1.0 / D, scalar2=1e-6, op0=ALU.mult, op1=ALU.add), reads=[ss[1]], writes=[ss[1]])
    S.add("act", lambda e: e.sqrt(out=ss[0][:], in_=ss[0][:]), reads=[ss[1]], writes=[ss[1]])
    S.add("dve", lambda e: e.reciprocal(out=ss[0][:], in_=ss[0][:]), reads=[ss[1]], writes=[ss[1]])
    S.add("dve", lambda e: e.scalar_tensor_tensor(out=t1[0][:], in0=x, scalar=ss[0][:, 0:1], in1=A, op0=ALU.mult, op1=ALU.mult),
          reads=[bx, ss[1], bA], writes=[t1[1]])
    S.add("dve", lambda e: e.tensor_tensor(out=out, in0=t1[0][:], in1=B, op=ALU.add), reads=[t1[1], bB], writes=[bout])


def build_sgu():
    k = KB()
    S = k.S
    x_d = k.din("x", [1024, D]); ccT_d = k.din("ccT", [128, 16, 1])
    wmA_d = k.din("wmodA", [D, 2 * D]); bmA_d = k.din("bmodBA", [128, 2 * D]); gM_d = k.din("gMix", [128, D])
    wmB_d = k.din("wmodB", [D, 3 * D]); bmB_d = k.din("bmodBB", [128, 3 * D]); gF_d = k.din("gF", [128, D])
    win_d = k.din("w_in", [D, 8192]); lng_d = k.din("lngT", [128, 32]); lnb_d = k.din("lnbT", [128, 32])
    wsT_d = k.din("wsT", [128, 16, 128]); bsB_d = k.din("bsB", [128, 16, 128]); wo_d = k.din("w_out", [4096, D])
    ident_d = k.din("ident", [128, 128]); ones_d = k.din("ones", [128, 128])
    x3_d = k.dout("x3", [1024, D]); hl_d = k.dout("hl", [1024, D])
    bx3 = Buf("x3d")
    ident, bident = V_(k, "ident_sb", [128, 128], BF16); k.load(ident[:], ident_d, bident, eng="pool")
    ones, bones = V_(k, "ones_sb", [128, 128], BF16); k.load(ones[:], ones_d, bones, eng="pool")
    ccT, bcc = V_(k, "ccT_sb", [128, 16, 1]); k.load(ccT[:], ccT_d, bcc)
    M0 = V_(k, "M0", [128, D]); M1 = V_(k, "M1", [128, D]); G2 = V_(k, "G2", [128, D])
    t1 = V_(k, "t1", [128, D]); junk = V_(k, "junk", [128, D], BF16); ss = V_(k, "ss", [128, 1])
    k.load(t1[0][:], gM_d, t1[1])
    sc = k.sb("sc", [128, 16, 1]); bsc = Buf()
    scb = k.sb("scb", [128, 16, 1, 128]); bscb = Buf()
    S.add("act", lambda e: e.activation(out=sc[:], in_=ccT[:], func=AF.Silu), reads=[bcc], writes=[bsc])
    S.add("dve", lambda e: e.tensor_copy(out=scb[:], in_=sc[:].unsqueeze(3).to_broadcast([128, 16, 1, 128])), reads=[bsc], writes=[bscb])
    NW = 128
    wst = [(k.sb(f"wst{i}", [128, 16, NW]), Buf()) for i in range(2)]
    bmt = [(k.sb(f"bmt{i}", [128, NW]), Buf()) for i in range(2)]

    def mod(wmod_d, bmodB_d, ncols, outf):
        for nb in range(ncols // NW):
            w, bw = wst[nb % 2]
            bm, bbm = bmt[nb % 2]
            k.load(w[:], wmod_d[:, nb * NW:(nb + 1) * NW].rearrange("(kc p) n -> p kc n", p=128), bw)
            k.load(bm[:], bmodB_d[:, nb * NW:(nb + 1) * NW], bbm)
            pt, bpt = k.bank()
            k.mm(pt[:, :NW], bpt, [(scb[:, kc, 0, :], w[:, kc, :]) for kc in range(16)], reads=[bscb, bw])
            dst, bdst = outf(nb * NW, NW)
            S.add("dve", lambda e, pt=pt, dst=dst, bm=bm: e.tensor_tensor(out=dst, in0=pt[:, :NW], in1=bm[:], op=ALU.add), reads=[bpt, bbm], writes=[bdst])
    MA = [M0, M1]
    mod(wmA_d, bmA_d, 2 * D, lambda c0, n: (MA[c0 // D][0][:, c0 % D:c0 % D + n], MA[c0 // D][1]))
    S.add("dve", lambda e: e.scalar_tensor_tensor(out=M1[0][:], in0=M1[0][:], scalar=1.0, in1=t1[0][:], op0=ALU.add, op1=ALU.mult),
          reads=[M1[1], t1[1]], writes=[M1[1]])
    mod(wmB_d[:, 0:D], bmB_d[:, 0:D], D, lambda c0, n: (G2[0][:, c0:c0 + n], G2[1]))
    lng = V_(k, "lng", [128, 32]); k.load(lng[0][:], lng_d, lng[1])
    lnb = V_(k, "lnb", [128, 32]); k.load(lnb[0][:], lnb_d, lnb[1])
    wsT = V_(k, "wsT_sb", [128, 16, 128], BF16); k.load(wsT[0][:], wsT_d, wsT[1], eng="pool")
    bsB = V_(k, "bsB_sb", [128, 16, 128]); k.load(bsB[0][:], bsB_d, bsB[1])
    rsB = V_(k, "rsB", [128, 16, 128])
    for g4 in range(4):
        pt, bpt = k.bank()
        k.mm(pt[:], bpt, [(ones[:], wsT[0][:, g4 * 4:(g4 + 1) * 4, :].rearrange("p g q -> p (g q)"))], reads=[bones, wsT[1]])
        S.add("act", lambda e, pt=pt, g4=g4: e.copy(out=rsB[0][:, g4 * 4:(g4 + 1) * 4, :].rearrange("p g q -> p (g q)"), in_=pt[:]), reads=[bpt], writes=[rsB[1]])
    hlT = V_(k, "hlT", [128, 16, 512], BF16)
    hlb = V_(k, "hlb", [128, D], BF16)
    vn = k.sb("vn", [128, 4, 4096], BF16)
    bvn = [Buf(f"vn{c}") for c in range(32)]
    X = V_(k, "X", [128, D])
    wv = V_(k, "wv", [128, 16, 512], BF16)
    wu = [V_(k, f"wu{i}", [128, 16, 128], BF16) for i in range(2)]
    wo = V_(k, "wo", [128, 32, 256], BF16)
    ga = V_(k, "ga", [128, 512]); gb = V_(k, "gb", [128, 512])
    sums = V_(k, "sums", [128, 4, 8]); sqs = V_(k, "sqs", [128, 4, 8])
    mean = V_(k, "mean", [128, 4]); var = V_(k, "var", [128, 4]); msq = V_(k, "msq", [128, 4])
    uTb = V_(k, "uTb", [128, 512]); svt = V_(k, "svt", [128, 4, 128]); bfull = V_(k, "bfull", [128, 128])
    xc = [V_(k, f"xc{i}", [128, 256]) for i in range(2)]
    Tt = [V_(k, f"Tt{i}", [128, 256]) for i in range(2)]
    for hf in range(2):
        for i in range(4):
            rs = slice(hf * 512 + i * 128, hf * 512 + (i + 1) * 128)
            k.load(X[0][:], x_d[rs, :], X[1], eng="sp")
            rms_mod(k, X[0][:], X[1], M1[0][:], M1[1], M0[0][:], M0[1], hlb[0][:], hlb[1], junk, ss, t1)
            transpose16(k, hlb[0], hlb[1], hlT[0][:, :, i * 128:(i + 1) * 128], hlT[1], ident, bident)
        for cbi in range(8):
            load_w16(k, wv[0], wv[1], win_d[:, 4096 + cbi * 512:4096 + (cbi + 1) * 512])
            for i in range(4):
                pt, bpt = k.bank()
                k.mm(pt[:], bpt, [(hlT[0][:, kc, i * 128:(i + 1) * 128], wv[0][:, kc, :]) for kc in range(16)], reads=[hlT[1], wv[1]])
                blks = bvn[cbi * 4:(cbi + 1) * 4]
                dst = vn[:, i, cbi * 512:(cbi + 1) * 512]
                S_ = k.S
                S_.add("act", lambda e, pt=pt: e.activation(out=ga[0][:], in_=pt[:], func=AF.Square), reads=[bpt], writes=[ga[1]])
                S_.add("dve", lambda e: e.tensor_scalar(out=ga[0][:], in0=ga[0][:], scalar1=0.044715, scalar2=1.0, op0=ALU.mult, op1=ALU.add), reads=[ga[1]], writes=[ga[1]])
                S_.add("dve", lambda e, pt=pt: e.tensor_tensor(out=ga[0][:], in0=ga[0][:], in1=pt[:], op=ALU.mult), reads=[ga[1], bpt], writes=[ga[1]])
                S_.add("act", lambda e: e.activation(out=gb[0][:], in_=ga[0][:], func=AF.Sigmoid, scale=1.5957691216057308), reads=[ga[1]], writes=[gb[1]])
                S_.add("dve", lambda e, pt=pt, dst=dst, i=i, cbi=cbi: e.scalar_tensor_tensor(out=dst, in0=pt[:], scalar=1.0, in1=gb[0][:], op0=ALU.mult, op1=ALU.mult,
                                                                                         accum_out=sums[0][:, i, cbi:cbi + 1]),
                       reads=[bpt, gb[1]], writes=blks + [sums[1]])
                S_.add("act", lambda e, dst=dst, i=i, cbi=cbi: e.activation(out=ga[0][:], in_=dst, func=AF.Square, accum_out=sqs[0][:, i, cbi:cbi + 1]),
                       reads=blks, writes=[ga[1], sqs[1]])
        S.add("dve", lambda e: e.tensor_reduce(out=mean[0][:], in_=sums[0][:], axis=mybir.AxisListType.X, op=ALU.add), reads=[sums[1]], writes=[mean[1]])
        S.add("dve", lambda e: e.tensor_reduce(out=var[0][:], in_=sqs[0][:], axis=mybir.AxisListType.X, op=ALU.add), reads=[sqs[1]], writes=[var[1]])
        S.add("dve", lambda e: e.tensor_single_scalar(out=mean[0][:], in_=mean[0][:], scalar=1.0 / 4096, op=ALU.mult), reads=[mean[1]], writes=[mean[1]])
        S.add("dve", lambda e: e.tensor_tensor(out=msq[0][:], in0=mean[0][:], in1=mean[0][:], op=ALU.mult), reads=[mean[1]], writes=[msq[1]])
        S.add("dve", lambda e: e.scalar_tensor_tensor(out=var[0][:], in0=var[0][:], scalar=1.0 / 4096, in1=msq[0][:], op0=ALU.mult, op1=ALU.subtract),
              reads=[var[1], msq[1]], writes=[var[1]])
        S.add("dve", lambda e: e.tensor_single_scalar(out=var[0][:], in_=var[0][:], scalar=1e-5, op=ALU.add), reads=[var[1]], writes=[var[1]])
        S.add("act", lambda e: e.sqrt(out=var[0][:], in_=var[0][:]), reads=[var[1]], writes=[var[1]])
        S.add("dve", lambda e: e.reciprocal(out=var[0][:], in_=var[0][:]), reads=[var[1]], writes=[var[1]])
        for i in range(4):
            S.add("dve", lambda e, i=i: e.tensor_scalar(out=vn[:, i, :], in0=vn[:, i, :], scalar1=mean[0][:, i:i + 1], scalar2=var[0][:, i:i + 1],
                                                       op0=ALU.subtract, op1=ALU.mult), reads=bvn + [mean[1], var[1]], writes=bvn)
        for cbk in range(32):
            g = cbk // 2
            w, bw = wu[cbk % 2]
            load_w16(k, w, bw, win_d[:, cbk * 128:(cbk + 1) * 128])
            pt, bpt = k.bank()
            k.mm(pt[:], bpt, [(w[:, kc, :], hlT[0][:, kc, :]) for kc in range(16)], reads=[bw, hlT[1]])
            gelu_tanh(k, uTb[0][:], uTb[1], pt[:], bpt, (ga[0][:], ga[1]), (gb[0][:], gb[1]))
            pt2, bpt2 = k.bank()

            def fn(e, pt2=pt2, cbk=cbk, g=g):
                ins = None
                for i in range(4):
                    ins = e.matmul(pt2[:, i * 128:(i + 1) * 128], vn[:, i, cbk * 128:(cbk + 1) * 128], wsT[0][:, g, :], start=True, stop=True)
                return ins
            S.add("pe", fn, reads=[bvn[cbk], wsT[1]], writes=[bpt2])
            S.add("dve", lambda e, cbk=cbk, g=g: e.scalar_tensor_tensor(out=bfull[0][:], in0=rsB[0][:, g, :], scalar=lnb[0][:, cbk:cbk + 1], in1=bsB[0][:, g, :],
                                                                      op0=ALU.mult, op1=ALU.add), reads=[rsB[1], lnb[1], bsB[1]], writes=[bfull[1]])
            S.add("dve", lambda e, pt2=pt2, cbk=cbk: e.scalar_tensor_tensor(out=svt[0][:], in0=pt2[:].rearrange("p (i q) -> p i q", q=128), scalar=lng[0][:, cbk:cbk + 1],
                                                                          in1=bfull[0][:].unsqueeze(1).to_broadcast([128, 4, 128]), op0=ALU.mult, op1=ALU.add),
                  reads=[bpt2, lng[1], bfull[1]], writes=[svt[1]])
            S.add("dve", lambda e, cbk=cbk: e.tensor_tensor(out=vn[:, :, cbk * 128:(cbk + 1) * 128], in0=uTb[0][:].rearrange("p (i q) -> p i q", q=128),
                                                            in1=svt[0][:], op=ALU.mult), reads=[uTb[1], svt[1]], writes=[bvn[cbk]])
        for nb in range(8):
            cs = slice(nb * 256, (nb + 1) * 256)
            load_w16(k, wo[0], wo[1], wo_d[:, cs], nk=32)
            for i in range(4):
                rs = slice(hf * 512 + i * 128, hf * 512 + (i + 1) * 128)
                x, bx = xc[(nb * 4 + i) % 2]
                T, bT = Tt[(nb * 4 + i) % 2]
                k.load(x[:], x_d[rs, cs], bx, eng="sp")
                pt, bpt = k.bank()
                k.mm(pt[:, :256], bpt, [(vn[:, i, cbk * 128:(cbk + 1) * 128], wo[0][:, cbk, :]) for cbk in range(32)], reads=bvn + [wo[1]])
                S.add("dve", lambda e, pt=pt, T=T, cs=cs: e.tensor_tensor(out=T[:], in0=pt[:, :256], in1=G2[0][:, cs], op=ALU.mult), reads=[bpt, G2[1]], writes=[bT])
                S.add("pool", lambda e, T=T, x=x: e.tensor_tensor(out=T[:], in0=T[:], in1=x[:], op=ALU.add), reads=[bT, bx], writes=[bT])
                op = S.add("sp", lambda e, T=T, rs=rs, cs=cs: e.dma_start(out=x3_d[rs, cs], in_=T[:]), reads=[bT], writes=[bx3], dma=True)
                k.finals.append(op)
    k.load(t1[0][:], gF_d, t1[1])
    mod(wmB_d[:, D:3 * D], bmB_d[:, D:3 * D], 2 * D, lambda c0, n: (MA[c0 // D][0][:, c0 % D:c0 % D + n], MA[c0 // D][1]))
    S.add("dve", lambda e: e.scalar_tensor_tensor(out=M1[0][:], in0=M1[0][:], scalar=1.0, in1=t1[0][:], op0=ALU.add, op1=ALU.mult),
          reads=[M1[1], t1[1]], writes=[M1[1]])
    HO = V_(k, "HO", [128, D])
    for i in range(8):
        rs = slice(i * 128, (i + 1) * 128)
        k.load(X[0][:], x3_d[rs, :], X[1], eng="sp", reads=[bx3])
        rms_mod(k, X[0][:], X[1], M1[0][:], M1[1], M0[0][:], M0[1], HO[0][:], HO[1], junk, ss, t1)
        k.store(hl_d[rs, :], HO[0][:], HO[1])
    return k.finish()


def launch_sgu(inp, x2):
    nc = build_sgu()
    L = 1
    def mcols(js):
        cols = np.concatenate([np.arange(j * D, (j + 1) * D) for j in js])
        return np.ascontiguousarray(inp["w_mod"][L][:, cols]), bc(inp["b_mod"][L][cols])
    wmA, bmA = mcols((0, 1)); wmB, bmB = mcols((2, 3, 4))
    lngT = np.ascontiguousarray(inp["sgu_ln_g"][0].reshape(32, 128).T); lnbT = np.ascontiguousarray(inp["sgu_ln_b"][0].reshape(32, 128).T)
    wsT = np.ascontiguousarray(inp["sgu_w_s"][0].transpose(2, 0, 1))
    bsB = np.ascontiguousarray(np.broadcast_to(inp["sgu_b_s"][0][None], (128, 16, 128)))
    maps = []
    for c in range(NCORES):
        b, q = c // 4, c % 4
        maps.append(dict(x=np.ascontiguousarray(x2[b, q * 1024:(q + 1) * 1024]), ccT=np.ascontiguousarray(inp["c"][b].reshape(1, 16, 128).transpose(2, 1, 0)),
                         wmodA=wmA, bmodBA=bmA, gMix=bc(inp["mix_norm_g"][L]), wmodB=wmB, bmodBB=bmB, gF=bc(inp["ffn_norm_g"][L]),
                         w_in=inp["sgu_w_in"][0], lngT=lngT, lnbT=lnbT, wsT=wsT, bsB=bsB, w_out=inp["sgu_w_out"][0],
                         ident=np.eye(128, dtype=np.float32), ones=np.ones((128, 128), np.float32)))
    res = run(nc, maps)
    x3 = np.zeros((2, 4096, D), np.float32); hl = np.zeros((2, 4096, D), np.float32)
    for c in range(NCORES):
        x3[c // 4, (c % 4) * 1024:(c % 4 + 1) * 1024] = res[c]["x3"]
        hl[c // 4, (c % 4) * 1024:(c % 4 + 1) * 1024] = res[c]["hl"]
    return x3, hl


def kernel(**inputs):
    inp = {k_: np.asarray(v, dtype=np.float32) for k_, v in inputs.items()}
    u_lat, u_ctx = launch_L1(inp)
    y_f, y_r = launch_L2(inp, u_lat, u_ctx)
    ml = launch_L3a(inp, y_f, y_r, u_lat)
    x1, hl0 = launch_resnorm(inp, inp["x"], ml, 0, inp["ffn_norm_g"][0], (2, 3, 4))
    x2 = launch_ffn(inp, x1, hl0, 0, False)
    x3, hl1 = launch_sgu(inp, x2)
    out = launch_ffn(inp, x3, hl1, 1, True)
    return out.astype(np.float32)
```

```python
import contextlib
import numpy as np
import concourse.bass as bass
import concourse.mybir as mybir
from concourse.bass_utils import run_bass_kernel_spmd

F32 = mybir.dt.float32
BF16 = mybir.dt.bfloat16
ALU = mybir.AluOpType
AF = mybir.ActivationFunctionType
NCORES = 8
D = 2048


class Buf:
    __slots__ = ("name", "last_writer", "readers")

    def __init__(self, name=""):
        self.name = name
        self.last_writer = None
        self.readers = []


class Op:
    __slots__ = ("eng", "fn", "deps", "dma", "sig", "needed", "inc")

    def __init__(self, eng, fn, dma, inc):
        self.eng, self.fn, self.dma, self.inc = eng, fn, dma, inc
        self.deps, self.sig, self.needed = [], None, False


ENGS = ("pe", "dve", "act", "pool", "sp")
N_DMA_SEMS = 8


class Sched:
    def __init__(self, nc):
        self.nc = nc
        self.ops = []

    def add(self, eng, fn, reads=(), writes=(), dma=False):
        op = Op(eng, fn, dma, 16 if dma else 1)
        deps = set()
        for r in reads:
            if r.last_writer is not None:
                deps.add(r.last_writer)
        for w in writes:
            if w.last_writer is not None:
                deps.add(w.last_writer)
            deps.update(w.readers)
        for r in reads:
            r.readers.append(op)
        for w in writes:
            w.last_writer = op
            w.readers = []
        op.deps = [d for d in deps if not (d.eng == "pe" and eng == "pe")]
        for d in op.deps:
            d.needed = True
        if dma:
            op.needed = True
        self.ops.append(op)
        return op

    def barrier(self):
        last = {}
        dmas = {}
        for op in self.ops:
            last[op.eng] = op
            if op.dma:
                dmas.setdefault(op.eng, []).append(op)
        deps = list(last.values())
        for e_, lst in dmas.items():
            deps.extend(lst[-N_DMA_SEMS:])
        for d in deps:
            d.needed = True
        for e_ in ENGS:
            op = Op(e_, lambda e: e.nop(), False, 1)
            op.deps = list(deps)
            self.ops.append(op)

    def emit(self, final_ops):
        nc = self.nc
        for o in final_ops:
            o.needed = True
        with contextlib.ExitStack() as st:
            comp_sem = {e: st.enter_context(nc.semaphore(f"s_{e}")) for e in ENGS}
            dma_sems = {e: [st.enter_context(nc.semaphore(f"d_{e}{i}")) for i in range(N_DMA_SEMS)]
                        for e in ("sp", "act", "pool")}
            comp_cnt = {e: 0 for e in ENGS}
            dma_cnt = {e: 0 for e in ENGS}
            sem_total = {}
            per_eng = {e: [] for e in ENGS}
            for op in self.ops:
                per_eng[op.eng].append(op)
                if op.dma:
                    n = dma_cnt[op.eng]
                    dma_cnt[op.eng] += 1
                    sem = dma_sems[op.eng][n % N_DMA_SEMS]
                    before = sem_total.get(id(sem), 0)
                    sem_total[id(sem)] = before + op.inc
                    op.sig = (sem, before + op.inc, before)
                elif op.needed:
                    comp_cnt[op.eng] += 1
                    op.sig = (comp_sem[op.eng], comp_cnt[op.eng], None)
            final = list(final_ops)

            def run_engine(ename, eng):
                seen = {}

                def wait(sem, val):
                    if seen.get(id(sem), 0) >= val:
                        return
                    eng.wait_ge(sem, val)
                    seen[id(sem)] = val

                for op in per_eng[ename]:
                    for d in op.deps:
                        wait(d.sig[0], d.sig[1])
                    if op.dma and op.sig[2] > 0:
                        wait(op.sig[0], op.sig[2])
                    ins = op.fn(eng)
                    if op.sig is not None:
                        ins.then_inc(op.sig[0], op.inc if op.dma else 1)
                if ename == "sp":
                    for o in final:
                        wait(o.sig[0], o.sig[1])

            with nc.Block() as block:
                block.tensor(lambda e: run_engine("pe", e))
                block.vector(lambda e: run_engine("dve", e))
                block.scalar(lambda e: run_engine("act", e))
                block.gpsimd(lambda e: run_engine("pool", e))
                block.sync(lambda e: run_engine("sp", e))


class KB:
    def __init__(self):
        self.nc = bass.Bass("TRN2", target_bir_lowering=False)
        self.S = Sched(self.nc)
        self.st = contextlib.ExitStack()
        self.finals = []
        self.banks = []
        for i in range(8):
            t = self.st.enter_context(self.nc.psum_tensor(f"ps{i}", [128, 512], F32))
            self.banks.append((t, Buf(f"ps{i}")))
        self.bi = 0
        self.rr = 0
        self.prefix = ""
        self.io = {}
        self.stacks = [self.st]

    @contextlib.contextmanager
    def stage(self, prefix, io=None):
        old = (self.prefix, self.io)
        self.prefix, self.io = prefix, dict(io or {})
        st = contextlib.ExitStack()
        self.stacks.append(st)
        try:
            yield self
        finally:
            self.S.barrier()
            self.stacks.pop()
            st.close()
            self.prefix, self.io = old

    def bank(self):
        b = self.banks[self.bi % 8]
        self.bi += 1
        return b

    def sb(self, name, shape, dt=F32):
        return self.stacks[-1].enter_context(self.nc.sbuf_tensor(self.prefix + name, list(shape), dt))

    def din(self, name, shape, dt=F32):
        if name in self.io:
            return self.io[name]
        return self.nc.dram_tensor(self.prefix + name, list(shape), dt, kind="ExternalInput").ap()

    def dout(self, name, shape, dt=F32):
        if name in self.io:
            return self.io[name]
        return self.nc.dram_tensor(self.prefix + name, list(shape), dt, kind="ExternalOutput").ap()

    def scratch(self, name, shape, dt=F32):
        return self.nc.dram_tensor(name, list(shape), dt, kind="Internal").ap()

    def load(self, dst_ap, src_ap, buf, eng=None, reads=()):
        if eng is None:
            eng = ("sp", "pool")[self.rr % 2]
            self.rr += 1
        return self.S.add(eng, lambda e: e.dma_start(out=dst_ap, in_=src_ap), reads=reads, writes=[buf], dma=True)

    def store(self, dst_ap, src_ap, buf, eng="sp"):
        op = self.S.add(eng, lambda e: e.dma_start(out=dst_ap, in_=src_ap), reads=[buf], dma=True)
        self.finals.append(op)
        return op

    def done(self, own):
        return self.finish() if own else None

    def mm(self, out_ap, out_buf, pairs, reads):
        def fn(e):
            n = len(pairs)
            ins = None
            for i, (l, r) in enumerate(pairs):
                ins = e.matmul(out_ap, l, r, start=(i == 0), stop=(i == n - 1))
            return ins
        return self.S.add("pe", fn, reads=reads, writes=[out_buf])

    def finish(self):
        self.S.emit(self.finals)
        self.st.close()
        return self.nc


def run(nc, in_maps):
    res = run_bass_kernel_spmd(nc, in_maps, core_ids=list(range(NCORES)))
    return res.results


def bc(v, p=128):
    return np.ascontiguousarray(np.broadcast_to(np.asarray(v, np.float32).reshape(1, -1), (p, v.size)))


def norm_mod_T(k, xt_ap, rows, A, B, bA, bB, bx, ident, bident, tag, bufs, want_hl=False):
    S = k.S
    junk, bjunk = bufs["junk"]
    ss, bss = bufs["ss"]
    rstd, brstd = bufs["rstd"]
    t1, bt1 = bufs["t1"]
    hl, bhl = bufs["hl"]
    hlT, bhlT = bufs["hlT"]
    S.add("act", lambda e: e.activation(out=junk[:rows, :], in_=xt_ap, func=AF.Square, accum_out=ss[:rows, 0:1]),
          reads=[bx], writes=[bjunk, bss])
    S.add("dve", lambda e: e.tensor_scalar(out=rstd[:rows, 0:1], in0=ss[:rows, 0:1], scalar1=1.0 / D, scalar2=1e-6,
                                           op0=ALU.mult, op1=ALU.add), reads=[bss], writes=[brstd])
    S.add("act", lambda e: e.sqrt(out=rstd[:rows, 0:1], in_=rstd[:rows, 0:1]), reads=[brstd], writes=[brstd])
    S.add("dve", lambda e: e.reciprocal(out=rstd[:rows, 0:1], in_=rstd[:rows, 0:1]), reads=[brstd], writes=[brstd])
    S.add("dve", lambda e: e.scalar_tensor_tensor(out=t1[:rows, :], in0=xt_ap, scalar=rstd[:rows, 0:1], in1=A[:rows, :],
                                                  op0=ALU.mult, op1=ALU.mult), reads=[bx, brstd, bA], writes=[bt1])
    S.add("dve", lambda e: e.tensor_tensor(out=hl[:rows, :], in0=t1[:rows, :], in1=B[:rows, :], op=ALU.add),
          reads=[bt1, bB], writes=[bhl])
    for half in range(2):
        pt, bpt = k.bank()
        ptb = pt[:].bitcast(BF16)

        def fn(e, half=half, ptb=ptb):
            ins = None
            for j in range(8):
                kc = half * 8 + j
                ins = e.transpose(ptb[:, j * 128:j * 128 + rows], hl[:rows, kc * 128:(kc + 1) * 128], ident[:rows, :rows])
            return ins
        S.add("pe", fn, reads=[bhl, bident], writes=[bpt])
        S.add("act", lambda e, half=half, ptb=ptb: e.copy(
            out=hlT[:, half * 8:(half + 1) * 8, :rows],
            in_=ptb.rearrange("p (j t) -> p j t", t=128)[:, :, :rows]), reads=[bpt], writes=[bhlT])


def build_mod(k, ccT, bcc, wmod_d, bmodB_d, ncols, nrows_m, outs):
    S = k.S
    sc = k.sb("sc", [128, 16, nrows_m]); bsc = Buf()
    scb = k.sb("scb", [128, 16, nrows_m, 128]); bscb = Buf()
    S.add("act", lambda e: e.activation(out=sc[:], in_=ccT[:], func=AF.Silu), reads=[bcc], writes=[bsc])
    S.add("dve", lambda e: e.tensor_copy(out=scb[:], in_=sc[:].unsqueeze(3).to_broadcast([128, 16, nrows_m, 128])),
          reads=[bsc], writes=[bscb])
    NW = 128
    wst = [(k.sb(f"wst{i}", [128, 16, NW]), Buf()) for i in range(2)]
    bmt = [(k.sb(f"bmt{i}", [128, NW]), Buf()) for i in range(2)]
    for nb in range(ncols // NW):
        w, bw = wst[nb % 2]
        bm, bbm = bmt[nb % 2]
        k.load(w[:], wmod_d[:, nb * NW:(nb + 1) * NW].rearrange("(kc p) n -> p kc n", p=128), bw)
        k.load(bm[:], bmodB_d[:, nb * NW:(nb + 1) * NW], bbm)
        for m in range(nrows_m):
            pt, bpt = k.bank()
            k.mm(pt[:, :NW], bpt, [(scb[:, kc, m, :], w[:, kc, :]) for kc in range(16)], reads=[bscb, bw])
            dst, bdst = outs[m](nb * NW, NW)
            S.add("dve", lambda e, pt=pt, dst=dst, bm=bm: e.tensor_tensor(out=dst, in0=pt[:, :NW], in1=bm[:], op=ALU.add),
                  reads=[bpt, bbm], writes=[bdst])


def build_L1():
    k = KB()
    S = k.S
    T = 1088
    xs = k.din("xs", [T, D]); ccT_d = k.din("ccT", [128, 16, 2]); gB_d = k.din("gB", [128, D])
    wmod_d = k.din("wmod", [D, 4096]); bmodB_d = k.din("bmodB", [128, 4096]); win_d = k.din("w_in", [D, D])
    ident_d = k.din("ident", [128, 128])
    u_d = k.dout("u", [T, D])
    ident = k.sb("ident_sb", [128, 128], BF16); bident = Buf()
    k.load(ident[:], ident_d, bident, eng="pool")
    ccT = k.sb("ccT_sb", [128, 16, 2]); bcc = Buf()
    k.load(ccT[:], ccT_d, bcc)
    gB = k.sb("gB_sb", [128, D]); bgB = Buf()
    k.load(gB[:], gB_d, bgB)
    AB = [[(k.sb(f"AB{m}{j}", [128, D]), Buf()) for j in range(2)] for m in range(2)]
    outs = [(lambda c0, n, m=m: (AB[m][c0 // D][0][:, c0 % D:c0 % D + n], AB[m][c0 // D][1])) for m in range(2)]
    build_mod(k, ccT, bcc, wmod_d, bmodB_d, 4096, 2, outs)
    for m in range(2):
        A, bA = AB[m][1]
        S.add("dve", lambda e, A=A: e.scalar_tensor_tensor(out=A[:], in0=A[:], scalar=1.0, in1=gB[:], op0=ALU.add, op1=ALU.mult),
              reads=[bA, bgB], writes=[bA])
    win = k.sb("win", [128, 16, D], BF16); bwin = Buf()
    k.load(win[:], win_d.rearrange("(kc p) n -> p kc n", p=128), bwin, eng="pool")
    xt = [(k.sb(f"xt{i}", [128, D]), Buf()) for i in range(2)]
    ut = [(k.sb(f"ut{i}", [128, D]), Buf()) for i in range(2)]
    junk_ = (k.sb("junk", [128, D], BF16), Buf())
    t1_ = (k.sb("t1", [128, D]), Buf())
    nb_ = [dict(junk=junk_, ss=(k.sb(f"ss{i}", [128, 1]), Buf()),
                rstd=(k.sb(f"rstd{i}", [128, 1]), Buf()), t1=t1_,
                hl=(k.sb(f"hl{i}", [128, D], BF16), Buf()), hlT=(k.sb(f"hlT{i}", [128, 16, 128], BF16), Buf()))
           for i in range(2)]
    for i in range(9):
        rows = 128 if i < 8 else 64
        m = 0 if i < 8 else 1
        x, bx = xt[i % 2]
        k.load(x[:rows, :], xs[i * 128:i * 128 + rows, :], bx, eng="sp")
        bufs = nb_[i % 2]
        norm_mod_T(k, x[:rows, :], rows, AB[m][1][0], AB[m][0][0], AB[m][1][1], AB[m][0][1], bx, ident, bident, "l1", bufs)
        hlT, bhlT = bufs["hlT"]
        u, bu = ut[i % 2]
        for nb in range(4):
            pt, bpt = k.bank()
            k.mm(pt[:rows, :], bpt, [(hlT[:, kc, :rows], win[:, kc, nb * 512:(nb + 1) * 512]) for kc in range(16)],
                 reads=[bhlT, bwin])
            eng = "act" if nb % 2 == 0 else "dve"
            if eng == "act":
                S.add("act", lambda e, pt=pt, u=u, nb=nb, rows=rows: e.copy(out=u[:rows, nb * 512:(nb + 1) * 512], in_=pt[:rows, :]),
                      reads=[bpt], writes=[bu])
            else:
                S.add("dve", lambda e, pt=pt, u=u, nb=nb, rows=rows: e.tensor_copy(out=u[:rows, nb * 512:(nb + 1) * 512], in_=pt[:rows, :]),
                      reads=[bpt], writes=[bu])
        k.store(u_d[i * 128:i * 128 + rows, :], u[:rows, :], bu)
    return k.finish()


def launch_L1(inp):
    nc = build_L1()
    maps = []
    ident = np.eye(128, dtype=np.float32)
    for c in range(NCORES):
        b, q = c // 4, c % 4
        xs = np.concatenate([inp["x"][b, q * 1024:(q + 1) * 1024], inp["ctx"][b, q * 64:(q + 1) * 64]], 0)
        cc = np.stack([inp["c"][b], inp["c_ctx"]], 0)
        ccT = np.ascontiguousarray(cc.reshape(2, 16, 128).transpose(2, 1, 0))
        maps.append(dict(xs=np.ascontiguousarray(xs), ccT=ccT, gB=bc(inp["mix_norm_g"][0]),
                         wmod=np.ascontiguousarray(inp["w_mod"][0][:, :4096]), bmodB=bc(inp["b_mod"][0][:4096]),
                         w_in=np.ascontiguousarray(inp["s5_w_in"][0]), ident=ident))
    res = run(nc, maps)
    u_lat = np.zeros((2, 4096, D), np.float32)
    u_ctx = np.zeros((2, 256, D), np.float32)
    for c in range(NCORES):
        b, q = c // 4, c % 4
        u_lat[b, q * 1024:(q + 1) * 1024] = res[c]["u"][:1024]
        u_ctx[b, q * 64:(q + 1) * 64] = res[c]["u"][1024:]
    return u_lat, u_ctx


SEQ_T = 4352
SEG = 1088


def build_L2():
    k = KB()
    S = k.S
    V = lambda name, shape, dt=F32: (k.sb(name, shape, dt), Buf(name))
    uTd = [k.din("uT_f", [256, 2 * SEQ_T]), k.din("uT_r", [256, 2 * SEQ_T])]
    yTd = [k.dout("yT_f", [256, 2 * SEQ_T]), k.dout("yT_r", [256, 2 * SEQ_T])]
    pd = {n: k.din(n, [128, 2, 8]) for n in ("are", "aim", "lst")}
    pd4 = {n: k.din(n, [128, 2, 8, 16]) for n in ("bre", "bim", "cre", "cim")}
    ident_d = k.din("ident", [128, 128])
    ident, bident = V("ident_sb", [128, 128])
    k.load(ident[:], ident_d, bident)
    P = {}
    for n, d_ in pd.items():
        P[n] = V(n + "_sb", [128, 16])
        k.load(P[n][0][:], d_.rearrange("p d r -> p (d r)"), P[n][1])
    P4 = {}
    for n, d_ in pd4.items():
        P4[n] = V(n + "_sb", [128, 16, 16])
        k.load(P4[n][0][:], d_.rearrange("p d r h -> p (d r) h"), P4[n][1])

    cnt = [0]

    def tmp(shape=(128, 16)):
        cnt[0] += 1
        return V(f"tmp{cnt[0]}", list(shape))

    def tt(out, a, b, op, eng="dve"):
        S.add(eng, lambda e: e.tensor_tensor(out=out[0][:], in0=a[0][:], in1=b[0][:], op=op), reads=[a[1], b[1]], writes=[out[1]])
        return out

    def ts(out, a, s1, op0, s2=None, op1=None):
        if op1 is None:
            S.add("dve", lambda e: e.tensor_single_scalar(out=out[0][:], in_=a[0][:], scalar=s1, op=op0), reads=[a[1]], writes=[out[1]])
        else:
            S.add("dve", lambda e: e.tensor_scalar(out=out[0][:], in0=a[0][:], scalar1=s1, scalar2=s2, op0=op0, op1=op1),
                  reads=[a[1]], writes=[out[1]])
        return out

    def stt(out, a, s, b, op0, op1):
        S.add("dve", lambda e: e.scalar_tensor_tensor(out=out[0][:], in0=a[0][:], scalar=s, in1=b[0][:], op0=op0, op1=op1),
              reads=[a[1], b[1]], writes=[out[1]])
        return out

    def act(out, a, func):
        S.add("act", lambda e: e.activation(out=out[0][:], in_=a[0][:], func=func), reads=[a[1]], writes=[out[1]])
        return out

    are, aim, lst = P["are"], P["aim"], P["lst"]
    dt = act(tmp(), lst, AF.Exp)
    adt = tt(tmp(), are, dt, ALU.mult)
    mag = act(tmp(), adt, AF.Exp)
    th = tt(tmp(), aim, dt, ALU.mult)
    y = ts(tmp(), th, 1.0 / 32.0, ALU.mult)
    y2 = tt(tmp(), y, y, ALU.mult)
    p = ts(tmp(), y2, 1.0 / 362880.0, ALU.mult)
    for c_ in (-1.0 / 5040.0, 1.0 / 120.0, -1.0 / 6.0):
        p = stt(tmp(), p, c_, y2, ALU.add, ALU.mult)
    s = stt(tmp(), p, 1.0, y, ALU.add, ALU.mult)
    q = ts(tmp(), y2, -1.0 / 3628800.0, ALU.mult)
    for c_ in (1.0 / 40320.0, -1.0 / 720.0, 1.0 / 24.0, -0.5):
        q = stt(tmp(), q, c_, y2, ALU.add, ALU.mult)
    c = ts(tmp(), q, 1.0, ALU.add)
    for _ in range(5):
        s_n = stt(tmp(), s, 2.0, c, ALU.mult, ALU.mult)
        t_ = stt(tmp(), s, -2.0, s, ALU.mult, ALU.mult)
        c = ts(tmp(), t_, 1.0, ALU.add)
        s = s_n
    cth, sth = c, s
    lr = tt(tmp(), mag, cth, ALU.mult)
    li = tt(tmp(), mag, sth, ALU.mult)
    den = tt(tmp(), tt(tmp(), are, are, ALU.mult), tt(tmp(), aim, aim, ALU.mult), ALU.add)
    rden = tmp()
    S.add("dve", lambda e: e.reciprocal(out=rden[0][:], in_=den[0][:]), reads=[den[1]], writes=[rden[1]])
    lm1 = ts(tmp(), lr, -1.0, ALU.add)
    kre = tt(tmp(), tt(tmp(), tt(tmp(), lm1, are, ALU.mult), tt(tmp(), li, aim, ALU.mult), ALU.add), rden, ALU.mult)
    kim = tt(tmp(), tt(tmp(), tt(tmp(), li, are, ALU.mult), tt(tmp(), lm1, aim, ALU.mult), ALU.subtract), rden, ALU.mult)

    def bmul(name, a, b4):
        o = V(name, [128, 16, 16])
        S.add("dve", lambda e: e.tensor_tensor(out=o[0][:], in0=b4[0][:], in1=a[0][:].unsqueeze(2).to_broadcast([128, 16, 16]), op=ALU.mult),
              reads=[a[1], b4[1]], writes=[o[1]])
        return o
    bbr = tt(V("bbr", [128, 16, 16]), bmul("m1", kre, P4["bre"]), bmul("m2", kim, P4["bim"]), ALU.subtract)
    bbi = tt(V("bbi", [128, 16, 16]), bmul("m3", kre, P4["bim"]), bmul("m4", kim, P4["bre"]), ALU.add)
    ncim = ts(V("ncim", [128, 16, 16]), P4["cim"], -1.0, ALU.mult)

    def blockify(name, src, dt_):
        o = V(name, [128, 16, 32], dt_)
        S.add("dve", lambda e: e.memset(o[0][:], 0.0), writes=[o[1]])
        S.add("dve", lambda e: e.tensor_copy(out=o[0][0:64, :, 0:16], in_=src[0][0:64, :, :]), reads=[src[1]], writes=[o[1]])
        S.add("dve", lambda e: e.tensor_copy(out=o[0][64:128, :, 16:32], in_=src[0][64:128, :, :]), reads=[src[1]], writes=[o[1]])
        return o
    CTre = blockify("CTre", P4["cre"], BF16)
    CTim = blockify("CTim", ncim, BF16)
    BLre = blockify("BLre", bbr, F32)
    BLim = blockify("BLim", bbi, F32)
    BbT = [V("BbTre", [32, 16, 128], BF16), V("BbTim", [32, 16, 128], BF16)]
    for ci, BL in enumerate((BLre, BLim)):
        for g4 in range(4):
            pt, bpt = k.bank()

            def fn(e, BL=BL, g4=g4, pt=pt):
                ins = None
                for j in range(4):
                    ins = e.transpose(pt[0:32, j * 128:(j + 1) * 128], BL[0][:, g4 * 4 + j, :], ident[:, :])
                return ins
            S.add("pe", fn, reads=[BL[1], bident], writes=[bpt])
            S.add("act", lambda e, pt=pt, g4=g4, ci=ci: e.copy(out=BbT[ci][0][:, g4 * 4:(g4 + 1) * 4, :],
                                                             in_=pt[0:32, :].rearrange("p (j t) -> p j t", t=128)),
                  reads=[bpt], writes=[BbT[ci][1]])

    CM = [V("cma", [128, 16, 64]), V("cmb", [128, 16, 64])]

    def cmul_bc(o_re, o_im, a_re, a_im, p_re, p_im, L, n):
        W_ = o_re[0].shape[2]
        t1 = (CM[0][0][:, :, 0:n], CM[0][1]); t2 = (CM[1][0][:, :, 0:n], CM[1][1])
        pb_re = lambda: p_re[0][:].unsqueeze(2).to_broadcast([128, 16, n])
        pb_im = lambda: p_im[0][:].unsqueeze(2).to_broadcast([128, 16, n])
        S.add("dve", lambda e: e.tensor_tensor(out=t1[0][:], in0=a_re[0][:, :, 0:n], in1=pb_re(), op=ALU.mult), reads=[a_re[1], p_re[1]], writes=[t1[1]])
        S.add("dve", lambda e: e.tensor_tensor(out=t2[0][:], in0=a_im[0][:, :, 0:n], in1=pb_im(), op=ALU.mult), reads=[a_im[1], p_im[1]], writes=[t2[1]])
        S.add("dve", lambda e: e.tensor_tensor(out=o_re[0][:, :, L:L + n], in0=t1[0][:], in1=t2[0][:], op=ALU.subtract), reads=[t1[1], t2[1]], writes=[o_re[1]])
        S.add("dve", lambda e: e.tensor_tensor(out=t1[0][:], in0=a_re[0][:, :, 0:n], in1=pb_im(), op=ALU.mult), reads=[a_re[1], p_im[1], o_re[1]], writes=[t1[1]])
        S.add("dve", lambda e: e.tensor_tensor(out=t2[0][:], in0=a_im[0][:, :, 0:n], in1=pb_re(), op=ALU.mult), reads=[a_im[1], p_re[1], o_re[1]], writes=[t2[1]])
        S.add("dve", lambda e: e.tensor_tensor(out=o_im[0][:, :, L:L + n], in0=t1[0][:], in1=t2[0][:], op=ALU.add), reads=[t1[1], t2[1]], writes=[o_im[1]])

    def csq(p_re, p_im):
        a = tt(tmp(), p_re, p_re, ALU.mult); b = tt(tmp(), p_im, p_im, ALU.mult)
        n_re = tt(tmp(), a, b, ALU.subtract)
        n_im = stt(tmp(), p_re, 2.0, p_im, ALU.mult, ALU.mult)
        return n_re, n_im

    def power_table(name, p_re, p_im, n):
        T_re = V(name + "re", [128, 16, n]); T_im = V(name + "im", [128, 16, n])
        S.add("dve", lambda e: e.memset(T_re[0][:, :, 0:1], 1.0), writes=[T_re[1]])
        S.add("dve", lambda e: e.memset(T_im[0][:, :, 0:1], 0.0), writes=[T_im[1]])
        L = 1
        while L < n:
            m = min(L, n - L)
            cmul_bc(T_re, T_im, T_re, T_im, p_re, p_im, L, m)
            p_re, p_im = csq(p_re, p_im)
            L *= 2
        return T_re, T_im, p_re, p_im
    E64re, E64im, q_re, q_im = power_table("E64", cth, sth, 64)
    Fre, Fim, _, _ = power_table("F68", q_re, q_im, 68)

    Ere, Eim = V("Ere", [128, 68, 64]), V("Eim", [128, 68, 64])
    Et1, Et2 = V("Et1", [128, 34, 64]), V("Et2", [128, 34, 64])
    Rt = V("Rt", [128, SEG])
    Wsets = [{n: V(n + str(i), [128, SEG]) for n in ("xre", "xim", "A1", "A2", "T1")} for i in range(2)]
    Ssets = [(V(f"sre{i}", [128, SEG], BF16), V(f"sim{i}", [128, SEG], BF16)) for i in range(2)]
    UPF = 2
    usb = [V(f"usb{i}", [32, SEG], BF16) for i in range(UPF + 1)]
    iters = [(d, pr, b, sgi) for d in range(2) for pr in range(8) for b in range(2) for sgi in range(SEQ_T // SEG)]

    def issue_u(n_it):
        d_, pr_, b_, sg_ = iters[n_it]
        u__, bu__ = usb[n_it % (UPF + 1)]
        k.load(u__[:], uTd[d_][pr_ * 32:(pr_ + 1) * 32, b_ * SEQ_T + sg_ * SEG:b_ * SEQ_T + (sg_ + 1) * SEG], bu__, eng="pool")

    yt = [V(f"yt{i}", [32, SEG]) for i in range(2)]
    carries = [V(f"carry{i}", [128, 2]) for i in range(2)]
    it = 0
    for d in range(2):
        for pr in range(8):
            dp = d * 8 + pr
            for hf in range(2):
                Fb = lambda T, hf=hf, dp=dp: T[0][:, dp, hf * 34:(hf + 1) * 34].unsqueeze(2).to_broadcast([128, 34, 64])
                Eb = lambda T, dp=dp: T[0][:, dp, :].unsqueeze(1).to_broadcast([128, 34, 64])
                Eo_re = Ere[0][:, hf * 34:(hf + 1) * 34, :]
                Eo_im = Eim[0][:, hf * 34:(hf + 1) * 34, :]
                S.add("pool", lambda e, Fb=Fb, Eb=Eb: e.tensor_tensor(out=Et1[0][:], in0=Fb(Fre), in1=Eb(E64re), op=ALU.mult), reads=[Fre[1], E64re[1]], writes=[Et1[1]])
                S.add("pool", lambda e, Fb=Fb, Eb=Eb: e.tensor_tensor(out=Et2[0][:], in0=Fb(Fim), in1=Eb(E64im), op=ALU.mult), reads=[Fim[1], E64im[1]], writes=[Et2[1]])
                S.add("pool", lambda e, Eo_re=Eo_re: e.tensor_tensor(out=Eo_re, in0=Et1[0][:], in1=Et2[0][:], op=ALU.subtract), reads=[Et1[1], Et2[1]], writes=[Ere[1]])
                S.add("pool", lambda e, Fb=Fb, Eb=Eb: e.tensor_tensor(out=Et1[0][:], in0=Fb(Fre), in1=Eb(E64im), op=ALU.mult), reads=[Fre[1], E64im[1], Ere[1]], writes=[Et1[1]])
                S.add("pool", lambda e, Fb=Fb, Eb=Eb: e.tensor_tensor(out=Et2[0][:], in0=Fb(Fim), in1=Eb(E64re), op=ALU.mult), reads=[Fim[1], E64re[1], Ere[1]], writes=[Et2[1]])
                S.add("pool", lambda e, Eo_im=Eo_im: e.tensor_tensor(out=Eo_im, in0=Et1[0][:], in1=Et2[0][:], op=ALU.add), reads=[Et1[1], Et2[1]], writes=[Eim[1]])
            S.add("pool", lambda e, dp=dp: e.tensor_copy(out=Rt[0][:], in_=mag[0][:, dp:dp + 1].to_broadcast([128, SEG])), reads=[mag[1]], writes=[Rt[1]])
            Ef_re = Ere[0][:].rearrange("p k t -> p (k t)")
            Ef_im = Eim[0][:].rearrange("p k t -> p (k t)")
            chunks = [(c0, min(512, SEG - c0)) for c0 in range(0, SEG, 512)]

            def tte(eng, out, a_, bap, bbuf, op):
                S.add(eng, lambda e: e.tensor_tensor(out=out[0][:], in0=a_[0][:], in1=bap, op=op), reads=[a_[1], bbuf], writes=[out[1]])

            def stage_A(cx):
                W = cx["W"]; u_, bu_ = cx["u"]
                if cx["it"] == 0:
                    for pf_ in range(UPF):
                        issue_u(pf_)
                if cx["it"] + UPF < len(iters):
                    issue_u(cx["it"] + UPF)
                for ci, (dst, bdst) in enumerate((W["xre"], W["xim"])):
                    for (c0, n) in chunks:
                        pt, bpt = k.bank()
                        k.mm(pt[:, :n], bpt, [(BbT[ci][0][:, dp, :], u_[:, c0:c0 + n])], reads=[BbT[ci][1], bu_])
                        S.add("act", lambda e, pt=pt, dst=dst, c0=c0, n=n: e.copy(out=dst[:, c0:c0 + n], in_=pt[:, :n]), reads=[bpt], writes=[bdst])
                Er, Ei = cx["Er"], cx["Ei"]
                xre, xim, A1, A2, T1 = W["xre"], W["xim"], W["A1"], W["A2"], W["T1"]
                tte("dve", A1, xre, Er, Ere[1], ALU.mult)
                tte("dve", T1, xim, Ei, Eim[1], ALU.mult)
                tt(A1, A1, T1, ALU.add)
                tte("dve", A2, xim, Er, Ere[1], ALU.mult)
                tte("dve", T1, xre, Ei, Eim[1], ALU.mult)
                tt(A2, A2, T1, ALU.subtract)
                pcarry = cx["pcarry"]; carry = cx["carry"]
                for ci, (src, dst) in enumerate(((A1, xre), (A2, xim))):
                    if cx["sgi"] == 0:
                        S.add("dve", lambda e, src=src, dst=dst: e.tensor_tensor_scan(out=dst[0][:], data0=Rt[0][:], data1=src[0][:], initial=0.0,
                                                                                     op0=ALU.mult, op1=ALU.add), reads=[Rt[1], src[1]], writes=[dst[1]])
                    else:
                        S.add("dve", lambda e, src=src, dst=dst, ci=ci, pcarry=pcarry: e.tensor_tensor_scan(out=dst[0][:], data0=Rt[0][:], data1=src[0][:],
                                                                                            initial=pcarry[0][:, ci:ci + 1], op0=ALU.mult, op1=ALU.add),
                              reads=[Rt[1], src[1], pcarry[1]], writes=[dst[1]])
                if cx["sgi"] < SEQ_T // SEG - 1:
                    S.add("act", lambda e, carry=carry, xre=xre: e.copy(out=carry[0][:, 0:1], in_=xre[0][:, SEG - 1:SEG]), reads=[xre[1]], writes=[carry[1]])
                    S.add("act", lambda e, carry=carry, xim=xim: e.copy(out=carry[0][:, 1:2], in_=xim[0][:, SEG - 1:SEG]), reads=[xim[1]], writes=[carry[1]])

            def stage_B(cx):
                W = cx["W"]; sre, sim = cx["S"]; y_, by_ = cx["y"]
                Er, Ei = cx["Er"], cx["Ei"]
                xre, xim, A1, T1 = W["xre"], W["xim"], W["A1"], W["T1"]
                tte("dve", T1, xre, Er, Ere[1], ALU.mult)
                tte("dve", A1, xim, Ei, Eim[1], ALU.mult)
                tt(sre, T1, A1, ALU.subtract)
                tte("dve", T1, xre, Ei, Eim[1], ALU.mult)
                tte("dve", A1, xim, Er, Ere[1], ALU.mult)
                tt(sim, T1, A1, ALU.add)
                for (c0, n) in chunks:
                    pt, bpt = k.bank()
                    k.mm(pt[0:32, :n], bpt, [(CTre[0][:, dp, :], sre[0][:, c0:c0 + n]), (CTim[0][:, dp, :], sim[0][:, c0:c0 + n])],
                         reads=[CTre[1], CTim[1], sre[1], sim[1]])
                    S.add("act", lambda e, pt=pt, y_=y_, c0=c0, n=n: e.copy(out=y_[:, c0:c0 + n], in_=pt[0:32, :n]), reads=[bpt], writes=[by_])
                k.store(yTd[d][pr * 32:(pr + 1) * 32, cx["b"] * SEQ_T + cx["t0"]:cx["b"] * SEQ_T + cx["t0"] + SEG], y_[:], by_)

            cxs = []
            for b in range(2):
                for sgi in range(SEQ_T // SEG):
                    t0 = sgi * SEG
                    cxs.append(dict(it=it, b=b, sgi=sgi, t0=t0, W=Wsets[it % 2], S=Ssets[it % 2], carry=carries[it % 2], pcarry=carries[(it - 1) % 2],
                                    u=usb[it % (UPF + 1)], y=yt[it % 2], Er=Ef_re[:, t0:t0 + SEG], Ei=Ef_im[:, t0:t0 + SEG]))
                    it += 1
            stage_A(cxs[0])
            for i_ in range(len(cxs)):
                if i_ + 1 < len(cxs):
                    stage_A(cxs[i_ + 1])
                stage_B(cxs[i_])
    return k.finish()


def s5_param_layout(inp, c):
    G0 = 16 * c
    out = {}
    for n, key in (("are", "s5_a_re"), ("aim", "s5_a_im")):
        a = inp[key][0][:, G0:G0 + 16, :]
        out[n] = np.ascontiguousarray(a.reshape(2, 8, 2, 64).transpose(2, 3, 0, 1).reshape(128, 2, 8))
    ls = inp["s5_log_step"][0][:, G0:G0 + 16]
    ls = np.broadcast_to(ls.reshape(2, 8, 2, 1), (2, 8, 2, 64))
    out["lst"] = np.ascontiguousarray(ls.transpose(2, 3, 0, 1).reshape(128, 2, 8))
    for n, key in (("bre", "s5_b_re"), ("bim", "s5_b_im")):
        a = inp[key][0][:, G0:G0 + 16]
        out[n] = np.ascontiguousarray(a.reshape(2, 8, 2, 64, 16).transpose(2, 3, 0, 1, 4).reshape(128, 2, 8, 16))
    for n, key in (("cre", "s5_c_re"), ("cim", "s5_c_im")):
        a = inp[key][0][:, G0:G0 + 16]
        out[n] = np.ascontiguousarray(a.reshape(2, 8, 2, 16, 64).transpose(2, 4, 0, 1, 3).reshape(128, 2, 8, 16))
    return out


def launch_L2(inp, u_lat, u_ctx):
    nc = build_L2()
    seq_f = np.concatenate([u_ctx, u_lat], 1)
    seq_r = np.concatenate([u_ctx[:, ::-1], u_lat[:, ::-1]], 1)
    ident = np.eye(128, dtype=np.float32)
    maps = []
    for c in range(NCORES):
        m = s5_param_layout(inp, c)
        m["uT_f"] = np.ascontiguousarray(seq_f[:, :, c * 256:(c + 1) * 256].transpose(2, 0, 1).reshape(256, 2 * SEQ_T))
        m["uT_r"] = np.ascontiguousarray(seq_r[:, :, c * 256:(c + 1) * 256].transpose(2, 0, 1).reshape(256, 2 * SEQ_T))
        m["ident"] = ident
        maps.append(m)
    res = run(nc, maps)
    y_f = np.zeros((2, 4096, D), np.float32)
    y_r = np.zeros((2, 4096, D), np.float32)
    for c in range(NCORES):
        yf = res[c]["yT_f"].reshape(256, 2, SEQ_T)[:, :, 256:]
        yr = res[c]["yT_r"].reshape(256, 2, SEQ_T)[:, :, 256:][:, :, ::-1]
        y_f[:, :, c * 256:(c + 1) * 256] = yf.transpose(1, 2, 0)
        y_r[:, :, c * 256:(c + 1) * 256] = yr.transpose(1, 2, 0)
    return y_f, y_r


def V_(k, name, shape, dt=F32):
    return (k.sb(name, shape, dt), Buf(name))


def gelu_tanh(k, out, bout, xin, bx, tmpa, tmpb, rows=128, accum=None, eng2="pool"):
    S = k.S
    ta, bta = tmpa
    tb, btb = tmpb
    S.add("act", lambda e: e.activation(out=ta, in_=xin, func=AF.Square), reads=[bx], writes=[bta])
    S.add("dve", lambda e: e.tensor_scalar(out=ta, in0=ta, scalar1=0.044715, scalar2=1.0, op0=ALU.mult, op1=ALU.add), reads=[bta], writes=[bta])
    S.add("dve", lambda e: e.tensor_tensor(out=ta, in0=ta, in1=xin, op=ALU.mult), reads=[bta, bx], writes=[bta])
    S.add("act", lambda e: e.activation(out=tb, in_=ta, func=AF.Sigmoid, scale=1.5957691216057308), reads=[bta], writes=[btb])
    if accum is None:
        S.add("dve", lambda e: e.tensor_tensor(out=out, in0=xin, in1=tb, op=ALU.mult), reads=[bx, btb], writes=[bout])
    else:
        acc_ap, bacc = accum
        S.add("dve", lambda e: e.scalar_tensor_tensor(out=out, in0=xin, scalar=1.0, in1=tb, op0=ALU.mult, op1=ALU.mult, accum_out=acc_ap),
              reads=[bx, btb], writes=[bout, bacc])


def transpose16(k, src, bsrc, dst, bdst, ident, bident, rows=128):
    S = k.S
    for half in range(2):
        pt, bpt = k.bank()
        ptb = pt[:].bitcast(BF16)

        def fn(e, half=half, ptb=ptb):
            ins = None
            for j in range(8):
                kc = half * 8 + j
                ins = e.transpose(ptb[:, j * 128:j * 128 + rows], src[:rows, kc * 128:(kc + 1) * 128], ident[:rows, :rows])
            return ins
        S.add("pe", fn, reads=[bsrc, bident], writes=[bpt])
        S.add("act", lambda e, half=half, ptb=ptb: e.copy(out=dst[:, half * 8:(half + 1) * 8, :rows],
                                                         in_=ptb.rearrange("p (j t) -> p j t", t=128)[:, :, :rows]), reads=[bpt], writes=[bdst])


def load_w16(k, w_sb, bw, w_d, nk=16, eng="pool"):
    k.load(w_sb[:, 0:nk, :], w_d.rearrange("(kc p) n -> p kc n", p=128), bw, eng=eng)


def build_L3a(k=None):
    own = k is None
    k = k or KB()
    S = k.S
    yf_d = k.din("yf", [1024, D]); yr_d = k.din("yr", [1024, D]); u_d = k.din("u", [1024, D])
    dB_d = k.din("dB", [128, D]); bg_d = k.din("bgluB", [128, D])
    wg_d = k.din("w_glu", [D, D]); wo_d = k.din("w_out", [D, D]); ident_d = k.din("ident", [128, 128])
    ml_d = k.dout("ml", [1024, D])
    ident, bident = V_(k, "ident_sb", [128, 128], BF16); k.load(ident[:], ident_d, bident, eng="pool")
    dB, bdB = V_(k, "dB_sb", [128, D]); k.load(dB[:], dB_d, bdB)
    bg, bbg = V_(k, "bg_sb", [128, D]); k.load(bg[:], bg_d, bbg)
    wg, bwg = V_(k, "wg", [128, 16, D], BF16); load_w16(k, wg, bwg, wg_d)
    wo, bwo = V_(k, "wo", [128, 16, D], BF16); load_w16(k, wo, bwo, wo_d)
    Y = [V_(k, f"Y{i}", [128, D]) for i in range(2)]
    T, bT = V_(k, "T", [128, D])
    T2, bT2 = V_(k, "T2", [128, D])
    g, bg_ = V_(k, "g", [128, D], BF16)
    gT, bgT = V_(k, "gT", [128, 16, 128], BF16)
    zs, bzs = V_(k, "zs", [128, D])
    v, bv = V_(k, "v", [128, D], BF16)
    vT, bvT = V_(k, "vT", [128, 16, 128], BF16)
    mlo, bmlo = zs, bzs
    for i in range(8):
        y, by = Y[i % 2]
        sl = slice(i * 128, (i + 1) * 128)
        k.load(y[:], yf_d[sl, :], by, eng="sp")
        k.load(T[:], yr_d[sl, :], bT, eng="sp")
        S.add("dve", lambda e, y=y: e.tensor_tensor(out=y[:], in0=y[:], in1=T[:], op=ALU.add), reads=[by, bT], writes=[by])
        k.load(T[:], u_d[sl, :], bT, eng="sp")
        S.add("pool", lambda e: e.tensor_tensor(out=T[:], in0=T[:], in1=dB[:], op=ALU.mult), reads=[bT, bdB], writes=[bT])
        S.add("dve", lambda e, y=y: e.tensor_tensor(out=y[:], in0=y[:], in1=T[:], op=ALU.add), reads=[by, bT], writes=[by])
        gelu_tanh(k, g[:], bg_, y[:], by, (T[:], bT), (T2[:], bT2))
        transpose16(k, g, bg_, gT, bgT, ident, bident)
        for nb in range(4):
            pt, bpt = k.bank()
            cs = slice(nb * 512, (nb + 1) * 512)
            k.mm(pt[:], bpt, [(gT[:, kc, :], wg[:, kc, cs]) for kc in range(16)], reads=[bgT, bwg])
            S.add("dve", lambda e, pt=pt, cs=cs: e.tensor_tensor(out=zs[:, cs], in0=pt[:], in1=bg[:, cs], op=ALU.add), reads=[bpt, bbg], writes=[bzs])
        S.add("act", lambda e: e.activation(out=zs[:], in_=zs[:], func=AF.Sigmoid), reads=[bzs], writes=[bzs])
        S.add("dve", lambda e: e.tensor_tensor(out=v[:], in0=g[:], in1=zs[:], op=ALU.mult), reads=[bg_, bzs], writes=[bv])
        transpose16(k, v, bv, vT, bvT, ident, bident)
        for nb in range(4):
            pt, bpt = k.bank()
            cs = slice(nb * 512, (nb + 1) * 512)
            k.mm(pt[:], bpt, [(vT[:, kc, :], wo[:, kc, cs]) for kc in range(16)], reads=[bvT, bwo])
            S.add("act", lambda e, pt=pt, cs=cs: e.copy(out=mlo[:, cs], in_=pt[:]), reads=[bpt], writes=[bmlo])
        k.store(ml_d[sl, :], mlo[:], bmlo)
    return k.done(own)


def launch_L3a(inp, y_f, y_r, u_lat):
    nc = build_L3a()
    ident = np.eye(128, dtype=np.float32)
    maps = []
    for c in range(NCORES):
        b, q = c // 4, c % 4
        sl = slice(q * 1024, (q + 1) * 1024)
        maps.append(dict(yf=np.ascontiguousarray(y_f[b, sl]), yr=np.ascontiguousarray(y_r[b, sl]), u=np.ascontiguousarray(u_lat[b, sl]),
                         dB=bc(inp["s5_d"][0]), bgluB=bc(inp["s5_b_glu"][0]), w_glu=inp["s5_w_glu"][0], w_out=inp["s5_w_out"][0], ident=ident))
    res = run(nc, maps)
    ml = np.zeros((2, 4096, D), np.float32)
    for c in range(NCORES):
        ml[c // 4, (c % 4) * 1024:(c % 4 + 1) * 1024] = res[c]["ml"]
    return ml


def build_resnorm(k=None):
    own = k is None
    k = k or KB()
    S = k.S
    x_d = k.din("x", [1024, D]); ml_d = k.din("ml", [1024, D]); ccT_d = k.din("ccT", [128, 16, 1])
    wmod_d = k.din("wmod", [D, 3 * D]); bmodB_d = k.din("bmodB", [128, 3 * D]); gF_d = k.din("gF", [128, D])
    x1_d = k.dout("x1", [1024, D]); hl_d = k.dout("hl", [1024, D])
    ccT, bcc = V_(k, "ccT_sb", [128, 16, 1]); k.load(ccT[:], ccT_d, bcc)
    gF, bgF = V_(k, "gF_sb", [128, D]); k.load(gF[:], gF_d, bgF)
    M = [V_(k, f"M{j}", [128, D]) for j in range(3)]
    outs = [lambda c0, n: (M[c0 // D][0][:, c0 % D:c0 % D + n], M[c0 // D][1])]
    build_mod(k, ccT, bcc, wmod_d, bmodB_d, 3 * D, 1, outs)
    A, bA = M[2]
    B, bB = M[1]
    S.add("dve", lambda e: e.scalar_tensor_tensor(out=A[:], in0=A[:], scalar=1.0, in1=gF[:], op0=ALU.add, op1=ALU.mult), reads=[bA, bgF], writes=[bA])
    X = [V_(k, f"X{i}", [128, D]) for i in range(2)]
    Ml = [V_(k, f"Ml{i}", [128, D]) for i in range(2)]
    H = [V_(k, f"H{i}", [128, D]) for i in range(2)]
    junk, bjunk = V_(k, "junk", [128, D], BF16)
    SS = [V_(k, f"ss{i}", [128, 1]) for i in range(2)]
    for i in range(8):
        sl = slice(i * 128, (i + 1) * 128)
        x, bx = X[i % 2]; m, bm = Ml[i % 2]; h, bh = H[i % 2]; ss, bss = SS[i % 2]
        k.load(x[:], x_d[sl, :], bx, eng="sp")
        k.load(m[:], ml_d[sl, :], bm, eng="pool")
        S.add("pool", lambda e, m=m: e.tensor_tensor(out=m[:], in0=m[:], in1=M[0][0][:], op=ALU.mult), reads=[bm, M[0][1]], writes=[bm])
        S.add("dve", lambda e, x=x, m=m: e.tensor_tensor(out=x[:], in0=x[:], in1=m[:], op=ALU.add), reads=[bx, bm], writes=[bx])
        k.store(x1_d[sl, :], x[:], bx)
        S.add("act", lambda e, x=x, ss=ss: e.activation(out=junk[:], in_=x[:], func=AF.Square, accum_out=ss[:, 0:1]), reads=[bx], writes=[bjunk, bss])
        S.add("dve", lambda e, ss=ss: e.tensor_scalar(out=ss[:], in0=ss[:], scalar1=1.0 / D, scalar2=1e-6, op0=ALU.mult, op1=ALU.add), reads=[bss], writes=[bss])
        S.add("act", lambda e, ss=ss: e.sqrt(out=ss[:], in_=ss[:]), reads=[bss], writes=[bss])
        S.add("dve", lambda e, ss=ss: e.reciprocal(out=ss[:], in_=ss[:]), reads=[bss], writes=[bss])
        S.add("dve", lambda e, x=x, h=h, ss=ss: e.scalar_tensor_tensor(out=h[:], in0=x[:], scalar=ss[:, 0:1], in1=A[:], op0=ALU.mult, op1=ALU.mult),
              reads=[bx, bss, bA], writes=[bh])
        S.add("pool", lambda e, h=h: e.tensor_tensor(out=h[:], in0=h[:], in1=B[:], op=ALU.add), reads=[bh, bB], writes=[bh])
        k.store(hl_d[sl, :], h[:], bh)
    return k.done(own)


def launch_resnorm(inp, x, ml, layer, gF, mod_cols):
    nc = build_resnorm()
    cols = np.concatenate([np.arange(j * D, (j + 1) * D) for j in mod_cols])
    wmod = np.ascontiguousarray(inp["w_mod"][layer][:, cols])
    bmodB = bc(inp["b_mod"][layer][cols])
    maps = []
    for c in range(NCORES):
        b, q = c // 4, c % 4
        sl = slice(q * 1024, (q + 1) * 1024)
        ccT = np.ascontiguousarray(inp["c"][b].reshape(1, 16, 128).transpose(2, 1, 0))
        maps.append(dict(x=np.ascontiguousarray(x[b, sl]), ml=np.ascontiguousarray(ml[b, sl]), ccT=ccT, wmod=wmod, bmodB=bmodB, gF=bc(gF)))
    res = run(nc, maps)
    x1 = np.zeros((2, 4096, D), np.float32); hl = np.zeros((2, 4096, D), np.float32)
    for c in range(NCORES):
        x1[c // 4, (c % 4) * 1024:(c % 4 + 1) * 1024] = res[c]["x1"]
        hl[c // 4, (c % 4) * 1024:(c % 4 + 1) * 1024] = res[c]["hl"]
    return x1, hl


FF = 5632


def build_ffn(final, k=None):
    own = k is None
    k = k or KB()
    S = k.S
    hl_d = k.din("hlT", [D, 1152]); x_d = k.din("x", [1024, D]); ccT_d = k.din("ccT", [128, 16, 1])
    wmod_d = k.din("wmod", [D, D]); bmodB_d = k.din("bmodB", [128, D])
    wup_d = k.din("w_up", [D, 2 * FF]); cw_d = k.din("cw", [128, 88, 9]); cb_d = k.din("cb", [128, 88])
    wdn_d = k.din("w_down", [FF, D])
    xo_d = k.dout("xo", [1024, D])
    if final:
        gfin_d = k.din("gfin", [128, D])
        xs_d = k.scratch(k.prefix + "xs_scratch", [1024, D])
        bxs = Buf("xs")
    ccT, bcc = V_(k, "ccT_sb", [128, 16, 1]); k.load(ccT[:], ccT_d, bcc)
    gate, bgate = V_(k, "gate", [128, D])
    build_mod(k, ccT, bcc, wmod_d, bmodB_d, D, 1, [lambda c0, n: (gate[:, c0:c0 + n], bgate)])
    cw, bcw = V_(k, "cw_sb", [128, 88, 9]); k.load(cw[:], cw_d, bcw)
    cb, bcb = V_(k, "cb_sb", [128, 88]); k.load(cb[:], cb_d, bcb)
    hlT, bhlT = V_(k, "hlT_sb", [128, 16, 1152], BF16)
    k.load(hlT[:], hl_d.rearrange("(kc p) n -> p kc n", p=128), bhlT, eng="pool")
    aT, baT = V_(k, "aT", [128, 44, 512], BF16)
    NWD = 128
    wds = [V_(k, f"wd{i}", [128, 44, NWD], BF16) for i in range(2)]
    PF = 2
    wu = [(k.sb(f"wu{i}", [128, 16, 256], BF16), Buf(), Buf()) for i in range(PF + 1)]

    def issue_wu(n_it):
        j_ = n_it % 44
        w_, bg__, bv__ = wu[n_it % (PF + 1)]
        k.load(w_[:, :, 0:128], wup_d[:, j_ * 128:(j_ + 1) * 128].rearrange("(kc p) n -> p kc n", p=128), bg__, eng="pool")
        k.load(w_[:, :, 128:256], wup_d[:, FF + j_ * 128:FF + (j_ + 1) * 128].rearrange("(kc p) n -> p kc n", p=128), bv__, eng="pool")
    zp = [[V_(k, f"zp{i}{c}", [128, 10, 66]) for c in range(2)] for i in range(2)]
    for i in range(2):
        for c in range(2):
            S.add("pool", lambda e, t=zp[i][c][0]: e.memset(t[:], 0.0), writes=[zp[i][c][1]])
    acc = [[V_(k, f"acc{i}{c}", [128, 8, 64]) for c in range(2)] for i in range(2)]
    sgt = [V_(k, f"sg{i}", [128, 8, 64]) for i in range(2)]
    ptmps = [V_(k, f"ptmp{i}", [128, 8, 64]) for i in range(4)]
    tcnt = [0]
    xc = [V_(k, f"xc{i}", [128, NWD]) for i in range(2)]
    Tt = [V_(k, f"Tt{i}", [128, NWD]) for i in range(2)]
    it = 0
    for h in range(2):
        e0 = h * 8 * 64
        for j in range(44):
            if it == 0:
                for pf in range(PF):
                    issue_wu(pf)
            if it + PF < 88:
                issue_wu(it + PF)
            w, bwg_, bwv_ = wu[it % (PF + 1)]
            for c in range(2):
                z, bz = zp[it % 2][c]
                a_, ba_ = acc[it % 2][c]
                ch = j + 44 * c
                for (c0, n, r0) in ((0, 512, 0), (512, 128, 8)):
                    pt, bpt = k.bank()
                    k.mm(pt[:, :n], bpt, [(w[:, kc, c * 128:(c + 1) * 128], hlT[:, kc, e0 + c0:e0 + c0 + n]) for kc in range(16)],
                         reads=[(bwg_, bwv_)[c], bhlT])
                    S.add("act", lambda e, pt=pt, z=z, n=n, r0=r0: e.copy(out=z[:, r0:r0 + n // 64, 1:65],
                                                                          in_=pt[:, :n].rearrange("p (r c) -> p r c", c=64)), reads=[bpt], writes=[bz])
                if c == 0:
                    S.add("dve", lambda e, z=z, a_=a_, ch=ch: e.tensor_scalar(out=a_[:], in0=z[:, 0:8, 0:64], scalar1=cw[:, ch, 0:1], scalar2=cb[:, ch:ch + 1],
                                                                            op0=ALU.mult, op1=ALU.add), reads=[bz, bcw, bcb], writes=[ba_])
                    for t in range(1, 9):
                        di, dj = divmod(t, 3)
                        S.add("dve", lambda e, z=z, a_=a_, ch=ch, t=t, di=di, dj=dj: e.scalar_tensor_tensor(
                            out=a_[:], in0=z[:, di:di + 8, dj:dj + 64], scalar=cw[:, ch, t:t + 1], in1=a_[:], op0=ALU.mult, op1=ALU.add),
                            reads=[bz, bcw, ba_], writes=[ba_])
                else:
                    S.add("act", lambda e, z=z, a_=a_, ch=ch: e.activation(out=a_[:], in_=z[:, 0:8, 0:64], func=AF.Identity, bias=cb[:, ch:ch + 1],
                                                                         scale=cw[:, ch, 0:1]), reads=[bz, bcw, bcb], writes=[ba_])
                    for t in range(1, 9):
                        di, dj = divmod(t, 3)
                        pt_, bpt_ = ptmps[tcnt[0] % 4]
                        tcnt[0] += 1
                        S.add("act", lambda e, z=z, ch=ch, t=t, di=di, dj=dj, pt_=pt_: e.activation(out=pt_[:], in_=z[:, di:di + 8, dj:dj + 64], func=AF.Copy,
                                                                                                 scale=cw[:, ch, t:t + 1]), reads=[bz, bcw], writes=[bpt_])
                        S.add("dve", lambda e, a_=a_, pt_=pt_: e.tensor_tensor(out=a_[:], in0=a_[:], in1=pt_[:], op=ALU.add), reads=[ba_, bpt_], writes=[ba_])
            ag, bag = acc[it % 2][0]
            av, bav = acc[it % 2][1]
            sg, bsg = sgt[it % 2]
            S.add("act", lambda e, ag=ag, sg=sg: e.activation(out=sg[:], in_=ag[:], func=AF.Silu), reads=[bag], writes=[bsg])
            S.add("dve", lambda e, sg=sg, av=av, j=j: e.tensor_tensor(out=aT[:, j, :].rearrange("p (r c) -> p r c", c=64), in0=sg[:], in1=av[:], op=ALU.mult),
                  reads=[bsg, bav], writes=[baT])
            it += 1
        for nb in range(D // NWD):
            cs = slice(nb * NWD, (nb + 1) * NWD)
            wd, bwd = wds[nb % 2]
            k.load(wd[:], wdn_d[:, cs].rearrange("(j p) n -> p j n", p=128), bwd, eng="pool")
            for tt_ in range(4):
                gi = h * 4 + tt_
                rs = slice(gi * 128, (gi + 1) * 128)
                x, bx = xc[(nb * 4 + tt_) % 2]
                T, bT = Tt[(nb * 4 + tt_) % 2]
                k.load(x[:], x_d[rs, cs], bx, eng="sp")
                pt, bpt = k.bank()
                k.mm(pt[:, :NWD], bpt, [(aT[:, j, tt_ * 128:(tt_ + 1) * 128], wd[:, j, :]) for j in range(44)], reads=[baT, bwd])
                S.add("dve", lambda e, pt=pt, T=T, cs=cs: e.tensor_tensor(out=T[:], in0=pt[:, :NWD], in1=gate[:, cs], op=ALU.mult), reads=[bpt, bgate], writes=[bT])
                S.add("pool", lambda e, T=T, x=x: e.tensor_tensor(out=T[:], in0=T[:], in1=x[:], op=ALU.add), reads=[bT, bx], writes=[bT])
                if final:
                    S.add("sp", lambda e, T=T, rs=rs, cs=cs: e.dma_start(out=xs_d[rs, cs], in_=T[:]), reads=[bT], writes=[bxs], dma=True)
                else:
                    k.store(xo_d[rs, cs], T[:], bT)
    if final:
        gfin, bgfin = V_(k, "gfin_sb", [128, D]); k.load(gfin[:], gfin_d, bgfin)
        XR = [(aT[:].rearrange("p j n -> p (j n)").bitcast(F32)[:, 0:D], baT)] * 2
        bjunk = bhlT
        SS = [V_(k, f"ss{i}", [128, 1]) for i in range(2)]
        for i in range(8):
            rs = slice(i * 128, (i + 1) * 128)
            x, bx = XR[i % 2]; ss, bss = SS[i % 2]
            k.load(x[:], xs_d[rs, :], bx, eng="sp", reads=[bxs])
            S.add("act", lambda e, x=x, ss=ss: e.activation(out=hlT[:, 0:2, 0:1024], in_=x[:].rearrange("p (a b) -> p a b", a=2), func=AF.Square,
                                                           accum_out=ss[:, 0:1]), reads=[bx], writes=[bjunk, bss])
            S.add("dve", lambda e, ss=ss: e.tensor_scalar(out=ss[:], in0=ss[:], scalar1=1.0 / D, scalar2=1e-6, op0=ALU.mult, op1=ALU.add), reads=[bss], writes=[bss])
            S.add("act", lambda e, ss=ss: e.sqrt(out=ss[:], in_=ss[:]), reads=[bss], writes=[bss])
            S.add("dve", lambda e, ss=ss: e.reciprocal(out=ss[:], in_=ss[:]), reads=[bss], writes=[bss])
            S.add("dve", lambda e, x=x, ss=ss: e.scalar_tensor_tensor(out=x[:], in0=x[:], scalar=ss[:, 0:1], in1=gfin[:], op0=ALU.mult, op1=ALU.mult),
                  reads=[bx, bss, bgfin], writes=[bx])
            k.store(xo_d[rs, :], x[:], bx)
    return k.done(own)


def launch_ffn(inp, x, hl, layer, final):
    nc = build_ffn(final)
    cols = np.arange(5 * D, 6 * D)
    wmod = np.ascontiguousarray(inp["w_mod"][layer][:, cols]); bmodB = bc(inp["b_mod"][layer][cols])
    cw = np.ascontiguousarray(inp["ffn_conv_w"][layer].reshape(9, 88, 128).transpose(2, 1, 0))
    cb = np.ascontiguousarray(inp["ffn_conv_b"][layer].reshape(88, 128).T)
    maps = []
    for c in range(NCORES):
        b, q = c // 4, c % 4
        ext = np.zeros((1152, D), np.float32)
        lo, hi = q * 1024 - 64, q * 1024 + 1024 + 64
        slo, shi = max(lo, 0), min(hi, 4096)
        ext[slo - lo:shi - lo] = hl[b, slo:shi]
        m = dict(hlT=np.ascontiguousarray(ext.T), x=np.ascontiguousarray(x[b, q * 1024:(q + 1) * 1024]),
                 ccT=np.ascontiguousarray(inp["c"][b].reshape(1, 16, 128).transpose(2, 1, 0)), wmod=wmod, bmodB=bmodB,
                 w_up=inp["ffn_w_up"][layer], cw=cw, cb=cb, w_down=inp["ffn_w_down"][layer])
        if final:
            m["gfin"] = bc(inp["final_norm_g"])
        maps.append(m)
    res = run(nc, maps)
    xo = np.zeros((2, 4096, D), np.float32)
    for c in range(NCORES):
        xo[c // 4, (c % 4) * 1024:(c % 4 + 1) * 1024] = res[c]["xo"]
    return xo


def rms_mod(k, x, bx, A, bA, B, bB, out, bout, junk, ss, t1):
    S = k.S
    S.add("act", lambda e: e.activation(out=junk[0][:], in_=x, func=AF.Square, accum_out=ss[0][:, 0:1]), reads=[bx], writes=[junk[1], ss[1]])
    S.add("dve", lambda e: e.tensor_scalar(out=ss[0][:], in0=ss[0][:], scalar1=1.0 / D, scalar2=1e-6, op0=ALU.mult, op1=ALU.add), reads=[ss[1]], writes=[ss[1]])
    S.add("act", lambda e: e.sqrt(out=ss[0][:], in_=ss[0][:]), reads=[ss[1]], writes=[ss[1]])
    S.add("dve", lambda e: e.reciprocal(out=ss[0][:], in_=ss[0][:]), reads=[ss[1]], writes=[ss[1]])
    S.add("dve", lambda e: e.scalar_tensor_tensor(out=t1[0][:], in0=x, scalar=ss[0][:, 0:1], in1=A, op0=ALU.mult, op1=ALU.mult),
          reads=[bx, ss[1], bA], writes=[t1[1]])
    S.add("dve", lambda e: e.tensor_tensor(out=out, in0=t1[0][:], in1=B, op=ALU.add), reads=[t1[1], bB], writes=[bout])


def build_sgu(k=None):
    own = k is None
    k = k or KB()
    S = k.S
    x_d = k.din("x", [1024, D]); ccT_d = k.din("ccT", [128, 16, 1])
    wmA_d = k.din("wmodA", [D, 2 * D]); bmA_d = k.din("bmodBA", [128, 2 * D]); gM_d = k.din("gMix", [128, D])
    wmB_d = k.din("wmodB", [D, 3 * D]); bmB_d = k.din("bmodBB", [128, 3 * D]); gF_d = k.din("gF", [128, D])
    win_d = k.din("w_in", [D, 8192]); lng_d = k.din("lngT", [128, 32]); lnb_d = k.din("lnbT", [128, 32])
    wsT_d = k.din("wsT", [128, 16, 128]); bsB_d = k.din("bsB", [128, 16, 128]); wo_d = k.din("w_out", [4096, D])
    ident_d = k.din("ident", [128, 128]); ones_d = k.din("ones", [128, 128])
    x3_d = k.dout("x3", [1024, D]); hl_d = k.dout("hl", [1024, D])
    bx3 = Buf("x3d")
    ident, bident = V_(k, "ident_sb", [128, 128], BF16); k.load(ident[:], ident_d, bident, eng="pool")
    ones, bones = V_(k, "ones_sb", [128, 128], BF16); k.load(ones[:], ones_d, bones, eng="pool")
    ccT, bcc = V_(k, "ccT_sb", [128, 16, 1]); k.load(ccT[:], ccT_d, bcc)
    M0 = V_(k, "M0", [128, D]); M1 = V_(k, "M1", [128, D]); G2 = V_(k, "G2", [128, D])
    t1 = V_(k, "t1", [128, D]); junk = V_(k, "junk", [128, D], BF16); ss = V_(k, "ss", [128, 1])
    k.load(t1[0][:], gM_d, t1[1])
    sc = k.sb("sc", [128, 16, 1]); bsc = Buf()
    scb = k.sb("scb", [128, 16, 1, 128]); bscb = Buf()
    S.add("act", lambda e: e.activation(out=sc[:], in_=ccT[:], func=AF.Silu), reads=[bcc], writes=[bsc])
    S.add("dve", lambda e: e.tensor_copy(out=scb[:], in_=sc[:].unsqueeze(3).to_broadcast([128, 16, 1, 128])), reads=[bsc], writes=[bscb])
    NW = 128
    wst = [(k.sb(f"wst{i}", [128, 16, NW]), Buf()) for i in range(2)]
    bmt = [(k.sb(f"bmt{i}", [128, NW]), Buf()) for i in range(2)]

    def mod(wmod_d, bmodB_d, ncols, outf):
        for nb in range(ncols // NW):
            w, bw = wst[nb % 2]
            bm, bbm = bmt[nb % 2]
            k.load(w[:], wmod_d[:, nb * NW:(nb + 1) * NW].rearrange("(kc p) n -> p kc n", p=128), bw)
            k.load(bm[:], bmodB_d[:, nb * NW:(nb + 1) * NW], bbm)
            pt, bpt = k.bank()
            k.mm(pt[:, :NW], bpt, [(scb[:, kc, 0, :], w[:, kc, :]) for kc in range(16)], reads=[bscb, bw])
            dst, bdst = outf(nb * NW, NW)
            S.add("dve", lambda e, pt=pt, dst=dst, bm=bm: e.tensor_tensor(out=dst, in0=pt[:, :NW], in1=bm[:], op=ALU.add), reads=[bpt, bbm], writes=[bdst])
    MA = [M0, M1]
    mod(wmA_d, bmA_d, 2 * D, lambda c0, n: (MA[c0 // D][0][:, c0 % D:c0 % D + n], MA[c0 // D][1]))
    S.add("dve", lambda e: e.scalar_tensor_tensor(out=M1[0][:], in0=M1[0][:], scalar=1.0, in1=t1[0][:], op0=ALU.add, op1=ALU.mult),
          reads=[M1[1], t1[1]], writes=[M1[1]])
    mod(wmB_d[:, 0:D], bmB_d[:, 0:D], D, lambda c0, n: (G2[0][:, c0:c0 + n], G2[1]))
    lng = V_(k, "lng", [128, 32]); k.load(lng[0][:], lng_d, lng[1])
    lnb = V_(k, "lnb", [128, 32]); k.load(lnb[0][:], lnb_d, lnb[1])
    wsT = V_(k, "wsT_sb", [128, 16, 128], BF16); k.load(wsT[0][:], wsT_d, wsT[1], eng="pool")
    bsB = V_(k, "bsB_sb", [128, 16, 128]); k.load(bsB[0][:], bsB_d, bsB[1])
    rsB = V_(k, "rsB", [128, 16, 128])
    for g4 in range(4):
        pt, bpt = k.bank()
        k.mm(pt[:], bpt, [(ones[:], wsT[0][:, g4 * 4:(g4 + 1) * 4, :].rearrange("p g q -> p (g q)"))], reads=[bones, wsT[1]])
        S.add("act", lambda e, pt=pt, g4=g4: e.copy(out=rsB[0][:, g4 * 4:(g4 + 1) * 4, :].rearrange("p g q -> p (g q)"), in_=pt[:]), reads=[bpt], writes=[rsB[1]])
    hlT = V_(k, "hlT", [128, 16, 512], BF16)
    hlb = V_(k, "hlb", [128, D], BF16)
    vn = k.sb("vn", [128, 4, 4096], BF16)
    bvn = [Buf(f"vn{c}") for c in range(32)]
    X = V_(k, "X", [128, D])
    wv = V_(k, "wv", [128, 16, 512], BF16)
    wu = [V_(k, f"wu{i}", [128, 16, 128], BF16) for i in range(2)]
    wo = V_(k, "wo", [128, 32, 256], BF16)
    ga = V_(k, "ga", [128, 512]); gb = V_(k, "gb", [128, 512])
    sums = V_(k, "sums", [128, 4, 8]); sqs = V_(k, "sqs", [128, 4, 8])
    mean = V_(k, "mean", [128, 4]); var = V_(k, "var", [128, 4]); msq = V_(k, "msq", [128, 4])
    uTb = V_(k, "uTb", [128, 512]); svt = V_(k, "svt", [128, 4, 128]); bfull = V_(k, "bfull", [128, 128])
    xc = [V_(k, f"xc{i}", [128, 256]) for i in range(2)]
    Tt = [V_(k, f"Tt{i}", [128, 256]) for i in range(2)]
    for hf in range(2):
        for i in range(4):
            rs = slice(hf * 512 + i * 128, hf * 512 + (i + 1) * 128)
            k.load(X[0][:], x_d[rs, :], X[1], eng="sp")
            rms_mod(k, X[0][:], X[1], M1[0][:], M1[1], M0[0][:], M0[1], hlb[0][:], hlb[1], junk, ss, t1)
            transpose16(k, hlb[0], hlb[1], hlT[0][:, :, i * 128:(i + 1) * 128], hlT[1], ident, bident)
        for cbi in range(8):
            load_w16(k, wv[0], wv[1], win_d[:, 4096 + cbi * 512:4096 + (cbi + 1) * 512])
            for i in range(4):
                pt, bpt = k.bank()
                k.mm(pt[:], bpt, [(hlT[0][:, kc, i * 128:(i + 1) * 128], wv[0][:, kc, :]) for kc in range(16)], reads=[hlT[1], wv[1]])
                blks = bvn[cbi * 4:(cbi + 1) * 4]
                dst = vn[:, i, cbi * 512:(cbi + 1) * 512]
                S_ = k.S
                S_.add("act", lambda e, pt=pt: e.activation(out=ga[0][:], in_=pt[:], func=AF.Square), reads=[bpt], writes=[ga[1]])
                S_.add("dve", lambda e: e.tensor_scalar(out=ga[0][:], in0=ga[0][:], scalar1=0.044715, scalar2=1.0, op0=ALU.mult, op1=ALU.add), reads=[ga[1]], writes=[ga[1]])
                S_.add("dve", lambda e, pt=pt: e.tensor_tensor(out=ga[0][:], in0=ga[0][:], in1=pt[:], op=ALU.mult), reads=[ga[1], bpt], writes=[ga[1]])
                S_.add("act", lambda e: e.activation(out=gb[0][:], in_=ga[0][:], func=AF.Sigmoid, scale=1.5957691216057308), reads=[ga[1]], writes=[gb[1]])
                S_.add("dve", lambda e, pt=pt, dst=dst, i=i, cbi=cbi: e.scalar_tensor_tensor(out=dst, in0=pt[:], scalar=1.0, in1=gb[0][:], op0=ALU.mult, op1=ALU.mult,
                                                                                         accum_out=sums[0][:, i, cbi:cbi + 1]),
                       reads=[bpt, gb[1]], writes=blks + [sums[1]])
                S_.add("act", lambda e, dst=dst, i=i, cbi=cbi: e.activation(out=ga[0][:], in_=dst, func=AF.Square, accum_out=sqs[0][:, i, cbi:cbi + 1]),
                       reads=blks, writes=[ga[1], sqs[1]])
        S.add("dve", lambda e: e.tensor_reduce(out=mean[0][:], in_=sums[0][:], axis=mybir.AxisListType.X, op=ALU.add), reads=[sums[1]], writes=[mean[1]])
        S.add("dve", lambda e: e.tensor_reduce(out=var[0][:], in_=sqs[0][:], axis=mybir.AxisListType.X, op=ALU.add), reads=[sqs[1]], writes=[var[1]])
        S.add("dve", lambda e: e.tensor_single_scalar(out=mean[0][:], in_=mean[0][:], scalar=1.0 / 4096, op=ALU.mult), reads=[mean[1]], writes=[mean[1]])
        S.add("dve", lambda e: e.tensor_tensor(out=msq[0][:], in0=mean[0][:], in1=mean[0][:], op=ALU.mult), reads=[mean[1]], writes=[msq[1]])
        S.add("dve", lambda e: e.scalar_tensor_tensor(out=var[0][:], in0=var[0][:], scalar=1.0 / 4096, in1=msq[0][:], op0=ALU.mult, op1=ALU.subtract),
              reads=[var[1], msq[1]], writes=[var[1]])
        S.add("dve", lambda e: e.tensor_single_scalar(out=var[0][:], in_=var[0][:], scalar=1e-5, op=ALU.add), reads=[var[1]], writes=[var[1]])
        S.add("act", lambda e: e.sqrt(out=var[0][:], in_=var[0][:]), reads=[var[1]], writes=[var[1]])
        S.add("dve", lambda e: e.reciprocal(out=var[0][:], in_=var[0][:]), reads=[var[1]], writes=[var[1]])
        for i in range(4):
            S.add("dve", lambda e, i=i: e.tensor_scalar(out=vn[:, i, :], in0=vn[:, i, :], scalar1=mean[0][:, i:i + 1], scalar2=var[0][:, i:i + 1],
                                                       op0=ALU.subtract, op1=ALU.mult), reads=bvn + [mean[1], var[1]], writes=bvn)
        for cbk in range(32):
            g = cbk // 2
            w, bw = wu[cbk % 2]
            load_w16(k, w, bw, win_d[:, cbk * 128:(cbk + 1) * 128])
            pt, bpt = k.bank()
            k.mm(pt[:], bpt, [(w[:, kc, :], hlT[0][:, kc, :]) for kc in range(16)], reads=[bw, hlT[1]])
            gelu_tanh(k, uTb[0][:], uTb[1], pt[:], bpt, (ga[0][:], ga[1]), (gb[0][:], gb[1]))
            pt2, bpt2 = k.bank()

            def fn(e, pt2=pt2, cbk=cbk, g=g):
                ins = None
                for i in range(4):
                    ins = e.matmul(pt2[:, i * 128:(i + 1) * 128], vn[:, i, cbk * 128:(cbk + 1) * 128], wsT[0][:, g, :], start=True, stop=True)
                return ins
            S.add("pe", fn, reads=[bvn[cbk], wsT[1]], writes=[bpt2])
            S.add("dve", lambda e, cbk=cbk, g=g: e.scalar_tensor_tensor(out=bfull[0][:], in0=rsB[0][:, g, :], scalar=lnb[0][:, cbk:cbk + 1], in1=bsB[0][:, g, :],
                                                                      op0=ALU.mult, op1=ALU.add), reads=[rsB[1], lnb[1], bsB[1]], writes=[bfull[1]])
            S.add("dve", lambda e, pt2=pt2, cbk=cbk: e.scalar_tensor_tensor(out=svt[0][:], in0=pt2[:].rearrange("p (i q) -> p i q", q=128), scalar=lng[0][:, cbk:cbk + 1],
                                                                          in1=bfull[0][:].unsqueeze(1).to_broadcast([128, 4, 128]), op0=ALU.mult, op1=ALU.add),
                  reads=[bpt2, lng[1], bfull[1]], writes=[svt[1]])
            S.add("dve", lambda e, cbk=cbk: e.tensor_tensor(out=vn[:, :, cbk * 128:(cbk + 1) * 128], in0=uTb[0][:].rearrange("p (i q) -> p i q", q=128),
                                                            in1=svt[0][:], op=ALU.mult), reads=[uTb[1], svt[1]], writes=[bvn[cbk]])
        for nb in range(8):
            cs = slice(nb * 256, (nb + 1) * 256)
            load_w16(k, wo[0], wo[1], wo_d[:, cs], nk=32)
            for i in range(4):
                rs = slice(hf * 512 + i * 128, hf * 512 + (i + 1) * 128)
                x, bx = xc[(nb * 4 + i) % 2]
                T, bT = Tt[(nb * 4 + i) % 2]
                k.load(x[:], x_d[rs, cs], bx, eng="sp")
                pt, bpt = k.bank()
                k.mm(pt[:, :256], bpt, [(vn[:, i, cbk * 128:(cbk + 1) * 128], wo[0][:, cbk, :]) for cbk in range(32)], reads=bvn + [wo[1]])
                S.add("dve", lambda e, pt=pt, T=T, cs=cs: e.tensor_tensor(out=T[:], in0=pt[:, :256], in1=G2[0][:, cs], op=ALU.mult), reads=[bpt, G2[1]], writes=[bT])
                S.add("pool", lambda e, T=T, x=x: e.tensor_tensor(out=T[:], in0=T[:], in1=x[:], op=ALU.add), reads=[bT, bx], writes=[bT])
                op = S.add("sp", lambda e, T=T, rs=rs, cs=cs: e.dma_start(out=x3_d[rs, cs], in_=T[:]), reads=[bT], writes=[bx3], dma=True)
                k.finals.append(op)
    k.load(t1[0][:], gF_d, t1[1])
    mod(wmB_d[:, D:3 * D], bmB_d[:, D:3 * D], 2 * D, lambda c0, n: (MA[c0 // D][0][:, c0 % D:c0 % D + n], MA[c0 // D][1]))
    S.add("dve", lambda e: e.scalar_tensor_tensor(out=M1[0][:], in0=M1[0][:], scalar=1.0, in1=t1[0][:], op0=ALU.add, op1=ALU.mult),
          reads=[M1[1], t1[1]], writes=[M1[1]])
    HO = V_(k, "HO", [128, D])
    for i in range(8):
        rs = slice(i * 128, (i + 1) * 128)
        k.load(X[0][:], x3_d[rs, :], X[1], eng="sp", reads=[bx3])
        rms_mod(k, X[0][:], X[1], M1[0][:], M1[1], M0[0][:], M0[1], HO[0][:], HO[1], junk, ss, t1)
        k.store(hl_d[rs, :], HO[0][:], HO[1])
    return k.done(own)


def launch_sgu(inp, x2):
    nc = build_sgu()
    L = 1
    def mcols(js):
        cols = np.concatenate([np.arange(j * D, (j + 1) * D) for j in js])
        return np.ascontiguousarray(inp["w_mod"][L][:, cols]), bc(inp["b_mod"][L][cols])
    wmA, bmA = mcols((0, 1)); wmB, bmB = mcols((2, 3, 4))
    lngT = np.ascontiguousarray(inp["sgu_ln_g"][0].reshape(32, 128).T); lnbT = np.ascontiguousarray(inp["sgu_ln_b"][0].reshape(32, 128).T)
    wsT = np.ascontiguousarray(inp["sgu_w_s"][0].transpose(2, 0, 1))
    bsB = np.ascontiguousarray(np.broadcast_to(inp["sgu_b_s"][0][None], (128, 16, 128)))
    maps = []
    for c in range(NCORES):
        b, q = c // 4, c % 4
        maps.append(dict(x=np.ascontiguousarray(x2[b, q * 1024:(q + 1) * 1024]), ccT=np.ascontiguousarray(inp["c"][b].reshape(1, 16, 128).transpose(2, 1, 0)),
                         wmodA=wmA, bmodBA=bmA, gMix=bc(inp["mix_norm_g"][L]), wmodB=wmB, bmodBB=bmB, gF=bc(inp["ffn_norm_g"][L]),
                         w_in=inp["sgu_w_in"][0], lngT=lngT, lnbT=lnbT, wsT=wsT, bsB=bsB, w_out=inp["sgu_w_out"][0],
                         ident=np.eye(128, dtype=np.float32), ones=np.ones((128, 128), np.float32)))
    res = run(nc, maps)
    x3 = np.zeros((2, 4096, D), np.float32); hl = np.zeros((2, 4096, D), np.float32)
    for c in range(NCORES):
        x3[c // 4, (c % 4) * 1024:(c % 4 + 1) * 1024] = res[c]["x3"]
        hl[c // 4, (c % 4) * 1024:(c % 4 + 1) * 1024] = res[c]["hl"]
    return x3, hl


def build_L3():
    k = KB()
    ml_s = k.scratch("ml_scratch", [1024, D])
    with k.stage("a_", io={"ml": ml_s}):
        build_L3a(k)
    with k.stage("b_", io={"ml": ml_s}):
        build_resnorm(k)
    return k.finish()


def launch_L3(inp, y_f, y_r, u_lat, x, layer, gF, mod_cols):
    nc = build_L3()
    ident = np.eye(128, dtype=np.float32)
    cols = np.concatenate([np.arange(j * D, (j + 1) * D) for j in mod_cols])
    wmod = np.ascontiguousarray(inp["w_mod"][layer][:, cols])
    bmodB = bc(inp["b_mod"][layer][cols])
    maps = []
    for c in range(NCORES):
        b, q = c // 4, c % 4
        sl = slice(q * 1024, (q + 1) * 1024)
        ccT = np.ascontiguousarray(inp["c"][b].reshape(1, 16, 128).transpose(2, 1, 0))
        maps.append(dict(a_yf=np.ascontiguousarray(y_f[b, sl]), a_yr=np.ascontiguousarray(y_r[b, sl]), a_u=np.ascontiguousarray(u_lat[b, sl]),
                         a_dB=bc(inp["s5_d"][0]), a_bgluB=bc(inp["s5_b_glu"][0]), a_w_glu=inp["s5_w_glu"][0], a_w_out=inp["s5_w_out"][0], a_ident=ident,
                         b_x=np.ascontiguousarray(x[b, sl]), b_ccT=ccT, b_wmod=wmod, b_bmodB=bmodB, b_gF=bc(gF)))
    res = run(nc, maps)
    x1 = np.zeros((2, 4096, D), np.float32); hl = np.zeros((2, 4096, D), np.float32)
    for c in range(NCORES):
        x1[c // 4, (c % 4) * 1024:(c % 4 + 1) * 1024] = res[c]["b_x1"]
        hl[c // 4, (c % 4) * 1024:(c % 4 + 1) * 1024] = res[c]["b_hl"]
    return x1, hl


def build_L4():
    k = KB()
    x2_s = k.scratch("x2_scratch", [1024, D])
    with k.stage("f_", io={"xo": x2_s}):
        build_ffn(False, k)
    with k.stage("s_", io={"x": x2_s}):
        build_sgu(k)
    return k.finish()


def ffn_maps(inp, x, hl, layer, final, prefix=""):
    cols = np.arange(5 * D, 6 * D)
    wmod = np.ascontiguousarray(inp["w_mod"][layer][:, cols]); bmodB = bc(inp["b_mod"][layer][cols])
    cw = np.ascontiguousarray(inp["ffn_conv_w"][layer].reshape(9, 88, 128).transpose(2, 1, 0))
    cb = np.ascontiguousarray(inp["ffn_conv_b"][layer].reshape(88, 128).T)
    maps = []
    for c in range(NCORES):
        b, q = c // 4, c % 4
        ext = np.zeros((1152, D), np.float32)
        lo, hi = q * 1024 - 64, q * 1024 + 1024 + 64
        slo, shi = max(lo, 0), min(hi, 4096)
        ext[slo - lo:shi - lo] = hl[b, slo:shi]
        m = dict(hlT=np.ascontiguousarray(ext.T), x=np.ascontiguousarray(x[b, q * 1024:(q + 1) * 1024]),
                 ccT=np.ascontiguousarray(inp["c"][b].reshape(1, 16, 128).transpose(2, 1, 0)), wmod=wmod, bmodB=bmodB,
                 w_up=inp["ffn_w_up"][layer], cw=cw, cb=cb, w_down=inp["ffn_w_down"][layer])
        if final:
            m["gfin"] = bc(inp["final_norm_g"])
        maps.append({prefix + k_: v for k_, v in m.items()})
    return maps


def sgu_maps(inp, x2, prefix="", with_x=True):
    L = 1

    def mcols(js):
        cols = np.concatenate([np.arange(j * D, (j + 1) * D) for j in js])
        return np.ascontiguousarray(inp["w_mod"][L][:, cols]), bc(inp["b_mod"][L][cols])
    wmA, bmA = mcols((0, 1)); wmB, bmB = mcols((2, 3, 4))
    lngT = np.ascontiguousarray(inp["sgu_ln_g"][0].reshape(32, 128).T); lnbT = np.ascontiguousarray(inp["sgu_ln_b"][0].reshape(32, 128).T)
    wsT = np.ascontiguousarray(inp["sgu_w_s"][0].transpose(2, 0, 1))
    bsB = np.ascontiguousarray(np.broadcast_to(inp["sgu_b_s"][0][None], (128, 16, 128)))
    maps = []
    for c in range(NCORES):
        b, q = c // 4, c % 4
        m = dict(ccT=np.ascontiguousarray(inp["c"][b].reshape(1, 16, 128).transpose(2, 1, 0)),
                 wmodA=wmA, bmodBA=bmA, gMix=bc(inp["mix_norm_g"][L]), wmodB=wmB, bmodBB=bmB, gF=bc(inp["ffn_norm_g"][L]),
                 w_in=inp["sgu_w_in"][0], lngT=lngT, lnbT=lnbT, wsT=wsT, bsB=bsB, w_out=inp["sgu_w_out"][0],
                 ident=np.eye(128, dtype=np.float32), ones=np.ones((128, 128), np.float32))
        if with_x:
            m["x"] = np.ascontiguousarray(x2[b, q * 1024:(q + 1) * 1024])
        maps.append({prefix + k_: v for k_, v in m.items()})
    return maps


def launch_L4(inp, x1, hl0):
    nc = build_L4()
    fm = ffn_maps(inp, x1, hl0, 0, False, "f_")
    sm = sgu_maps(inp, None, "s_", with_x=False)
    maps = [dict(**fm[c], **sm[c]) for c in range(NCORES)]
    res = run(nc, maps)
    x3 = np.zeros((2, 4096, D), np.float32); hl = np.zeros((2, 4096, D), np.float32)
    for c in range(NCORES):
        x3[c // 4, (c % 4) * 1024:(c % 4 + 1) * 1024] = res[c]["s_x3"]
        hl[c // 4, (c % 4) * 1024:(c % 4 + 1) * 1024] = res[c]["s_hl"]
    return x3, hl


def kernel(**inputs):
    inp = {k_: np.asarray(v, dtype=np.float32) for k_, v in inputs.items()}
    u_lat, u_ctx = launch_L1(inp)
    y_f, y_r = launch_L2(inp, u_lat, u_ctx)
    x1, hl0 = launch_L3(inp, y_f, y_r, u_lat, inp["x"], 0, inp["ffn_norm_g"][0], (2, 3, 4))
    x3, hl1 = launch_L4(inp, x1, hl0)
    out = launch_ffn(inp, x3, hl1, 1, True)
    return out.astype(np.float32)
```

```python
import contextlib
import numpy as np
import concourse.bass as bass
import concourse.mybir as mybir
from concourse.bass_utils import run_bass_kernel_spmd

F32 = mybir.dt.float32
BF16 = mybir.dt.bfloat16
ALU = mybir.AluOpType
AF = mybir.ActivationFunctionType
NCORES = 8
D = 2048


class Buf:
    __slots__ = ("name", "last_writer", "readers")

    def __init__(self, name=""):
        self.name = name
        self.last_writer = None
        self.readers = []


class Op:
    __slots__ = ("eng", "fn", "deps", "dma", "sig", "needed", "inc")

    def __init__(self, eng, fn, dma, inc):
        self.eng, self.fn, self.dma, self.inc = eng, fn, dma, inc
        self.deps, self.sig, self.needed = [], None, False


ENGS = ("pe", "dve", "act", "pool", "sp")
N_DMA_SEMS = 8


class Sched:
    def __init__(self, nc):
        self.nc = nc
        self.ops = []

    def add(self, eng, fn, reads=(), writes=(), dma=False):
        op = Op(eng, fn, dma, 16 if dma else 1)
        deps = set()
        for r in reads:
            if r.last_writer is not None:
                deps.add(r.last_writer)
        for w in writes:
            if w.last_writer is not None:
                deps.add(w.last_writer)
            deps.update(w.readers)
        for r in reads:
            r.readers.append(op)
        for w in writes:
            w.last_writer = op
            w.readers = []
        op.deps = [d for d in deps if not (d.eng == "pe" and eng == "pe")]
        for d in op.deps:
            d.needed = True
        if dma:
            op.needed = True
        self.ops.append(op)
        return op

    def barrier(self):
        last = {}
        dmas = {}
        for op in self.ops:
            last[op.eng] = op
            if op.dma:
                dmas.setdefault(op.eng, []).append(op)
        deps = list(last.values())
        for e_, lst in dmas.items():
            deps.extend(lst[-N_DMA_SEMS:])
        for d in deps:
            d.needed = True
        for e_ in ENGS:
            op = Op(e_, lambda e: e.nop(), False, 1)
            op.deps = list(deps)
            self.ops.append(op)

    def emit(self, final_ops):
        nc = self.nc
        for o in final_ops:
            o.needed = True
        with contextlib.ExitStack() as st:
            comp_sem = {e: st.enter_context(nc.semaphore(f"s_{e}")) for e in ENGS}
            dma_sems = {e: [st.enter_context(nc.semaphore(f"d_{e}{i}")) for i in range(N_DMA_SEMS)]
                        for e in ("sp", "act", "pool")}
            comp_cnt = {e: 0 for e in ENGS}
            dma_cnt = {e: 0 for e in ENGS}
            sem_total = {}
            per_eng = {e: [] for e in ENGS}
            for op in self.ops:
                per_eng[op.eng].append(op)
                if op.dma:
                    n = dma_cnt[op.eng]
                    dma_cnt[op.eng] += 1
                    sem = dma_sems[op.eng][n % N_DMA_SEMS]
                    before = sem_total.get(id(sem), 0)
                    sem_total[id(sem)] = before + op.inc
                    op.sig = (sem, before + op.inc, before)
                elif op.needed:
                    comp_cnt[op.eng] += 1
                    op.sig = (comp_sem[op.eng], comp_cnt[op.eng], None)
            final = list(final_ops)

            def run_engine(ename, eng):
                seen = {}

                def wait(sem, val):
                    if seen.get(id(sem), 0) >= val:
                        return
                    eng.wait_ge(sem, val)
                    seen[id(sem)] = val

                for op in per_eng[ename]:
                    for d in op.deps:
                        wait(d.sig[0], d.sig[1])
                    if op.dma and op.sig[2] > 0:
                        wait(op.sig[0], op.sig[2])
                    ins = op.fn(eng)
                    if op.sig is not None:
                        ins.then_inc(op.sig[0], op.inc if op.dma else 1)
                if ename == "sp":
                    for o in final:
                        wait(o.sig[0], o.sig[1])

            with nc.Block() as block:
                block.tensor(lambda e: run_engine("pe", e))
                block.vector(lambda e: run_engine("dve", e))
                block.scalar(lambda e: run_engine("act", e))
                block.gpsimd(lambda e: run_engine("pool", e))
                block.sync(lambda e: run_engine("sp", e))


class KB:
    def __init__(self):
        self.nc = bass.Bass("TRN2", target_bir_lowering=False)
        self.S = Sched(self.nc)
        self.st = contextlib.ExitStack()
        self.finals = []
        self.banks = []
        for i in range(8):
            t = self.st.enter_context(self.nc.psum_tensor(f"ps{i}", [128, 512], F32))
            self.banks.append((t, Buf(f"ps{i}")))
        self.bi = 0
        self.rr = 0
        self.prefix = ""
        self.io = {}
        self.stacks = [self.st]

    @contextlib.contextmanager
    def stage(self, prefix, io=None):
        old = (self.prefix, self.io)
        self.prefix, self.io = prefix, dict(io or {})
        st = contextlib.ExitStack()
        self.stacks.append(st)
        try:
            yield self
        finally:
            self.S.barrier()
            self.stacks.pop()
            st.close()
            self.prefix, self.io = old

    def bank(self):
        b = self.banks[self.bi % 8]
        self.bi += 1
        return b

    def sb(self, name, shape, dt=F32):
        return self.stacks[-1].enter_context(self.nc.sbuf_tensor(self.prefix + name, list(shape), dt))

    def din(self, name, shape, dt=F32):
        if name in self.io:
            return self.io[name]
        return self.nc.dram_tensor(self.prefix + name, list(shape), dt, kind="ExternalInput").ap()

    def dout(self, name, shape, dt=F32):
        if name in self.io:
            return self.io[name]
        return self.nc.dram_tensor(self.prefix + name, list(shape), dt, kind="ExternalOutput").ap()

    def scratch(self, name, shape, dt=F32):
        return self.nc.dram_tensor(name, list(shape), dt, kind="Internal").ap()

    def load(self, dst_ap, src_ap, buf, eng=None, reads=()):
        if eng is None:
            eng = ("sp", "pool")[self.rr % 2]
            self.rr += 1
        return self.S.add(eng, lambda e: e.dma_start(out=dst_ap, in_=src_ap), reads=reads, writes=[buf], dma=True)

    def store(self, dst_ap, src_ap, buf, eng="sp"):
        op = self.S.add(eng, lambda e: e.dma_start(out=dst_ap, in_=src_ap), reads=[buf], dma=True)
        self.finals.append(op)
        return op

    def done(self, own):
        return self.finish() if own else None

    def mm(self, out_ap, out_buf, pairs, reads):
        def fn(e):
            n = len(pairs)
            ins = None
            for i, (l, r) in enumerate(pairs):
                ins = e.matmul(out_ap, l, r, start=(i == 0), stop=(i == n - 1))
            return ins
        return self.S.add("pe", fn, reads=reads, writes=[out_buf])

    def finish(self):
        self.S.emit(self.finals)
        self.st.close()
        return self.nc


def run(nc, in_maps):
    res = run_bass_kernel_spmd(nc, in_maps, core_ids=list(range(NCORES)))
    return res.results


def bc(v, p=128):
    return np.ascontiguousarray(np.broadcast_to(np.asarray(v, np.float32).reshape(1, -1), (p, v.size)))


def norm_mod_T(k, xt_ap, rows, A, B, bA, bB, bx, ident, bident, tag, bufs, want_hl=False):
    S = k.S
    junk, bjunk = bufs["junk"]
    ss, bss = bufs["ss"]
    rstd, brstd = bufs["rstd"]
    t1, bt1 = bufs["t1"]
    hl, bhl = bufs["hl"]
    hlT, bhlT = bufs["hlT"]
    S.add("act", lambda e: e.activation(out=junk[:rows, :], in_=xt_ap, func=AF.Square, accum_out=ss[:rows, 0:1]),
          reads=[bx], writes=[bjunk, bss])
    S.add("dve", lambda e: e.tensor_scalar(out=rstd[:rows, 0:1], in0=ss[:rows, 0:1], scalar1=1.0 / D, scalar2=1e-6,
                                           op0=ALU.mult, op1=ALU.add), reads=[bss], writes=[brstd])
    S.add("act", lambda e: e.sqrt(out=rstd[:rows, 0:1], in_=rstd[:rows, 0:1]), reads=[brstd], writes=[brstd])
    S.add("dve", lambda e: e.reciprocal(out=rstd[:rows, 0:1], in_=rstd[:rows, 0:1]), reads=[brstd], writes=[brstd])
    S.add("dve", lambda e: e.scalar_tensor_tensor(out=t1[:rows, :], in0=xt_ap, scalar=rstd[:rows, 0:1], in1=A[:rows, :],
                                                  op0=ALU.mult, op1=ALU.mult), reads=[bx, brstd, bA], writes=[bt1])
    S.add("dve", lambda e: e.tensor_tensor(out=hl[:rows, :], in0=t1[:rows, :], in1=B[:rows, :], op=ALU.add),
          reads=[bt1, bB], writes=[bhl])
    for half in range(2):
        pt, bpt = k.bank()
        ptb = pt[:].bitcast(BF16)

        def fn(e, half=half, ptb=ptb):
            ins = None
            for j in range(8):
                kc = half * 8 + j
                ins = e.transpose(ptb[:, j * 128:j * 128 + rows], hl[:rows, kc * 128:(kc + 1) * 128], ident[:rows, :rows])
            return ins
        S.add("pe", fn, reads=[bhl, bident], writes=[bpt])
        S.add("act", lambda e, half=half, ptb=ptb: e.copy(
            out=hlT[:, half * 8:(half + 1) * 8, :rows],
            in_=ptb.rearrange("p (j t) -> p j t", t=128)[:, :, :rows]), reads=[bpt], writes=[bhlT])


def build_mod(k, ccT, bcc, wmod_d, bmodB_d, ncols, nrows_m, outs, NW=128, nbuf=2):
    S = k.S
    sc = k.sb("sc", [128, 16, nrows_m]); bsc = Buf()
    scb = k.sb("scb", [128, 16, nrows_m, 128]); bscb = Buf()
    S.add("act", lambda e: e.activation(out=sc[:], in_=ccT[:], func=AF.Silu), reads=[bcc], writes=[bsc])
    S.add("dve", lambda e: e.tensor_copy(out=scb[:], in_=sc[:].unsqueeze(3).to_broadcast([128, 16, nrows_m, 128])),
          reads=[bsc], writes=[bscb])
    wst = [(k.sb(f"wst{i}", [128, 16, NW]), Buf()) for i in range(nbuf)]
    bmt = [(k.sb(f"bmt{i}", [128, NW]), Buf()) for i in range(nbuf)]
    for nb in range(ncols // NW):
        w, bw = wst[nb % nbuf]
        bm, bbm = bmt[nb % nbuf]
        k.load(w[:], wmod_d[:, nb * NW:(nb + 1) * NW].rearrange("(kc p) n -> p kc n", p=128), bw)
        k.load(bm[:], bmodB_d[:, nb * NW:(nb + 1) * NW], bbm)
        for m in range(nrows_m):
            pt, bpt = k.bank()
            k.mm(pt[:, :NW], bpt, [(scb[:, kc, m, :], w[:, kc, :]) for kc in range(16)], reads=[bscb, bw])
            dst, bdst = outs[m](nb * NW, NW)
            S.add("dve", lambda e, pt=pt, dst=dst, bm=bm: e.tensor_tensor(out=dst, in0=pt[:, :NW], in1=bm[:], op=ALU.add),
                  reads=[bpt, bbm], writes=[bdst])


def build_L1():
    k = KB()
    S = k.S
    T = 1088
    xs = k.din("xs", [T, D]); ccT_d = k.din("ccT", [128, 16, 2]); gB_d = k.din("gB", [128, D])
    wmod_d = k.din("wmod", [D, 4096]); bmodB_d = k.din("bmodB", [128, 4096]); win_d = k.din("w_in", [D, D])
    ident_d = k.din("ident", [128, 128])
    u_d = k.dout("u", [T, D])
    ident = k.sb("ident_sb", [128, 128], BF16); bident = Buf()
    k.load(ident[:], ident_d, bident, eng="pool")
    ccT = k.sb("ccT_sb", [128, 16, 2]); bcc = Buf()
    k.load(ccT[:], ccT_d, bcc)
    gB = k.sb("gB_sb", [128, D]); bgB = Buf()
    k.load(gB[:], gB_d, bgB)
    AB = [[(k.sb(f"AB{m}{j}", [128, D]), Buf()) for j in range(2)] for m in range(2)]
    outs = [(lambda c0, n, m=m: (AB[m][c0 // D][0][:, c0 % D:c0 % D + n], AB[m][c0 // D][1])) for m in range(2)]
    build_mod(k, ccT, bcc, wmod_d, bmodB_d, 4096, 2, outs)
    for m in range(2):
        A, bA = AB[m][1]
        S.add("dve", lambda e, A=A: e.scalar_tensor_tensor(out=A[:], in0=A[:], scalar=1.0, in1=gB[:], op0=ALU.add, op1=ALU.mult),
              reads=[bA, bgB], writes=[bA])
    win = k.sb("win", [128, 16, D], BF16); bwin = Buf()
    k.load(win[:], win_d.rearrange("(kc p) n -> p kc n", p=128), bwin, eng="pool")
    xt = [(k.sb(f"xt{i}", [128, D]), Buf()) for i in range(2)]
    ut = [(k.sb(f"ut{i}", [128, D]), Buf()) for i in range(2)]
    junk_ = (k.sb("junk", [128, D], BF16), Buf())
    t1_ = (k.sb("t1", [128, D]), Buf())
    nb_ = [dict(junk=junk_, ss=(k.sb(f"ss{i}", [128, 1]), Buf()),
                rstd=(k.sb(f"rstd{i}", [128, 1]), Buf()), t1=t1_,
                hl=(k.sb(f"hl{i}", [128, D], BF16), Buf()), hlT=(k.sb(f"hlT{i}", [128, 16, 128], BF16), Buf()))
           for i in range(2)]
    for i in range(9):
        rows = 128 if i < 8 else 64
        m = 0 if i < 8 else 1
        x, bx = xt[i % 2]
        k.load(x[:rows, :], xs[i * 128:i * 128 + rows, :], bx, eng="sp")
        bufs = nb_[i % 2]
        norm_mod_T(k, x[:rows, :], rows, AB[m][1][0], AB[m][0][0], AB[m][1][1], AB[m][0][1], bx, ident, bident, "l1", bufs)
        hlT, bhlT = bufs["hlT"]
        u, bu = ut[i % 2]
        for nb in range(4):
            pt, bpt = k.bank()
            k.mm(pt[:rows, :], bpt, [(hlT[:, kc, :rows], win[:, kc, nb * 512:(nb + 1) * 512]) for kc in range(16)],
                 reads=[bhlT, bwin])
            eng = "act" if nb % 2 == 0 else "dve"
            if eng == "act":
                S.add("act", lambda e, pt=pt, u=u, nb=nb, rows=rows: e.copy(out=u[:rows, nb * 512:(nb + 1) * 512], in_=pt[:rows, :]),
                      reads=[bpt], writes=[bu])
            else:
                S.add("dve", lambda e, pt=pt, u=u, nb=nb, rows=rows: e.tensor_copy(out=u[:rows, nb * 512:(nb + 1) * 512], in_=pt[:rows, :]),
                      reads=[bpt], writes=[bu])
        k.store(u_d[i * 128:i * 128 + rows, :], u[:rows, :], bu)
    return k.finish()


def launch_L1(inp):
    nc = build_L1()
    maps = []
    ident = np.eye(128, dtype=np.float32)
    for c in range(NCORES):
        b, q = c // 4, c % 4
        xs = np.concatenate([inp["x"][b, q * 1024:(q + 1) * 1024], inp["ctx"][b, q * 64:(q + 1) * 64]], 0)
        cc = np.stack([inp["c"][b], inp["c_ctx"]], 0)
        ccT = np.ascontiguousarray(cc.reshape(2, 16, 128).transpose(2, 1, 0))
        maps.append(dict(xs=np.ascontiguousarray(xs), ccT=ccT, gB=bc(inp["mix_norm_g"][0]),
                         wmod=np.ascontiguousarray(inp["w_mod"][0][:, :4096]), bmodB=bc(inp["b_mod"][0][:4096]),
                         w_in=np.ascontiguousarray(inp["s5_w_in"][0]), ident=ident))
    res = run(nc, maps)
    u_lat = np.zeros((2, 4096, D), np.float32)
    u_ctx = np.zeros((2, 256, D), np.float32)
    for c in range(NCORES):
        b, q = c // 4, c % 4
        u_lat[b, q * 1024:(q + 1) * 1024] = res[c]["u"][:1024]
        u_ctx[b, q * 64:(q + 1) * 64] = res[c]["u"][1024:]
    return u_lat, u_ctx


SEQ_T = 4352
SEG = 1088


def build_L2():
    k = KB()
    S = k.S
    V = lambda name, shape, dt=F32: (k.sb(name, shape, dt), Buf(name))
    uTd = [k.din("uT_f", [256, 2 * SEQ_T]), k.din("uT_r", [256, 2 * SEQ_T])]
    yTd = [k.dout("yT_f", [256, 2 * SEQ_T]), k.dout("yT_r", [256, 2 * SEQ_T])]
    pd = {n: k.din(n, [128, 2, 8]) for n in ("are", "aim", "lst")}
    pd4 = {n: k.din(n, [128, 2, 8, 16]) for n in ("bre", "bim", "cre", "cim")}
    ident_d = k.din("ident", [128, 128])
    ident, bident = V("ident_sb", [128, 128])
    k.load(ident[:], ident_d, bident)
    P = {}
    for n, d_ in pd.items():
        P[n] = V(n + "_sb", [128, 16])
        k.load(P[n][0][:], d_.rearrange("p d r -> p (d r)"), P[n][1])
    P4 = {}
    for n, d_ in pd4.items():
        P4[n] = V(n + "_sb", [128, 16, 16])
        k.load(P4[n][0][:], d_.rearrange("p d r h -> p (d r) h"), P4[n][1])

    cnt = [0]

    def tmp(shape=(128, 16)):
        cnt[0] += 1
        return V(f"tmp{cnt[0]}", list(shape))

    def tt(out, a, b, op, eng="dve"):
        S.add(eng, lambda e: e.tensor_tensor(out=out[0][:], in0=a[0][:], in1=b[0][:], op=op), reads=[a[1], b[1]], writes=[out[1]])
        return out

    def ts(out, a, s1, op0, s2=None, op1=None):
        if op1 is None:
            S.add("dve", lambda e: e.tensor_single_scalar(out=out[0][:], in_=a[0][:], scalar=s1, op=op0), reads=[a[1]], writes=[out[1]])
        else:
            S.add("dve", lambda e: e.tensor_scalar(out=out[0][:], in0=a[0][:], scalar1=s1, scalar2=s2, op0=op0, op1=op1),
                  reads=[a[1]], writes=[out[1]])
        return out

    def stt(out, a, s, b, op0, op1):
        S.add("dve", lambda e: e.scalar_tensor_tensor(out=out[0][:], in0=a[0][:], scalar=s, in1=b[0][:], op0=op0, op1=op1),
              reads=[a[1], b[1]], writes=[out[1]])
        return out

    def act(out, a, func):
        S.add("act", lambda e: e.activation(out=out[0][:], in_=a[0][:], func=func), reads=[a[1]], writes=[out[1]])
        return out

    are, aim, lst = P["are"], P["aim"], P["lst"]
    dt = act(tmp(), lst, AF.Exp)
    adt = tt(tmp(), are, dt, ALU.mult)
    mag = act(tmp(), adt, AF.Exp)
    th = tt(tmp(), aim, dt, ALU.mult)
    y = ts(tmp(), th, 1.0 / 32.0, ALU.mult)
    y2 = tt(tmp(), y, y, ALU.mult)
    p = ts(tmp(), y2, 1.0 / 362880.0, ALU.mult)
    for c_ in (-1.0 / 5040.0, 1.0 / 120.0, -1.0 / 6.0):
        p = stt(tmp(), p, c_, y2, ALU.add, ALU.mult)
    s = stt(tmp(), p, 1.0, y, ALU.add, ALU.mult)
    q = ts(tmp(), y2, -1.0 / 3628800.0, ALU.mult)
    for c_ in (1.0 / 40320.0, -1.0 / 720.0, 1.0 / 24.0, -0.5):
        q = stt(tmp(), q, c_, y2, ALU.add, ALU.mult)
    c = ts(tmp(), q, 1.0, ALU.add)
    for _ in range(5):
        s_n = stt(tmp(), s, 2.0, c, ALU.mult, ALU.mult)
        t_ = stt(tmp(), s, -2.0, s, ALU.mult, ALU.mult)
        c = ts(tmp(), t_, 1.0, ALU.add)
        s = s_n
    cth, sth = c, s
    lr = tt(tmp(), mag, cth, ALU.mult)
    li = tt(tmp(), mag, sth, ALU.mult)
    den = tt(tmp(), tt(tmp(), are, are, ALU.mult), tt(tmp(), aim, aim, ALU.mult), ALU.add)
    rden = tmp()
    S.add("dve", lambda e: e.reciprocal(out=rden[0][:], in_=den[0][:]), reads=[den[1]], writes=[rden[1]])
    lm1 = ts(tmp(), lr, -1.0, ALU.add)
    kre = tt(tmp(), tt(tmp(), tt(tmp(), lm1, are, ALU.mult), tt(tmp(), li, aim, ALU.mult), ALU.add), rden, ALU.mult)
    kim = tt(tmp(), tt(tmp(), tt(tmp(), li, are, ALU.mult), tt(tmp(), lm1, aim, ALU.mult), ALU.subtract), rden, ALU.mult)

    def bmul(name, a, b4):
        o = V(name, [128, 16, 16])
        S.add("dve", lambda e: e.tensor_tensor(out=o[0][:], in0=b4[0][:], in1=a[0][:].unsqueeze(2).to_broadcast([128, 16, 16]), op=ALU.mult),
              reads=[a[1], b4[1]], writes=[o[1]])
        return o
    bbr = tt(V("bbr", [128, 16, 16]), bmul("m1", kre, P4["bre"]), bmul("m2", kim, P4["bim"]), ALU.subtract)
    bbi = tt(V("bbi", [128, 16, 16]), bmul("m3", kre, P4["bim"]), bmul("m4", kim, P4["bre"]), ALU.add)
    ncim = ts(V("ncim", [128, 16, 16]), P4["cim"], -1.0, ALU.mult)

    def blockify(name, src, dt_):
        o = V(name, [128, 16, 32], dt_)
        S.add("dve", lambda e: e.memset(o[0][:], 0.0), writes=[o[1]])
        S.add("dve", lambda e: e.tensor_copy(out=o[0][0:64, :, 0:16], in_=src[0][0:64, :, :]), reads=[src[1]], writes=[o[1]])
        S.add("dve", lambda e: e.tensor_copy(out=o[0][64:128, :, 16:32], in_=src[0][64:128, :, :]), reads=[src[1]], writes=[o[1]])
        return o
    CTre = blockify("CTre", P4["cre"], BF16)
    CTim = blockify("CTim", ncim, BF16)
    BLre = blockify("BLre", bbr, F32)
    BLim = blockify("BLim", bbi, F32)
    BbT = [V("BbTre", [32, 16, 128], BF16), V("BbTim", [32, 16, 128], BF16)]
    for ci, BL in enumerate((BLre, BLim)):
        for g4 in range(4):
            pt, bpt = k.bank()

            def fn(e, BL=BL, g4=g4, pt=pt):
                ins = None
                for j in range(4):
                    ins = e.transpose(pt[0:32, j * 128:(j + 1) * 128], BL[0][:, g4 * 4 + j, :], ident[:, :])
                return ins
            S.add("pe", fn, reads=[BL[1], bident], writes=[bpt])
            S.add("act", lambda e, pt=pt, g4=g4, ci=ci: e.copy(out=BbT[ci][0][:, g4 * 4:(g4 + 1) * 4, :],
                                                             in_=pt[0:32, :].rearrange("p (j t) -> p j t", t=128)),
                  reads=[bpt], writes=[BbT[ci][1]])

    CM = [V("cma", [128, 16, 64]), V("cmb", [128, 16, 64])]

    def cmul_bc(o_re, o_im, a_re, a_im, p_re, p_im, L, n):
        W_ = o_re[0].shape[2]
        t1 = (CM[0][0][:, :, 0:n], CM[0][1]); t2 = (CM[1][0][:, :, 0:n], CM[1][1])
        pb_re = lambda: p_re[0][:].unsqueeze(2).to_broadcast([128, 16, n])
        pb_im = lambda: p_im[0][:].unsqueeze(2).to_broadcast([128, 16, n])
        S.add("dve", lambda e: e.tensor_tensor(out=t1[0][:], in0=a_re[0][:, :, 0:n], in1=pb_re(), op=ALU.mult), reads=[a_re[1], p_re[1]], writes=[t1[1]])
        S.add("dve", lambda e: e.tensor_tensor(out=t2[0][:], in0=a_im[0][:, :, 0:n], in1=pb_im(), op=ALU.mult), reads=[a_im[1], p_im[1]], writes=[t2[1]])
        S.add("dve", lambda e: e.tensor_tensor(out=o_re[0][:, :, L:L + n], in0=t1[0][:], in1=t2[0][:], op=ALU.subtract), reads=[t1[1], t2[1]], writes=[o_re[1]])
        S.add("dve", lambda e: e.tensor_tensor(out=t1[0][:], in0=a_re[0][:, :, 0:n], in1=pb_im(), op=ALU.mult), reads=[a_re[1], p_im[1], o_re[1]], writes=[t1[1]])
        S.add("dve", lambda e: e.tensor_tensor(out=t2[0][:], in0=a_im[0][:, :, 0:n], in1=pb_re(), op=ALU.mult), reads=[a_im[1], p_re[1], o_re[1]], writes=[t2[1]])
        S.add("dve", lambda e: e.tensor_tensor(out=o_im[0][:, :, L:L + n], in0=t1[0][:], in1=t2[0][:], op=ALU.add), reads=[t1[1], t2[1]], writes=[o_im[1]])

    def csq(p_re, p_im):
        a = tt(tmp(), p_re, p_re, ALU.mult); b = tt(tmp(), p_im, p_im, ALU.mult)
        n_re = tt(tmp(), a, b, ALU.subtract)
        n_im = stt(tmp(), p_re, 2.0, p_im, ALU.mult, ALU.mult)
        return n_re, n_im

    def power_table(name, p_re, p_im, n):
        T_re = V(name + "re", [128, 16, n]); T_im = V(name + "im", [128, 16, n])
        S.add("dve", lambda e: e.memset(T_re[0][:, :, 0:1], 1.0), writes=[T_re[1]])
        S.add("dve", lambda e: e.memset(T_im[0][:, :, 0:1], 0.0), writes=[T_im[1]])
        L = 1
        while L < n:
            m = min(L, n - L)
            cmul_bc(T_re, T_im, T_re, T_im, p_re, p_im, L, m)
            p_re, p_im = csq(p_re, p_im)
            L *= 2
        return T_re, T_im, p_re, p_im
    E64re, E64im, q_re, q_im = power_table("E64", cth, sth, 64)
    Fre, Fim, _, _ = power_table("F68", q_re, q_im, 68)

    Ere, Eim = V("Ere", [128, 68, 64]), V("Eim", [128, 68, 64])
    Et1, Et2 = V("Et1", [128, 34, 64]), V("Et2", [128, 34, 64])
    Rt = V("Rt", [128, SEG])
    Wsets = [{n: V(n + str(i), [128, SEG]) for n in ("xre", "xim", "A1", "A2", "T1")} for i in range(2)]
    Ssets = [(V(f"sre{i}", [128, SEG], BF16), V(f"sim{i}", [128, SEG], BF16)) for i in range(2)]
    UPF = 2
    usb = [V(f"usb{i}", [32, SEG], BF16) for i in range(UPF + 1)]
    iters = [(d, pr, b, sgi) for d in range(2) for pr in range(8) for b in range(2) for sgi in range(SEQ_T // SEG)]

    def issue_u(n_it):
        d_, pr_, b_, sg_ = iters[n_it]
        u__, bu__ = usb[n_it % (UPF + 1)]
        k.load(u__[:], uTd[d_][pr_ * 32:(pr_ + 1) * 32, b_ * SEQ_T + sg_ * SEG:b_ * SEQ_T + (sg_ + 1) * SEG], bu__, eng="pool")

    yt = [V(f"yt{i}", [32, SEG]) for i in range(2)]
    carries = [V(f"carry{i}", [128, 2]) for i in range(2)]
    it = 0
    for d in range(2):
        for pr in range(8):
            dp = d * 8 + pr
            for hf in range(2):
                Fb = lambda T, hf=hf, dp=dp: T[0][:, dp, hf * 34:(hf + 1) * 34].unsqueeze(2).to_broadcast([128, 34, 64])
                Eb = lambda T, dp=dp: T[0][:, dp, :].unsqueeze(1).to_broadcast([128, 34, 64])
                Eo_re = Ere[0][:, hf * 34:(hf + 1) * 34, :]
                Eo_im = Eim[0][:, hf * 34:(hf + 1) * 34, :]
                S.add("pool", lambda e, Fb=Fb, Eb=Eb: e.tensor_tensor(out=Et1[0][:], in0=Fb(Fre), in1=Eb(E64re), op=ALU.mult), reads=[Fre[1], E64re[1]], writes=[Et1[1]])
                S.add("pool", lambda e, Fb=Fb, Eb=Eb: e.tensor_tensor(out=Et2[0][:], in0=Fb(Fim), in1=Eb(E64im), op=ALU.mult), reads=[Fim[1], E64im[1]], writes=[Et2[1]])
                S.add("pool", lambda e, Eo_re=Eo_re: e.tensor_tensor(out=Eo_re, in0=Et1[0][:], in1=Et2[0][:], op=ALU.subtract), reads=[Et1[1], Et2[1]], writes=[Ere[1]])
                S.add("pool", lambda e, Fb=Fb, Eb=Eb: e.tensor_tensor(out=Et1[0][:], in0=Fb(Fre), in1=Eb(E64im), op=ALU.mult), reads=[Fre[1], E64im[1], Ere[1]], writes=[Et1[1]])
                S.add("pool", lambda e, Fb=Fb, Eb=Eb: e.tensor_tensor(out=Et2[0][:], in0=Fb(Fim), in1=Eb(E64re), op=ALU.mult), reads=[Fim[1], E64re[1], Ere[1]], writes=[Et2[1]])
                S.add("pool", lambda e, Eo_im=Eo_im: e.tensor_tensor(out=Eo_im, in0=Et1[0][:], in1=Et2[0][:], op=ALU.add), reads=[Et1[1], Et2[1]], writes=[Eim[1]])
            S.add("pool", lambda e, dp=dp: e.tensor_copy(out=Rt[0][:], in_=mag[0][:, dp:dp + 1].to_broadcast([128, SEG])), reads=[mag[1]], writes=[Rt[1]])
            Ef_re = Ere[0][:].rearrange("p k t -> p (k t)")
            Ef_im = Eim[0][:].rearrange("p k t -> p (k t)")
            chunks = [(c0, min(512, SEG - c0)) for c0 in range(0, SEG, 512)]

            def tte(eng, out, a_, bap, bbuf, op):
                S.add(eng, lambda e: e.tensor_tensor(out=out[0][:], in0=a_[0][:], in1=bap, op=op), reads=[a_[1], bbuf], writes=[out[1]])

            def stage_A(cx):
                W = cx["W"]; u_, bu_ = cx["u"]
                if cx["it"] == 0:
                    for pf_ in range(UPF):
                        issue_u(pf_)
                if cx["it"] + UPF < len(iters):
                    issue_u(cx["it"] + UPF)
                for ci, (dst, bdst) in enumerate((W["xre"], W["xim"])):
                    for (c0, n) in chunks:
                        pt, bpt = k.bank()
                        k.mm(pt[:, :n], bpt, [(BbT[ci][0][:, dp, :], u_[:, c0:c0 + n])], reads=[BbT[ci][1], bu_])
                        S.add("act", lambda e, pt=pt, dst=dst, c0=c0, n=n: e.copy(out=dst[:, c0:c0 + n], in_=pt[:, :n]), reads=[bpt], writes=[bdst])
                Er, Ei = cx["Er"], cx["Ei"]
                xre, xim, A1, A2, T1 = W["xre"], W["xim"], W["A1"], W["A2"], W["T1"]
                tte("dve", A1, xre, Er, Ere[1], ALU.mult)
                tte("dve", T1, xim, Ei, Eim[1], ALU.mult)
                tt(A1, A1, T1, ALU.add)
                tte("dve", A2, xim, Er, Ere[1], ALU.mult)
                tte("dve", T1, xre, Ei, Eim[1], ALU.mult)
                tt(A2, A2, T1, ALU.subtract)
                pcarry = cx["pcarry"]; carry = cx["carry"]
                for ci, (src, dst) in enumerate(((A1, xre), (A2, xim))):
                    if cx["sgi"] == 0:
                        S.add("dve", lambda e, src=src, dst=dst: e.tensor_tensor_scan(out=dst[0][:], data0=Rt[0][:], data1=src[0][:], initial=0.0,
                                                                                     op0=ALU.mult, op1=ALU.add), reads=[Rt[1], src[1]], writes=[dst[1]])
                    else:
                        S.add("dve", lambda e, src=src, dst=dst, ci=ci, pcarry=pcarry: e.tensor_tensor_scan(out=dst[0][:], data0=Rt[0][:], data1=src[0][:],
                                                                                            initial=pcarry[0][:, ci:ci + 1], op0=ALU.mult, op1=ALU.add),
                              reads=[Rt[1], src[1], pcarry[1]], writes=[dst[1]])
                if cx["sgi"] < SEQ_T // SEG - 1:
                    S.add("act", lambda e, carry=carry, xre=xre: e.copy(out=carry[0][:, 0:1], in_=xre[0][:, SEG - 1:SEG]), reads=[xre[1]], writes=[carry[1]])
                    S.add("act", lambda e, carry=carry, xim=xim: e.copy(out=carry[0][:, 1:2], in_=xim[0][:, SEG - 1:SEG]), reads=[xim[1]], writes=[carry[1]])

            def stage_B(cx):
                W = cx["W"]; sre, sim = cx["S"]; y_, by_ = cx["y"]
                Er, Ei = cx["Er"], cx["Ei"]
                xre, xim, A1, T1 = W["xre"], W["xim"], W["A1"], W["T1"]
                tte("dve", T1, xre, Er, Ere[1], ALU.mult)
                tte("dve", A1, xim, Ei, Eim[1], ALU.mult)
                tt(sre, T1, A1, ALU.subtract)
                tte("dve", T1, xre, Ei, Eim[1], ALU.mult)
                tte("dve", A1, xim, Er, Ere[1], ALU.mult)
                tt(sim, T1, A1, ALU.add)
                for (c0, n) in chunks:
                    pt, bpt = k.bank()
                    k.mm(pt[0:32, :n], bpt, [(CTre[0][:, dp, :], sre[0][:, c0:c0 + n]), (CTim[0][:, dp, :], sim[0][:, c0:c0 + n])],
                         reads=[CTre[1], CTim[1], sre[1], sim[1]])
                    S.add("act", lambda e, pt=pt, y_=y_, c0=c0, n=n: e.copy(out=y_[:, c0:c0 + n], in_=pt[0:32, :n]), reads=[bpt], writes=[by_])
                k.store(yTd[d][pr * 32:(pr + 1) * 32, cx["b"] * SEQ_T + cx["t0"]:cx["b"] * SEQ_T + cx["t0"] + SEG], y_[:], by_)

            cxs = []
            for b in range(2):
                for sgi in range(SEQ_T // SEG):
                    t0 = sgi * SEG
                    cxs.append(dict(it=it, b=b, sgi=sgi, t0=t0, W=Wsets[it % 2], S=Ssets[it % 2], carry=carries[it % 2], pcarry=carries[(it - 1) % 2],
                                    u=usb[it % (UPF + 1)], y=yt[it % 2], Er=Ef_re[:, t0:t0 + SEG], Ei=Ef_im[:, t0:t0 + SEG]))
                    it += 1
            stage_A(cxs[0])
            for i_ in range(len(cxs)):
                if i_ + 1 < len(cxs):
                    stage_A(cxs[i_ + 1])
                stage_B(cxs[i_])
    return k.finish()


def s5_param_layout(inp, c):
    G0 = 16 * c
    out = {}
    for n, key in (("are", "s5_a_re"), ("aim", "s5_a_im")):
        a = inp[key][0][:, G0:G0 + 16, :]
        out[n] = np.ascontiguousarray(a.reshape(2, 8, 2, 64).transpose(2, 3, 0, 1).reshape(128, 2, 8))
    ls = inp["s5_log_step"][0][:, G0:G0 + 16]
    ls = np.broadcast_to(ls.reshape(2, 8, 2, 1), (2, 8, 2, 64))
    out["lst"] = np.ascontiguousarray(ls.transpose(2, 3, 0, 1).reshape(128, 2, 8))
    for n, key in (("bre", "s5_b_re"), ("bim", "s5_b_im")):
        a = inp[key][0][:, G0:G0 + 16]
        out[n] = np.ascontiguousarray(a.reshape(2, 8, 2, 64, 16).transpose(2, 3, 0, 1, 4).reshape(128, 2, 8, 16))
    for n, key in (("cre", "s5_c_re"), ("cim", "s5_c_im")):
        a = inp[key][0][:, G0:G0 + 16]
        out[n] = np.ascontiguousarray(a.reshape(2, 8, 2, 16, 64).transpose(2, 4, 0, 1, 3).reshape(128, 2, 8, 16))
    return out


def launch_L2(inp, u_lat, u_ctx):
    nc = build_L2()
    seq_f = np.concatenate([u_ctx, u_lat], 1)
    seq_r = np.concatenate([u_ctx[:, ::-1], u_lat[:, ::-1]], 1)
    ident = np.eye(128, dtype=np.float32)
    maps = []
    for c in range(NCORES):
        m = s5_param_layout(inp, c)
        m["uT_f"] = np.ascontiguousarray(seq_f[:, :, c * 256:(c + 1) * 256].transpose(2, 0, 1).reshape(256, 2 * SEQ_T))
        m["uT_r"] = np.ascontiguousarray(seq_r[:, :, c * 256:(c + 1) * 256].transpose(2, 0, 1).reshape(256, 2 * SEQ_T))
        m["ident"] = ident
        maps.append(m)
    res = run(nc, maps)
    y_f = np.zeros((2, 4096, D), np.float32)
    y_r = np.zeros((2, 4096, D), np.float32)
    for c in range(NCORES):
        yf = res[c]["yT_f"].reshape(256, 2, SEQ_T)[:, :, 256:]
        yr = res[c]["yT_r"].reshape(256, 2, SEQ_T)[:, :, 256:][:, :, ::-1]
        y_f[:, :, c * 256:(c + 1) * 256] = yf.transpose(1, 2, 0)
        y_r[:, :, c * 256:(c + 1) * 256] = yr.transpose(1, 2, 0)
    return y_f, y_r


def V_(k, name, shape, dt=F32):
    return (k.sb(name, shape, dt), Buf(name))


def gelu_tanh(k, out, bout, xin, bx, tmpa, tmpb, rows=128, accum=None, eng2="pool"):
    S = k.S
    ta, bta = tmpa
    tb, btb = tmpb
    S.add("act", lambda e: e.activation(out=ta, in_=xin, func=AF.Square), reads=[bx], writes=[bta])
    S.add("dve", lambda e: e.tensor_scalar(out=ta, in0=ta, scalar1=0.044715, scalar2=1.0, op0=ALU.mult, op1=ALU.add), reads=[bta], writes=[bta])
    S.add("dve", lambda e: e.tensor_tensor(out=ta, in0=ta, in1=xin, op=ALU.mult), reads=[bta, bx], writes=[bta])
    S.add("act", lambda e: e.activation(out=tb, in_=ta, func=AF.Sigmoid, scale=1.5957691216057308), reads=[bta], writes=[btb])
    if accum is None:
        S.add("dve", lambda e: e.tensor_tensor(out=out, in0=xin, in1=tb, op=ALU.mult), reads=[bx, btb], writes=[bout])
    else:
        acc_ap, bacc = accum
        S.add("dve", lambda e: e.scalar_tensor_tensor(out=out, in0=xin, scalar=1.0, in1=tb, op0=ALU.mult, op1=ALU.mult, accum_out=acc_ap),
              reads=[bx, btb], writes=[bout, bacc])


def transpose16(k, src, bsrc, dst, bdst, ident, bident, rows=128):
    S = k.S
    for half in range(2):
        pt, bpt = k.bank()
        ptb = pt[:].bitcast(BF16)

        def fn(e, half=half, ptb=ptb):
            ins = None
            for j in range(8):
                kc = half * 8 + j
                ins = e.transpose(ptb[:, j * 128:j * 128 + rows], src[:rows, kc * 128:(kc + 1) * 128], ident[:rows, :rows])
            return ins
        S.add("pe", fn, reads=[bsrc, bident], writes=[bpt])
        S.add("act", lambda e, half=half, ptb=ptb: e.copy(out=dst[:, half * 8:(half + 1) * 8, :rows],
                                                         in_=ptb.rearrange("p (j t) -> p j t", t=128)[:, :, :rows]), reads=[bpt], writes=[bdst])


def load_w16(k, w_sb, bw, w_d, nk=16, eng="pool"):
    k.load(w_sb[:, 0:nk, :], w_d.rearrange("(kc p) n -> p kc n", p=128), bw, eng=eng)


def build_L3a(k=None):
    own = k is None
    k = k or KB()
    S = k.S
    yf_d = k.din("yf", [1024, D]); yr_d = k.din("yr", [1024, D]); u_d = k.din("u", [1024, D])
    dB_d = k.din("dB", [128, D]); bg_d = k.din("bgluB", [128, D])
    wg_d = k.din("w_glu", [D, D]); wo_d = k.din("w_out", [D, D]); ident_d = k.din("ident", [128, 128])
    ml_d = k.dout("ml", [1024, D])
    ident, bident = V_(k, "ident_sb", [128, 128], BF16); k.load(ident[:], ident_d, bident, eng="pool")
    dB, bdB = V_(k, "dB_sb", [128, D]); k.load(dB[:], dB_d, bdB)
    bg, bbg = V_(k, "bg_sb", [128, D]); k.load(bg[:], bg_d, bbg)
    wg, bwg = V_(k, "wg", [128, 16, D], BF16); load_w16(k, wg, bwg, wg_d)
    wo, bwo = V_(k, "wo", [128, 16, D], BF16); load_w16(k, wo, bwo, wo_d)
    Y = [V_(k, f"Y{i}", [128, D]) for i in range(2)]
    T, bT = V_(k, "T", [128, D])
    T2, bT2 = V_(k, "T2", [128, D])
    g, bg_ = V_(k, "g", [128, D], BF16)
    gT, bgT = V_(k, "gT", [128, 16, 128], BF16)
    zs, bzs = V_(k, "zs", [128, D])
    v, bv = V_(k, "v", [128, D], BF16)
    vT, bvT = V_(k, "vT", [128, 16, 128], BF16)
    mlo, bmlo = zs, bzs
    for i in range(8):
        y, by = Y[i % 2]
        sl = slice(i * 128, (i + 1) * 128)
        k.load(y[:], yf_d[sl, :], by, eng="sp")
        k.load(T[:], yr_d[sl, :], bT, eng="sp")
        S.add("dve", lambda e, y=y: e.tensor_tensor(out=y[:], in0=y[:], in1=T[:], op=ALU.add), reads=[by, bT], writes=[by])
        k.load(T[:], u_d[sl, :], bT, eng="sp")
        S.add("pool", lambda e: e.tensor_tensor(out=T[:], in0=T[:], in1=dB[:], op=ALU.mult), reads=[bT, bdB], writes=[bT])
        S.add("dve", lambda e, y=y: e.tensor_tensor(out=y[:], in0=y[:], in1=T[:], op=ALU.add), reads=[by, bT], writes=[by])
        gelu_tanh(k, g[:], bg_, y[:], by, (T[:], bT), (T2[:], bT2))
        transpose16(k, g, bg_, gT, bgT, ident, bident)
        for nb in range(4):
            pt, bpt = k.bank()
            cs = slice(nb * 512, (nb + 1) * 512)
            k.mm(pt[:], bpt, [(gT[:, kc, :], wg[:, kc, cs]) for kc in range(16)], reads=[bgT, bwg])
            S.add("dve", lambda e, pt=pt, cs=cs: e.tensor_tensor(out=zs[:, cs], in0=pt[:], in1=bg[:, cs], op=ALU.add), reads=[bpt, bbg], writes=[bzs])
        S.add("act", lambda e: e.activation(out=zs[:], in_=zs[:], func=AF.Sigmoid), reads=[bzs], writes=[bzs])
        S.add("dve", lambda e: e.tensor_tensor(out=v[:], in0=g[:], in1=zs[:], op=ALU.mult), reads=[bg_, bzs], writes=[bv])
        transpose16(k, v, bv, vT, bvT, ident, bident)
        for nb in range(4):
            pt, bpt = k.bank()
            cs = slice(nb * 512, (nb + 1) * 512)
            k.mm(pt[:], bpt, [(vT[:, kc, :], wo[:, kc, cs]) for kc in range(16)], reads=[bvT, bwo])
            S.add("act", lambda e, pt=pt, cs=cs: e.copy(out=mlo[:, cs], in_=pt[:]), reads=[bpt], writes=[bmlo])
        k.store(ml_d[sl, :], mlo[:], bmlo)
    return k.done(own)


def launch_L3a(inp, y_f, y_r, u_lat):
    nc = build_L3a()
    ident = np.eye(128, dtype=np.float32)
    maps = []
    for c in range(NCORES):
        b, q = c // 4, c % 4
        sl = slice(q * 1024, (q + 1) * 1024)
        maps.append(dict(yf=np.ascontiguousarray(y_f[b, sl]), yr=np.ascontiguousarray(y_r[b, sl]), u=np.ascontiguousarray(u_lat[b, sl]),
                         dB=bc(inp["s5_d"][0]), bgluB=bc(inp["s5_b_glu"][0]), w_glu=inp["s5_w_glu"][0], w_out=inp["s5_w_out"][0], ident=ident))
    res = run(nc, maps)
    ml = np.zeros((2, 4096, D), np.float32)
    for c in range(NCORES):
        ml[c // 4, (c % 4) * 1024:(c % 4 + 1) * 1024] = res[c]["ml"]
    return ml


def build_resnorm(k=None):
    own = k is None
    k = k or KB()
    S = k.S
    x_d = k.din("x", [1024, D]); ml_d = k.din("ml", [1024, D]); ccT_d = k.din("ccT", [128, 16, 1])
    wmod_d = k.din("wmod", [D, 3 * D]); bmodB_d = k.din("bmodB", [128, 3 * D]); gF_d = k.din("gF", [128, D])
    x1_d = k.dout("x1", [1024, D]); hl_d = k.dout("hl", [1024, D])
    ccT, bcc = V_(k, "ccT_sb", [128, 16, 1]); k.load(ccT[:], ccT_d, bcc)
    gF, bgF = V_(k, "gF_sb", [128, D]); k.load(gF[:], gF_d, bgF)
    M = [V_(k, f"M{j}", [128, D]) for j in range(3)]
    outs = [lambda c0, n: (M[c0 // D][0][:, c0 % D:c0 % D + n], M[c0 // D][1])]
    build_mod(k, ccT, bcc, wmod_d, bmodB_d, 3 * D, 1, outs, NW=512, nbuf=3)
    A, bA = M[2]
    B, bB = M[1]
    S.add("dve", lambda e: e.scalar_tensor_tensor(out=A[:], in0=A[:], scalar=1.0, in1=gF[:], op0=ALU.add, op1=ALU.mult), reads=[bA, bgF], writes=[bA])
    X = [V_(k, f"X{i}", [128, D]) for i in range(2)]
    Ml = [V_(k, f"Ml{i}", [128, D]) for i in range(2)]
    H = [V_(k, f"H{i}", [128, D]) for i in range(2)]
    junk, bjunk = V_(k, "junk", [128, D], BF16)
    SS = [V_(k, f"ss{i}", [128, 1]) for i in range(2)]
    for i in range(8):
        sl = slice(i * 128, (i + 1) * 128)
        x, bx = X[i % 2]; m, bm = Ml[i % 2]; h, bh = H[i % 2]; ss, bss = SS[i % 2]
        k.load(x[:], x_d[sl, :], bx, eng="sp")
        k.load(m[:], ml_d[sl, :], bm, eng="pool")
        S.add("pool", lambda e, m=m: e.tensor_tensor(out=m[:], in0=m[:], in1=M[0][0][:], op=ALU.mult), reads=[bm, M[0][1]], writes=[bm])
        S.add("dve", lambda e, x=x, m=m: e.tensor_tensor(out=x[:], in0=x[:], in1=m[:], op=ALU.add), reads=[bx, bm], writes=[bx])
        k.store(x1_d[sl, :], x[:], bx)
        S.add("act", lambda e, x=x, ss=ss: e.activation(out=junk[:], in_=x[:], func=AF.Square, accum_out=ss[:, 0:1]), reads=[bx], writes=[bjunk, bss])
        S.add("dve", lambda e, ss=ss: e.tensor_scalar(out=ss[:], in0=ss[:], scalar1=1.0 / D, scalar2=1e-6, op0=ALU.mult, op1=ALU.add), reads=[bss], writes=[bss])
        S.add("act", lambda e, ss=ss: e.sqrt(out=ss[:], in_=ss[:]), reads=[bss], writes=[bss])
        S.add("dve", lambda e, ss=ss: e.reciprocal(out=ss[:], in_=ss[:]), reads=[bss], writes=[bss])
        S.add("dve", lambda e, x=x, h=h, ss=ss: e.scalar_tensor_tensor(out=h[:], in0=x[:], scalar=ss[:, 0:1], in1=A[:], op0=ALU.mult, op1=ALU.mult),
              reads=[bx, bss, bA], writes=[bh])
        S.add("pool", lambda e, h=h: e.tensor_tensor(out=h[:], in0=h[:], in1=B[:], op=ALU.add), reads=[bh, bB], writes=[bh])
        k.store(hl_d[sl, :], h[:], bh)
    return k.done(own)


def launch_resnorm(inp, x, ml, layer, gF, mod_cols):
    nc = build_resnorm()
    cols = np.concatenate([np.arange(j * D, (j + 1) * D) for j in mod_cols])
    wmod = np.ascontiguousarray(inp["w_mod"][layer][:, cols])
    bmodB = bc(inp["b_mod"][layer][cols])
    maps = []
    for c in range(NCORES):
        b, q = c // 4, c % 4
        sl = slice(q * 1024, (q + 1) * 1024)
        ccT = np.ascontiguousarray(inp["c"][b].reshape(1, 16, 128).transpose(2, 1, 0))
        maps.append(dict(x=np.ascontiguousarray(x[b, sl]), ml=np.ascontiguousarray(ml[b, sl]), ccT=ccT, wmod=wmod, bmodB=bmodB, gF=bc(gF)))
    res = run(nc, maps)
    x1 = np.zeros((2, 4096, D), np.float32); hl = np.zeros((2, 4096, D), np.float32)
    for c in range(NCORES):
        x1[c // 4, (c % 4) * 1024:(c % 4 + 1) * 1024] = res[c]["x1"]
        hl[c // 4, (c % 4) * 1024:(c % 4 + 1) * 1024] = res[c]["hl"]
    return x1, hl


FF = 5632


def build_ffn(final, k=None):
    own = k is None
    k = k or KB()
    S = k.S
    hl_d = k.din("hlT", [D, 1152]); x_d = k.din("x", [1024, D]); ccT_d = k.din("ccT", [128, 16, 1])
    wmod_d = k.din("wmod", [D, D]); bmodB_d = k.din("bmodB", [128, D])
    wup_d = k.din("w_up", [D, 2 * FF]); cw_d = k.din("cw", [128, 88, 9]); cb_d = k.din("cb", [128, 88])
    wdn_d = k.din("w_down", [FF, D])
    xo_d = k.dout("xo", [1024, D])
    if final:
        gfin_d = k.din("gfin", [128, D])
        xs_d = k.scratch(k.prefix + "xs_scratch", [1024, D])
        bxs = Buf("xs")
    gate, bgate = V_(k, "gate", [128, D])
    with k.stage(k.prefix + "m_", io=k.io):
        ccT, bcc = V_(k, "ccT_sb", [128, 16, 1]); k.load(ccT[:], ccT_d, bcc)
        build_mod(k, ccT, bcc, wmod_d, bmodB_d, D, 1, [lambda c0, n: (gate[:, c0:c0 + n], bgate)], NW=256, nbuf=3)
    cw, bcw = V_(k, "cw_sb", [128, 88, 9]); k.load(cw[:], cw_d, bcw)
    cb, bcb = V_(k, "cb_sb", [128, 88]); k.load(cb[:], cb_d, bcb)
    hlT, bhlT = V_(k, "hlT_sb", [128, 16, 1152], BF16)
    k.load(hlT[:], hl_d.rearrange("(kc p) n -> p kc n", p=128), bhlT, eng="pool")
    aT, baT = V_(k, "aT", [128, 44, 512], BF16)
    NWD = 256
    wds = [V_(k, f"wd{i}", [128, 44, NWD], BF16) for i in range(2)]
    PF = 2
    wu = [(k.sb(f"wu{i}", [128, 16, 256], BF16), Buf(), Buf()) for i in range(PF + 1)]

    def issue_wu(n_it):
        j_ = n_it % 44
        w_, bg__, bv__ = wu[n_it % (PF + 1)]
        k.load(w_[:, :, 0:128], wup_d[:, j_ * 128:(j_ + 1) * 128].rearrange("(kc p) n -> p kc n", p=128), bg__, eng="pool")
        k.load(w_[:, :, 128:256], wup_d[:, FF + j_ * 128:FF + (j_ + 1) * 128].rearrange("(kc p) n -> p kc n", p=128), bv__, eng="pool")
    zp = [[V_(k, f"zp{i}{c}", [128, 10, 66]) for c in range(2)] for i in range(2)]
    for i in range(2):
        for c in range(2):
            S.add("pool", lambda e, t=zp[i][c][0]: e.memset(t[:], 0.0), writes=[zp[i][c][1]])
    acc = [[V_(k, f"acc{i}{c}", [128, 8, 64]) for c in range(2)] for i in range(2)]
    sgt = [V_(k, f"sg{i}", [128, 8, 64]) for i in range(2)]
    ptmps = [V_(k, f"ptmp{i}", [128, 8, 64]) for i in range(4)]
    tcnt = [0]
    xc = [V_(k, f"xc{i}", [128, NWD]) for i in range(2)]
    Tt = [V_(k, f"Tt{i}", [128, NWD]) for i in range(2)]
    it = 0
    for h in range(2):
        e0 = h * 8 * 64
        for j in range(44):
            if it == 0:
                for pf in range(PF):
                    issue_wu(pf)
            if it + PF < 88:
                issue_wu(it + PF)
            w, bwg_, bwv_ = wu[it % (PF + 1)]
            for c in range(2):
                z, bz = zp[it % 2][c]
                a_, ba_ = acc[it % 2][c]
                ch = j + 44 * c
                for (c0, n, r0) in ((0, 512, 0), (512, 128, 8)):
                    pt, bpt = k.bank()
                    k.mm(pt[:, :n], bpt, [(w[:, kc, c * 128:(c + 1) * 128], hlT[:, kc, e0 + c0:e0 + c0 + n]) for kc in range(16)],
                         reads=[(bwg_, bwv_)[c], bhlT])
                    S.add("act", lambda e, pt=pt, z=z, n=n, r0=r0: e.copy(out=z[:, r0:r0 + n // 64, 1:65],
                                                                          in_=pt[:, :n].rearrange("p (r c) -> p r c", c=64)), reads=[bpt], writes=[bz])
                if c == 0:
                    S.add("dve", lambda e, z=z, a_=a_, ch=ch: e.tensor_scalar(out=a_[:], in0=z[:, 0:8, 0:64], scalar1=cw[:, ch, 0:1], scalar2=cb[:, ch:ch + 1],
                                                                            op0=ALU.mult, op1=ALU.add), reads=[bz, bcw, bcb], writes=[ba_])
                    for t in range(1, 9):
                        di, dj = divmod(t, 3)
                        S.add("dve", lambda e, z=z, a_=a_, ch=ch, t=t, di=di, dj=dj: e.scalar_tensor_tensor(
                            out=a_[:], in0=z[:, di:di + 8, dj:dj + 64], scalar=cw[:, ch, t:t + 1], in1=a_[:], op0=ALU.mult, op1=ALU.add),
                            reads=[bz, bcw, ba_], writes=[ba_])
                else:
                    S.add("act", lambda e, z=z, a_=a_, ch=ch: e.activation(out=a_[:], in_=z[:, 0:8, 0:64], func=AF.Identity, bias=cb[:, ch:ch + 1],
                                                                         scale=cw[:, ch, 0:1]), reads=[bz, bcw, bcb], writes=[ba_])
                    for t in range(1, 9):
                        di, dj = divmod(t, 3)
                        pt_, bpt_ = ptmps[tcnt[0] % 4]
                        tcnt[0] += 1
                        S.add("act", lambda e, z=z, ch=ch, t=t, di=di, dj=dj, pt_=pt_: e.activation(out=pt_[:], in_=z[:, di:di + 8, dj:dj + 64], func=AF.Copy,
                                                                                                 scale=cw[:, ch, t:t + 1]), reads=[bz, bcw], writes=[bpt_])
                        S.add("dve", lambda e, a_=a_, pt_=pt_: e.tensor_tensor(out=a_[:], in0=a_[:], in1=pt_[:], op=ALU.add), reads=[ba_, bpt_], writes=[ba_])
            ag, bag = acc[it % 2][0]
            av, bav = acc[it % 2][1]
            sg, bsg = sgt[it % 2]
            S.add("act", lambda e, ag=ag, sg=sg: e.activation(out=sg[:], in_=ag[:], func=AF.Silu), reads=[bag], writes=[bsg])
            S.add("dve", lambda e, sg=sg, av=av, j=j: e.tensor_tensor(out=aT[:, j, :].rearrange("p (r c) -> p r c", c=64), in0=sg[:], in1=av[:], op=ALU.mult),
                  reads=[bsg, bav], writes=[baT])
            it += 1
        for nb in range(D // NWD):
            cs = slice(nb * NWD, (nb + 1) * NWD)
            wd, bwd = wds[nb % 2]
            k.load(wd[:], wdn_d[:, cs].rearrange("(j p) n -> p j n", p=128), bwd, eng="pool")
            for tt_ in range(4):
                gi = h * 4 + tt_
                rs = slice(gi * 128, (gi + 1) * 128)
                x, bx = xc[(nb * 4 + tt_) % 2]
                T, bT = Tt[(nb * 4 + tt_) % 2]
                k.load(x[:], x_d[rs, cs], bx, eng="sp")
                pt, bpt = k.bank()
                k.mm(pt[:, :NWD], bpt, [(aT[:, j, tt_ * 128:(tt_ + 1) * 128], wd[:, j, :]) for j in range(44)], reads=[baT, bwd])
                S.add("dve", lambda e, pt=pt, T=T, cs=cs: e.tensor_tensor(out=T[:], in0=pt[:, :NWD], in1=gate[:, cs], op=ALU.mult), reads=[bpt, bgate], writes=[bT])
                S.add("pool", lambda e, T=T, x=x: e.tensor_tensor(out=T[:], in0=T[:], in1=x[:], op=ALU.add), reads=[bT, bx], writes=[bT])
                if final:
                    S.add("sp", lambda e, T=T, rs=rs, cs=cs: e.dma_start(out=xs_d[rs, cs], in_=T[:]), reads=[bT], writes=[bxs], dma=True)
                else:
                    k.store(xo_d[rs, cs], T[:], bT)
    if final:
        gfin, bgfin = V_(k, "gfin_sb", [128, D]); k.load(gfin[:], gfin_d, bgfin)
        XR = [(aT[:].rearrange("p j n -> p (j n)").bitcast(F32)[:, 0:D], baT)] * 2
        bjunk = bhlT
        SS = [V_(k, f"ss{i}", [128, 1]) for i in range(2)]
        for i in range(8):
            rs = slice(i * 128, (i + 1) * 128)
            x, bx = XR[i % 2]; ss, bss = SS[i % 2]
            k.load(x[:], xs_d[rs, :], bx, eng="sp", reads=[bxs])
            S.add("act", lambda e, x=x, ss=ss: e.activation(out=hlT[:, 0:2, 0:1024], in_=x[:].rearrange("p (a b) -> p a b", a=2), func=AF.Square,
                                                           accum_out=ss[:, 0:1]), reads=[bx], writes=[bjunk, bss])
            S.add("dve", lambda e, ss=ss: e.tensor_scalar(out=ss[:], in0=ss[:], scalar1=1.0 / D, scalar2=1e-6, op0=ALU.mult, op1=ALU.add), reads=[bss], writes=[bss])
            S.add("act", lambda e, ss=ss: e.sqrt(out=ss[:], in_=ss[:]), reads=[bss], writes=[bss])
            S.add("dve", lambda e, ss=ss: e.reciprocal(out=ss[:], in_=ss[:]), reads=[bss], writes=[bss])
            S.add("dve", lambda e, x=x, ss=ss: e.scalar_tensor_tensor(out=x[:], in0=x[:], scalar=ss[:, 0:1], in1=gfin[:], op0=ALU.mult, op1=ALU.mult),
                  reads=[bx, bss, bgfin], writes=[bx])
            k.store(xo_d[rs, :], x[:], bx)
    return k.done(own)


def launch_ffn(inp, x, hl, layer, final):
    nc = build_ffn(final)
    cols = np.arange(5 * D, 6 * D)
    wmod = np.ascontiguousarray(inp["w_mod"][layer][:, cols]); bmodB = bc(inp["b_mod"][layer][cols])
    cw = np.ascontiguousarray(inp["ffn_conv_w"][layer].reshape(9, 88, 128).transpose(2, 1, 0))
    cb = np.ascontiguousarray(inp["ffn_conv_b"][layer].reshape(88, 128).T)
    maps = []
    for c in range(NCORES):
        b, q = c // 4, c % 4
        ext = np.zeros((1152, D), np.float32)
        lo, hi = q * 1024 - 64, q * 1024 + 1024 + 64
        slo, shi = max(lo, 0), min(hi, 4096)
        ext[slo - lo:shi - lo] = hl[b, slo:shi]
        m = dict(hlT=np.ascontiguousarray(ext.T), x=np.ascontiguousarray(x[b, q * 1024:(q + 1) * 1024]),
                 ccT=np.ascontiguousarray(inp["c"][b].reshape(1, 16, 128).transpose(2, 1, 0)), wmod=wmod, bmodB=bmodB,
                 w_up=inp["ffn_w_up"][layer], cw=cw, cb=cb, w_down=inp["ffn_w_down"][layer])
        if final:
            m["gfin"] = bc(inp["final_norm_g"])
        maps.append(m)
    res = run(nc, maps)
    xo = np.zeros((2, 4096, D), np.float32)
    for c in range(NCORES):
        xo[c // 4, (c % 4) * 1024:(c % 4 + 1) * 1024] = res[c]["xo"]
    return xo


def rms_mod(k, x, bx, A, bA, B, bB, out, bout, junk, ss, t1):
    S = k.S
    S.add("act", lambda e: e.activation(out=junk[0][:], in_=x, func=AF.Square, accum_out=ss[0][:, 0:1]), reads=[bx], writes=[junk[1], ss[1]])
    S.add("dve", lambda e: e.tensor_scalar(out=ss[0][:], in0=ss[0][:], scalar1=1.0 / D, scalar2=1e-6, op0=ALU.mult, op1=ALU.add), reads=[ss[1]], writes=[ss[1]])
    S.add("act", lambda e: e.sqrt(out=ss[0][:], in_=ss[0][:]), reads=[ss[1]], writes=[ss[1]])
    S.add("dve", lambda e: e.reciprocal(out=ss[0][:], in_=ss[0][:]), reads=[ss[1]], writes=[ss[1]])
    S.add("dve", lambda e: e.scalar_tensor_tensor(out=t1[0][:], in0=x, scalar=ss[0][:, 0:1], in1=A, op0=ALU.mult, op1=ALU.mult),
          reads=[bx, ss[1], bA], writes=[t1[1]])
    S.add("dve", lambda e: e.tensor_tensor(out=out, in0=t1[0][:], in1=B, op=ALU.add), reads=[t1[1], bB], writes=[bout])


def build_sgu(k=None):
    own = k is None
    k = k or KB()
    S = k.S
    x_d = k.din("x", [1024, D]); ccT_d = k.din("ccT", [128, 16, 1])
    wmA_d = k.din("wmodA", [D, 2 * D]); bmA_d = k.din("bmodBA", [128, 2 * D]); gM_d = k.din("gMix", [128, D])
    wmB_d = k.din("wmodB", [D, 3 * D]); bmB_d = k.din("bmodBB", [128, 3 * D]); gF_d = k.din("gF", [128, D])
    win_d = k.din("w_in", [D, 8192]); lng_d = k.din("lngT", [128, 32]); lnb_d = k.din("lnbT", [128, 32])
    wsT_d = k.din("wsT", [128, 16, 128]); bsB_d = k.din("bsB", [128, 16, 128]); wo_d = k.din("w_out", [4096, D])
    ident_d = k.din("ident", [128, 128]); ones_d = k.din("ones", [128, 128])
    x3_d = k.dout("x3", [1024, D]); hl_d = k.dout("hl", [1024, D])
    bx3 = Buf("x3d")
    ident, bident = V_(k, "ident_sb", [128, 128], BF16); k.load(ident[:], ident_d, bident, eng="pool")
    ones, bones = V_(k, "ones_sb", [128, 128], BF16); k.load(ones[:], ones_d, bones, eng="pool")
    ccT, bcc = V_(k, "ccT_sb", [128, 16, 1]); k.load(ccT[:], ccT_d, bcc)
    M0 = V_(k, "M0", [128, D]); M1 = V_(k, "M1", [128, D]); G2 = V_(k, "G2", [128, D])
    t1 = V_(k, "t1", [128, D]); junk = V_(k, "junk", [128, D], BF16); ss = V_(k, "ss", [128, 1])
    k.load(t1[0][:], gM_d, t1[1])
    sc = k.sb("sc", [128, 16, 1]); bsc = Buf()
    scb = k.sb("scb", [128, 16, 1, 128]); bscb = Buf()
    S.add("act", lambda e: e.activation(out=sc[:], in_=ccT[:], func=AF.Silu), reads=[bcc], writes=[bsc])
    S.add("dve", lambda e: e.tensor_copy(out=scb[:], in_=sc[:].unsqueeze(3).to_broadcast([128, 16, 1, 128])), reads=[bsc], writes=[bscb])
    NW = 128
    wst = [(k.sb(f"wst{i}", [128, 16, NW]), Buf()) for i in range(2)]
    bmt = [(k.sb(f"bmt{i}", [128, NW]), Buf()) for i in range(2)]

    def mod(wmod_d, bmodB_d, ncols, outf):
        for nb in range(ncols // NW):
            w, bw = wst[nb % 2]
            bm, bbm = bmt[nb % 2]
            k.load(w[:], wmod_d[:, nb * NW:(nb + 1) * NW].rearrange("(kc p) n -> p kc n", p=128), bw)
            k.load(bm[:], bmodB_d[:, nb * NW:(nb + 1) * NW], bbm)
            pt, bpt = k.bank()
            k.mm(pt[:, :NW], bpt, [(scb[:, kc, 0, :], w[:, kc, :]) for kc in range(16)], reads=[bscb, bw])
            dst, bdst = outf(nb * NW, NW)
            S.add("dve", lambda e, pt=pt, dst=dst, bm=bm: e.tensor_tensor(out=dst, in0=pt[:, :NW], in1=bm[:], op=ALU.add), reads=[bpt, bbm], writes=[bdst])
    MA = [M0, M1]
    mod(wmA_d, bmA_d, 2 * D, lambda c0, n: (MA[c0 // D][0][:, c0 % D:c0 % D + n], MA[c0 // D][1]))
    S.add("dve", lambda e: e.scalar_tensor_tensor(out=M1[0][:], in0=M1[0][:], scalar=1.0, in1=t1[0][:], op0=ALU.add, op1=ALU.mult),
          reads=[M1[1], t1[1]], writes=[M1[1]])
    mod(wmB_d[:, 0:D], bmB_d[:, 0:D], D, lambda c0, n: (G2[0][:, c0:c0 + n], G2[1]))
    lng = V_(k, "lng", [128, 32]); k.load(lng[0][:], lng_d, lng[1])
    lnb = V_(k, "lnb", [128, 32]); k.load(lnb[0][:], lnb_d, lnb[1])
    wsT = V_(k, "wsT_sb", [128, 16, 128], BF16); k.load(wsT[0][:], wsT_d, wsT[1], eng="pool")
    bsB = V_(k, "bsB_sb", [128, 16, 128]); k.load(bsB[0][:], bsB_d, bsB[1])
    rsB = V_(k, "rsB", [128, 16, 128])
    for g4 in range(4):
        pt, bpt = k.bank()
        k.mm(pt[:], bpt, [(ones[:], wsT[0][:, g4 * 4:(g4 + 1) * 4, :].rearrange("p g q -> p (g q)"))], reads=[bones, wsT[1]])
        S.add("act", lambda e, pt=pt, g4=g4: e.copy(out=rsB[0][:, g4 * 4:(g4 + 1) * 4, :].rearrange("p g q -> p (g q)"), in_=pt[:]), reads=[bpt], writes=[rsB[1]])
    hlT = V_(k, "hlT", [128, 16, 512], BF16)
    hlb = V_(k, "hlb", [128, D], BF16)
    vn = k.sb("vn", [128, 4, 4096], BF16)
    bvn = [Buf(f"vn{c}") for c in range(32)]
    X = V_(k, "X", [128, D])
    wv = V_(k, "wv", [128, 16, 512], BF16)
    wu = [V_(k, f"wu{i}", [128, 16, 128], BF16) for i in range(2)]
    wo = V_(k, "wo", [128, 32, 256], BF16)
    ga = V_(k, "ga", [128, 512]); gb = V_(k, "gb", [128, 512])
    sums = V_(k, "sums", [128, 4, 8]); sqs = V_(k, "sqs", [128, 4, 8])
    mean = V_(k, "mean", [128, 4]); var = V_(k, "var", [128, 4]); msq = V_(k, "msq", [128, 4])
    uTb = V_(k, "uTb", [128, 512]); svt = V_(k, "svt", [128, 4, 128]); bfull = V_(k, "bfull", [128, 128])
    xc = [V_(k, f"xc{i}", [128, 256]) for i in range(2)]
    Tt = [V_(k, f"Tt{i}", [128, 256]) for i in range(2)]
    for hf in range(2):
        for i in range(4):
            rs = slice(hf * 512 + i * 128, hf * 512 + (i + 1) * 128)
            k.load(X[0][:], x_d[rs, :], X[1], eng="sp")
            rms_mod(k, X[0][:], X[1], M1[0][:], M1[1], M0[0][:], M0[1], hlb[0][:], hlb[1], junk, ss, t1)
            transpose16(k, hlb[0], hlb[1], hlT[0][:, :, i * 128:(i + 1) * 128], hlT[1], ident, bident)
        for cbi in range(8):
            load_w16(k, wv[0], wv[1], win_d[:, 4096 + cbi * 512:4096 + (cbi + 1) * 512])
            for i in range(4):
                pt, bpt = k.bank()
                k.mm(pt[:], bpt, [(hlT[0][:, kc, i * 128:(i + 1) * 128], wv[0][:, kc, :]) for kc in range(16)], reads=[hlT[1], wv[1]])
                blks = bvn[cbi * 4:(cbi + 1) * 4]
                dst = vn[:, i, cbi * 512:(cbi + 1) * 512]
                S_ = k.S
                S_.add("act", lambda e, pt=pt: e.activation(out=ga[0][:], in_=pt[:], func=AF.Square), reads=[bpt], writes=[ga[1]])
                S_.add("dve", lambda e: e.tensor_scalar(out=ga[0][:], in0=ga[0][:], scalar1=0.044715, scalar2=1.0, op0=ALU.mult, op1=ALU.add), reads=[ga[1]], writes=[ga[1]])
                S_.add("dve", lambda e, pt=pt: e.tensor_tensor(out=ga[0][:], in0=ga[0][:], in1=pt[:], op=ALU.mult), reads=[ga[1], bpt], writes=[ga[1]])
                S_.add("act", lambda e: e.activation(out=gb[0][:], in_=ga[0][:], func=AF.Sigmoid, scale=1.5957691216057308), reads=[ga[1]], writes=[gb[1]])
                S_.add("dve", lambda e, pt=pt, dst=dst, i=i, cbi=cbi: e.scalar_tensor_tensor(out=dst, in0=pt[:], scalar=1.0, in1=gb[0][:], op0=ALU.mult, op1=ALU.mult,
                                                                                         accum_out=sums[0][:, i, cbi:cbi + 1]),
                       reads=[bpt, gb[1]], writes=blks + [sums[1]])
                S_.add("act", lambda e, dst=dst, i=i, cbi=cbi: e.activation(out=ga[0][:], in_=dst, func=AF.Square, accum_out=sqs[0][:, i, cbi:cbi + 1]),
                       reads=blks, writes=[ga[1], sqs[1]])
        S.add("dve", lambda e: e.tensor_reduce(out=mean[0][:], in_=sums[0][:], axis=mybir.AxisListType.X, op=ALU.add), reads=[sums[1]], writes=[mean[1]])
        S.add("dve", lambda e: e.tensor_reduce(out=var[0][:], in_=sqs[0][:], axis=mybir.AxisListType.X, op=ALU.add), reads=[sqs[1]], writes=[var[1]])
        S.add("dve", lambda e: e.tensor_single_scalar(out=mean[0][:], in_=mean[0][:], scalar=1.0 / 4096, op=ALU.mult), reads=[mean[1]], writes=[mean[1]])
        S.add("dve", lambda e: e.tensor_tensor(out=msq[0][:], in0=mean[0][:], in1=mean[0][:], op=ALU.mult), reads=[mean[1]], writes=[msq[1]])
        S.add("dve", lambda e: e.scalar_tensor_tensor(out=var[0][:], in0=var[0][:], scalar=1.0 / 4096, in1=msq[0][:], op0=ALU.mult, op1=ALU.subtract),
              reads=[var[1], msq[1]], writes=[var[1]])
        S.add("dve", lambda e: e.tensor_single_scalar(out=var[0][:], in_=var[0][:], scalar=1e-5, op=ALU.add), reads=[var[1]], writes=[var[1]])
        S.add("act", lambda e: e.sqrt(out=var[0][:], in_=var[0][:]), reads=[var[1]], writes=[var[1]])
        S.add("dve", lambda e: e.reciprocal(out=var[0][:], in_=var[0][:]), reads=[var[1]], writes=[var[1]])
        for i in range(4):
            S.add("dve", lambda e, i=i: e.tensor_scalar(out=vn[:, i, :], in0=vn[:, i, :], scalar1=mean[0][:, i:i + 1], scalar2=var[0][:, i:i + 1],
                                                       op0=ALU.subtract, op1=ALU.mult), reads=bvn + [mean[1], var[1]], writes=bvn)
        for cbk in range(32):
            g = cbk // 2
            w, bw = wu[cbk % 2]
            load_w16(k, w, bw, win_d[:, cbk * 128:(cbk + 1) * 128])
            pt, bpt = k.bank()
            k.mm(pt[:], bpt, [(w[:, kc, :], hlT[0][:, kc, :]) for kc in range(16)], reads=[bw, hlT[1]])
            gelu_tanh(k, uTb[0][:], uTb[1], pt[:], bpt, (ga[0][:], ga[1]), (gb[0][:], gb[1]))
            pt2, bpt2 = k.bank()

            def fn(e, pt2=pt2, cbk=cbk, g=g):
                ins = None
                for i in range(4):
                    ins = e.matmul(pt2[:, i * 128:(i + 1) * 128], vn[:, i, cbk * 128:(cbk + 1) * 128], wsT[0][:, g, :], start=True, stop=True)
                return ins
            S.add("pe", fn, reads=[bvn[cbk], wsT[1]], writes=[bpt2])
            S.add("dve", lambda e, cbk=cbk, g=g: e.scalar_tensor_tensor(out=bfull[0][:], in0=rsB[0][:, g, :], scalar=lnb[0][:, cbk:cbk + 1], in1=bsB[0][:, g, :],
                                                                      op0=ALU.mult, op1=ALU.add), reads=[rsB[1], lnb[1], bsB[1]], writes=[bfull[1]])
            S.add("dve", lambda e, pt2=pt2, cbk=cbk: e.scalar_tensor_tensor(out=svt[0][:], in0=pt2[:].rearrange("p (i q) -> p i q", q=128), scalar=lng[0][:, cbk:cbk + 1],
                                                                          in1=bfull[0][:].unsqueeze(1).to_broadcast([128, 4, 128]), op0=ALU.mult, op1=ALU.add),
                  reads=[bpt2, lng[1], bfull[1]], writes=[svt[1]])
            S.add("dve", lambda e, cbk=cbk: e.tensor_tensor(out=vn[:, :, cbk * 128:(cbk + 1) * 128], in0=uTb[0][:].rearrange("p (i q) -> p i q", q=128),
                                                            in1=svt[0][:], op=ALU.mult), reads=[uTb[1], svt[1]], writes=[bvn[cbk]])
        for nb in range(8):
            cs = slice(nb * 256, (nb + 1) * 256)
            load_w16(k, wo[0], wo[1], wo_d[:, cs], nk=32)
            for i in range(4):
                rs = slice(hf * 512 + i * 128, hf * 512 + (i + 1) * 128)
                x, bx = xc[(nb * 4 + i) % 2]
                T, bT = Tt[(nb * 4 + i) % 2]
                k.load(x[:], x_d[rs, cs], bx, eng="sp")
                pt, bpt = k.bank()
                k.mm(pt[:, :256], bpt, [(vn[:, i, cbk * 128:(cbk + 1) * 128], wo[0][:, cbk, :]) for cbk in range(32)], reads=bvn + [wo[1]])
                S.add("dve", lambda e, pt=pt, T=T, cs=cs: e.tensor_tensor(out=T[:], in0=pt[:, :256], in1=G2[0][:, cs], op=ALU.mult), reads=[bpt, G2[1]], writes=[bT])
                S.add("pool", lambda e, T=T, x=x: e.tensor_tensor(out=T[:], in0=T[:], in1=x[:], op=ALU.add), reads=[bT, bx], writes=[bT])
                op = S.add("sp", lambda e, T=T, rs=rs, cs=cs: e.dma_start(out=x3_d[rs, cs], in_=T[:]), reads=[bT], writes=[bx3], dma=True)
                k.finals.append(op)
    k.load(t1[0][:], gF_d, t1[1])
    mod(wmB_d[:, D:3 * D], bmB_d[:, D:3 * D], 2 * D, lambda c0, n: (MA[c0 // D][0][:, c0 % D:c0 % D + n], MA[c0 // D][1]))
    S.add("dve", lambda e: e.scalar_tensor_tensor(out=M1[0][:], in0=M1[0][:], scalar=1.0, in1=t1[0][:], op0=ALU.add, op1=ALU.mult),
          reads=[M1[1], t1[1]], writes=[M1[1]])
    HO = V_(k, "HO", [128, D])
    for i in range(8):
        rs = slice(i * 128, (i + 1) * 128)
        k.load(X[0][:], x3_d[rs, :], X[1], eng="sp", reads=[bx3])
        rms_mod(k, X[0][:], X[1], M1[0][:], M1[1], M0[0][:], M0[1], HO[0][:], HO[1], junk, ss, t1)
        k.store(hl_d[rs, :], HO[0][:], HO[1])
    return k.done(own)


def launch_sgu(inp, x2):
    nc = build_sgu()
    L = 1
    def mcols(js):
        cols = np.concatenate([np.arange(j * D, (j + 1) * D) for j in js])
        return np.ascontiguousarray(inp["w_mod"][L][:, cols]), bc(inp["b_mod"][L][cols])
    wmA, bmA = mcols((0, 1)); wmB, bmB = mcols((2, 3, 4))
    lngT = np.ascontiguousarray(inp["sgu_ln_g"][0].reshape(32, 128).T); lnbT = np.ascontiguousarray(inp["sgu_ln_b"][0].reshape(32, 128).T)
    wsT = np.ascontiguousarray(inp["sgu_w_s"][0].transpose(2, 0, 1))
    bsB = np.ascontiguousarray(np.broadcast_to(inp["sgu_b_s"][0][None], (128, 16, 128)))
    maps = []
    for c in range(NCORES):
        b, q = c // 4, c % 4
        maps.append(dict(x=np.ascontiguousarray(x2[b, q * 1024:(q + 1) * 1024]), ccT=np.ascontiguousarray(inp["c"][b].reshape(1, 16, 128).transpose(2, 1, 0)),
                         wmodA=wmA, bmodBA=bmA, gMix=bc(inp["mix_norm_g"][L]), wmodB=wmB, bmodBB=bmB, gF=bc(inp["ffn_norm_g"][L]),
                         w_in=inp["sgu_w_in"][0], lngT=lngT, lnbT=lnbT, wsT=wsT, bsB=bsB, w_out=inp["sgu_w_out"][0],
                         ident=np.eye(128, dtype=np.float32), ones=np.ones((128, 128), np.float32)))
    res = run(nc, maps)
    x3 = np.zeros((2, 4096, D), np.float32); hl = np.zeros((2, 4096, D), np.float32)
    for c in range(NCORES):
        x3[c // 4, (c % 4) * 1024:(c % 4 + 1) * 1024] = res[c]["x3"]
        hl[c // 4, (c % 4) * 1024:(c % 4 + 1) * 1024] = res[c]["hl"]
    return x3, hl


def build_L3():
    k = KB()
    ml_s = k.scratch("ml_scratch", [1024, D])
    with k.stage("a_", io={"ml": ml_s}):
        build_L3a(k)
    with k.stage("b_", io={"ml": ml_s}):
        build_resnorm(k)
    return k.finish()


def launch_L3(inp, y_f, y_r, u_lat, x, layer, gF, mod_cols):
    nc = build_L3()
    ident = np.eye(128, dtype=np.float32)
    cols = np.concatenate([np.arange(j * D, (j + 1) * D) for j in mod_cols])
    wmod = np.ascontiguousarray(inp["w_mod"][layer][:, cols])
    bmodB = bc(inp["b_mod"][layer][cols])
    maps = []
    for c in range(NCORES):
        b, q = c // 4, c % 4
        sl = slice(q * 1024, (q + 1) * 1024)
        ccT = np.ascontiguousarray(inp["c"][b].reshape(1, 16, 128).transpose(2, 1, 0))
        maps.append(dict(a_yf=np.ascontiguousarray(y_f[b, sl]), a_yr=np.ascontiguousarray(y_r[b, sl]), a_u=np.ascontiguousarray(u_lat[b, sl]),
                         a_dB=bc(inp["s5_d"][0]), a_bgluB=bc(inp["s5_b_glu"][0]), a_w_glu=inp["s5_w_glu"][0], a_w_out=inp["s5_w_out"][0], a_ident=ident,
                         b_x=np.ascontiguousarray(x[b, sl]), b_ccT=ccT, b_wmod=wmod, b_bmodB=bmodB, b_gF=bc(gF)))
    res = run(nc, maps)
    x1 = np.zeros((2, 4096, D), np.float32); hl = np.zeros((2, 4096, D), np.float32)
    for c in range(NCORES):
        x1[c // 4, (c % 4) * 1024:(c % 4 + 1) * 1024] = res[c]["b_x1"]
        hl[c // 4, (c % 4) * 1024:(c % 4 + 1) * 1024] = res[c]["b_hl"]
    return x1, hl


def build_L4():
    k = KB()
    x2_s = k.scratch("x2_scratch", [1024, D])
    with k.stage("f_", io={"xo": x2_s}):
        build_ffn(False, k)
    with k.stage("s_", io={"x": x2_s}):
        build_sgu(k)
    return k.finish()


def ffn_maps(inp, x, hl, layer, final, prefix=""):
    cols = np.arange(5 * D, 6 * D)
    wmod = np.ascontiguousarray(inp["w_mod"][layer][:, cols]); bmodB = bc(inp["b_mod"][layer][cols])
    cw = np.ascontiguousarray(inp["ffn_conv_w"][layer].reshape(9, 88, 128).transpose(2, 1, 0))
    cb = np.ascontiguousarray(inp["ffn_conv_b"][layer].reshape(88, 128).T)
    maps = []
    for c in range(NCORES):
        b, q = c // 4, c % 4
        ext = np.zeros((1152, D), np.float32)
        lo, hi = q * 1024 - 64, q * 1024 + 1024 + 64
        slo, shi = max(lo, 0), min(hi, 4096)
        ext[slo - lo:shi - lo] = hl[b, slo:shi]
        m = dict(hlT=np.ascontiguousarray(ext.T), x=np.ascontiguousarray(x[b, q * 1024:(q + 1) * 1024]),
                 ccT=np.ascontiguousarray(inp["c"][b].reshape(1, 16, 128).transpose(2, 1, 0)), wmod=wmod, bmodB=bmodB,
                 w_up=inp["ffn_w_up"][layer], cw=cw, cb=cb, w_down=inp["ffn_w_down"][layer])
        if final:
            m["gfin"] = bc(inp["final_norm_g"])
        maps.append({prefix + k_: v for k_, v in m.items()})
    return maps


def sgu_maps(inp, x2, prefix="", with_x=True):
    L = 1

    def mcols(js):
        cols = np.concatenate([np.arange(j * D, (j + 1) * D) for j in js])
        return np.ascontiguousarray(inp["w_mod"][L][:, cols]), bc(inp["b_mod"][L][cols])
    wmA, bmA = mcols((0, 1)); wmB, bmB = mcols((2, 3, 4))
    lngT = np.ascontiguousarray(inp["sgu_ln_g"][0].reshape(32, 128).T); lnbT = np.ascontiguousarray(inp["sgu_ln_b"][0].reshape(32, 128).T)
    wsT = np.ascontiguousarray(inp["sgu_w_s"][0].transpose(2, 0, 1))
    bsB = np.ascontiguousarray(np.broadcast_to(inp["sgu_b_s"][0][None], (128, 16, 128)))
    maps = []
    for c in range(NCORES):
        b, q = c // 4, c % 4
        m = dict(ccT=np.ascontiguousarray(inp["c"][b].reshape(1, 16, 128).transpose(2, 1, 0)),
                 wmodA=wmA, bmodBA=bmA, gMix=bc(inp["mix_norm_g"][L]), wmodB=wmB, bmodBB=bmB, gF=bc(inp["ffn_norm_g"][L]),
                 w_in=inp["sgu_w_in"][0], lngT=lngT, lnbT=lnbT, wsT=wsT, bsB=bsB, w_out=inp["sgu_w_out"][0],
                 ident=np.eye(128, dtype=np.float32), ones=np.ones((128, 128), np.float32))
        if with_x:
            m["x"] = np.ascontiguousarray(x2[b, q * 1024:(q + 1) * 1024])
        maps.append({prefix + k_: v for k_, v in m.items()})
    return maps


def launch_L4(inp, x1, hl0):
    nc = build_L4()
    fm = ffn_maps(inp, x1, hl0, 0, False, "f_")
    sm = sgu_maps(inp, None, "s_", with_x=False)
    maps = [dict(**fm[c], **sm[c]) for c in range(NCORES)]
    res = run(nc, maps)
    x3 = np.zeros((2, 4096, D), np.float32); hl = np.zeros((2, 4096, D), np.float32)
    for c in range(NCORES):
        x3[c // 4, (c % 4) * 1024:(c % 4 + 1) * 1024] = res[c]["s_x3"]
        hl[c // 4, (c % 4) * 1024:(c % 4 + 1) * 1024] = res[c]["s_hl"]
    return x3, hl


def kernel(**inputs):
    inp = {k_: np.asarray(v, dtype=np.float32) for k_, v in inputs.items()}
    u_lat, u_ctx = launch_L1(inp)
    y_f, y_r = launch_L2(inp, u_lat, u_ctx)
    x1, hl0 = launch_L3(inp, y_f, y_r, u_lat, inp["x"], 0, inp["ffn_norm_g"][0], (2, 3, 4))
    x3, hl1 = launch_L4(inp, x1, hl0)
    out = launch_ffn(inp, x3, hl1, 1, True)
    return out.astype(np.float32)
```

```python
import contextlib
import numpy as np
import concourse.bass as bass
import concourse.mybir as mybir
from concourse.bass_utils import run_bass_kernel_spmd

F32 = mybir.dt.float32
BF16 = mybir.dt.bfloat16
ALU = mybir.AluOpType
AF = mybir.ActivationFunctionType
NCORES = 8
D = 2048


class Buf:
    __slots__ = ("name", "last_writer", "readers")

    def __init__(self, name=""):
        self.name = name
        self.last_writer = None
        self.readers = []


class Op:
    __slots__ = ("eng", "fn", "deps", "dma", "sig", "needed", "inc")

    def __init__(self, eng, fn, dma, inc):
        self.eng, self.fn, self.dma, self.inc = eng, fn, dma, inc
        self.deps, self.sig, self.needed = [], None, False


ENGS = ("pe", "dve", "act", "pool", "sp")
N_DMA_SEMS = 8


class Sched:
    def __init__(self, nc):
        self.nc = nc
        self.ops = []

    def add(self, eng, fn, reads=(), writes=(), dma=False):
        op = Op(eng, fn, dma, 16 if dma else 1)
        deps = set()
        for r in reads:
            if r.last_writer is not None:
                deps.add(r.last_writer)
        for w in writes:
            if w.last_writer is not None:
                deps.add(w.last_writer)
            deps.update(w.readers)
        for r in reads:
            r.readers.append(op)
        for w in writes:
            w.last_writer = op
            w.readers = []
        op.deps = [d for d in deps if not (d.eng == "pe" and eng == "pe")]
        for d in op.deps:
            d.needed = True
        if dma:
            op.needed = True
        self.ops.append(op)
        return op

    def barrier(self):
        last = {}
        dmas = {}
        for op in self.ops:
            last[op.eng] = op
            if op.dma:
                dmas.setdefault(op.eng, []).append(op)
        deps = list(last.values())
        for e_, lst in dmas.items():
            deps.extend(lst[-N_DMA_SEMS:])
        for d in deps:
            d.needed = True
        for e_ in ENGS:
            op = Op(e_, lambda e: e.nop(), False, 1)
            op.deps = list(deps)
            self.ops.append(op)

    def emit(self, final_ops):
        nc = self.nc
        for o in final_ops:
            o.needed = True
        with contextlib.ExitStack() as st:
            comp_sem = {e: st.enter_context(nc.semaphore(f"s_{e}")) for e in ENGS}
            dma_sems = {e: [st.enter_context(nc.semaphore(f"d_{e}{i}")) for i in range(N_DMA_SEMS)]
                        for e in ("sp", "act", "pool")}
            comp_cnt = {e: 0 for e in ENGS}
            dma_cnt = {e: 0 for e in ENGS}
            sem_total = {}
            per_eng = {e: [] for e in ENGS}
            for op in self.ops:
                per_eng[op.eng].append(op)
                if op.dma:
                    n = dma_cnt[op.eng]
                    dma_cnt[op.eng] += 1
                    sem = dma_sems[op.eng][n % N_DMA_SEMS]
                    before = sem_total.get(id(sem), 0)
                    sem_total[id(sem)] = before + op.inc
                    op.sig = (sem, before + op.inc, before)
                elif op.needed:
                    comp_cnt[op.eng] += 1
                    op.sig = (comp_sem[op.eng], comp_cnt[op.eng], None)
            final = list(final_ops)

            def run_engine(ename, eng):
                seen = {}

                def wait(sem, val):
                    if seen.get(id(sem), 0) >= val:
                        return
                    eng.wait_ge(sem, val)
                    seen[id(sem)] = val

                for op in per_eng[ename]:
                    for d in op.deps:
                        wait(d.sig[0], d.sig[1])
                    if op.dma and op.sig[2] > 0:
                        wait(op.sig[0], op.sig[2])
                    ins = op.fn(eng)
                    if op.sig is not None:
                        ins.then_inc(op.sig[0], op.inc if op.dma else 1)
                if ename == "sp":
                    for o in final:
                        wait(o.sig[0], o.sig[1])

            with nc.Block() as block:
                block.tensor(lambda e: run_engine("pe", e))
                block.vector(lambda e: run_engine("dve", e))
                block.scalar(lambda e: run_engine("act", e))
                block.gpsimd(lambda e: run_engine("pool", e))
                block.sync(lambda e: run_engine("sp", e))


class KB:
    def __init__(self):
        self.nc = bass.Bass("TRN2", target_bir_lowering=False)
        self.S = Sched(self.nc)
        self.st = contextlib.ExitStack()
        self.finals = []
        self.banks = []
        for i in range(8):
            t = self.st.enter_context(self.nc.psum_tensor(f"ps{i}", [128, 512], F32))
            self.banks.append((t, Buf(f"ps{i}")))
        self.bi = 0
        self.rr = 0
        self.prefix = ""
        self.io = {}
        self.stacks = [self.st]

    @contextlib.contextmanager
    def stage(self, prefix, io=None):
        old = (self.prefix, self.io)
        self.prefix, self.io = prefix, dict(io or {})
        st = contextlib.ExitStack()
        self.stacks.append(st)
        try:
            yield self
        finally:
            self.S.barrier()
            self.stacks.pop()
            st.close()
            self.prefix, self.io = old

    def bank(self):
        b = self.banks[self.bi % 8]
        self.bi += 1
        return b

    def sb(self, name, shape, dt=F32):
        return self.stacks[-1].enter_context(self.nc.sbuf_tensor(self.prefix + name, list(shape), dt))

    def din(self, name, shape, dt=F32):
        if name in self.io:
            return self.io[name]
        return self.nc.dram_tensor(self.prefix + name, list(shape), dt, kind="ExternalInput").ap()

    def dout(self, name, shape, dt=F32):
        if name in self.io:
            return self.io[name]
        return self.nc.dram_tensor(self.prefix + name, list(shape), dt, kind="ExternalOutput").ap()

    def scratch(self, name, shape, dt=F32):
        return self.nc.dram_tensor(name, list(shape), dt, kind="Internal").ap()

    def load(self, dst_ap, src_ap, buf, eng=None, reads=()):
        if eng is None:
            eng = ("sp", "pool")[self.rr % 2]
            self.rr += 1
        return self.S.add(eng, lambda e: e.dma_start(out=dst_ap, in_=src_ap), reads=reads, writes=[buf], dma=True)

    def store(self, dst_ap, src_ap, buf, eng="sp"):
        op = self.S.add(eng, lambda e: e.dma_start(out=dst_ap, in_=src_ap), reads=[buf], dma=True)
        self.finals.append(op)
        return op

    def done(self, own):
        return self.finish() if own else None

    def mm(self, out_ap, out_buf, pairs, reads):
        def fn(e):
            n = len(pairs)
            ins = None
            for i, (l, r) in enumerate(pairs):
                ins = e.matmul(out_ap, l, r, start=(i == 0), stop=(i == n - 1))
            return ins
        return self.S.add("pe", fn, reads=reads, writes=[out_buf])

    def finish(self):
        self.S.emit(self.finals)
        self.st.close()
        return self.nc


def run(nc, in_maps):
    res = run_bass_kernel_spmd(nc, in_maps, core_ids=list(range(NCORES)))
    return res.results


def bc(v, p=128):
    return np.ascontiguousarray(np.broadcast_to(np.asarray(v, np.float32).reshape(1, -1), (p, v.size)))


def norm_mod_T(k, xt_ap, rows, A, B, bA, bB, bx, ident, bident, tag, bufs, want_hl=False):
    S = k.S
    junk, bjunk = bufs["junk"]
    ss, bss = bufs["ss"]
    rstd, brstd = bufs["rstd"]
    t1, bt1 = bufs["t1"]
    hl, bhl = bufs["hl"]
    hlT, bhlT = bufs["hlT"]
    S.add("act", lambda e: e.activation(out=junk[:rows, :], in_=xt_ap, func=AF.Square, accum_out=ss[:rows, 0:1]),
          reads=[bx], writes=[bjunk, bss])
    S.add("dve", lambda e: e.tensor_scalar(out=rstd[:rows, 0:1], in0=ss[:rows, 0:1], scalar1=1.0 / D, scalar2=1e-6,
                                           op0=ALU.mult, op1=ALU.add), reads=[bss], writes=[brstd])
    S.add("act", lambda e: e.sqrt(out=rstd[:rows, 0:1], in_=rstd[:rows, 0:1]), reads=[brstd], writes=[brstd])
    S.add("dve", lambda e: e.reciprocal(out=rstd[:rows, 0:1], in_=rstd[:rows, 0:1]), reads=[brstd], writes=[brstd])
    S.add("dve", lambda e: e.scalar_tensor_tensor(out=t1[:rows, :], in0=xt_ap, scalar=rstd[:rows, 0:1], in1=A[:rows, :],
                                                  op0=ALU.mult, op1=ALU.mult), reads=[bx, brstd, bA], writes=[bt1])
    S.add("dve", lambda e: e.tensor_tensor(out=hl[:rows, :], in0=t1[:rows, :], in1=B[:rows, :], op=ALU.add),
          reads=[bt1, bB], writes=[bhl])
    for half in range(2):
        pt, bpt = k.bank()
        ptb = pt[:].bitcast(BF16)

        def fn(e, half=half, ptb=ptb):
            ins = None
            for j in range(8):
                kc = half * 8 + j
                ins = e.transpose(ptb[:, j * 128:j * 128 + rows], hl[:rows, kc * 128:(kc + 1) * 128], ident[:rows, :rows])
            return ins
        S.add("pe", fn, reads=[bhl, bident], writes=[bpt])
        S.add("act", lambda e, half=half, ptb=ptb: e.copy(
            out=hlT[:, half * 8:(half + 1) * 8, :rows],
            in_=ptb.rearrange("p (j t) -> p j t", t=128)[:, :, :rows]), reads=[bpt], writes=[bhlT])


def build_mod(k, ccT, bcc, wmod_d, bmodB_d, ncols, nrows_m, outs, NW=128, nbuf=2):
    S = k.S
    sc = k.sb("sc", [128, 16, nrows_m]); bsc = Buf()
    scb = k.sb("scb", [128, 16, nrows_m, 128], BF16); bscb = Buf()
    S.add("act", lambda e: e.activation(out=sc[:], in_=ccT[:], func=AF.Silu), reads=[bcc], writes=[bsc])
    S.add("dve", lambda e: e.tensor_copy(out=scb[:], in_=sc[:].unsqueeze(3).to_broadcast([128, 16, nrows_m, 128])),
          reads=[bsc], writes=[bscb])
    wst = [(k.sb(f"wst{i}", [128, 16, NW], BF16), Buf()) for i in range(nbuf)]
    bmt = [(k.sb(f"bmt{i}", [128, NW]), Buf()) for i in range(nbuf)]
    for nb in range(ncols // NW):
        w, bw = wst[nb % nbuf]
        bm, bbm = bmt[nb % nbuf]
        k.load(w[:], wmod_d[:, nb * NW:(nb + 1) * NW].rearrange("(kc p) n -> p kc n", p=128), bw, eng="pool")
        k.load(bm[:], bmodB_d[:, nb * NW:(nb + 1) * NW], bbm, eng="sp")
        for m in range(nrows_m):
            pt, bpt = k.bank()
            k.mm(pt[:, :NW], bpt, [(scb[:, kc, m, :], w[:, kc, :]) for kc in range(16)], reads=[bscb, bw])
            dst, bdst = outs[m](nb * NW, NW)
            S.add("dve", lambda e, pt=pt, dst=dst, bm=bm: e.tensor_tensor(out=dst, in0=pt[:, :NW], in1=bm[:], op=ALU.add),
                  reads=[bpt, bbm], writes=[bdst])


def build_L1():
    k = KB()
    S = k.S
    T = 1088
    xs = k.din("xs", [T, D]); ccT_d = k.din("ccT", [128, 16, 2]); gB_d = k.din("gB", [128, D])
    wmod_d = k.din("wmod", [D, 4096]); bmodB_d = k.din("bmodB", [128, 4096]); win_d = k.din("w_in", [D, D])
    ident_d = k.din("ident", [128, 128])
    u_d = k.dout("u", [T, D])
    ident = k.sb("ident_sb", [128, 128], BF16); bident = Buf()
    k.load(ident[:], ident_d, bident, eng="pool")
    ccT = k.sb("ccT_sb", [128, 16, 2]); bcc = Buf()
    k.load(ccT[:], ccT_d, bcc)
    gB = k.sb("gB_sb", [128, D]); bgB = Buf()
    k.load(gB[:], gB_d, bgB)
    AB = [[(k.sb(f"AB{m}{j}", [128, D]), Buf()) for j in range(2)] for m in range(2)]
    outs = [(lambda c0, n, m=m: (AB[m][c0 // D][0][:, c0 % D:c0 % D + n], AB[m][c0 // D][1])) for m in range(2)]
    build_mod(k, ccT, bcc, wmod_d, bmodB_d, 4096, 2, outs)
    for m in range(2):
        A, bA = AB[m][1]
        S.add("dve", lambda e, A=A: e.scalar_tensor_tensor(out=A[:], in0=A[:], scalar=1.0, in1=gB[:], op0=ALU.add, op1=ALU.mult),
              reads=[bA, bgB], writes=[bA])
    win = k.sb("win", [128, 16, D], BF16); bwin = Buf()
    k.load(win[:], win_d.rearrange("(kc p) n -> p kc n", p=128), bwin, eng="pool")
    xt = [(k.sb(f"xt{i}", [128, D]), Buf()) for i in range(2)]
    ut = [(k.sb(f"ut{i}", [128, D]), Buf()) for i in range(2)]
    junk_ = (k.sb("junk", [128, D], BF16), Buf())
    t1_ = (k.sb("t1", [128, D]), Buf())
    nb_ = [dict(junk=junk_, ss=(k.sb(f"ss{i}", [128, 1]), Buf()),
                rstd=(k.sb(f"rstd{i}", [128, 1]), Buf()), t1=t1_,
                hl=(k.sb(f"hl{i}", [128, D], BF16), Buf()), hlT=(k.sb(f"hlT{i}", [128, 16, 128], BF16), Buf()))
           for i in range(2)]
    for i in range(9):
        rows = 128 if i < 8 else 64
        m = 0 if i < 8 else 1
        x, bx = xt[i % 2]
        k.load(x[:rows, :], xs[i * 128:i * 128 + rows, :], bx, eng="sp")
        bufs = nb_[i % 2]
        norm_mod_T(k, x[:rows, :], rows, AB[m][1][0], AB[m][0][0], AB[m][1][1], AB[m][0][1], bx, ident, bident, "l1", bufs)
        hlT, bhlT = bufs["hlT"]
        u, bu = ut[i % 2]
        for nb in range(4):
            pt, bpt = k.bank()
            k.mm(pt[:rows, :], bpt, [(hlT[:, kc, :rows], win[:, kc, nb * 512:(nb + 1) * 512]) for kc in range(16)],
                 reads=[bhlT, bwin])
            eng = "act" if nb % 2 == 0 else "dve"
            if eng == "act":
                S.add("act", lambda e, pt=pt, u=u, nb=nb, rows=rows: e.copy(out=u[:rows, nb * 512:(nb + 1) * 512], in_=pt[:rows, :]),
                      reads=[bpt], writes=[bu])
            else:
                S.add("dve", lambda e, pt=pt, u=u, nb=nb, rows=rows: e.tensor_copy(out=u[:rows, nb * 512:(nb + 1) * 512], in_=pt[:rows, :]),
                      reads=[bpt], writes=[bu])
        k.store(u_d[i * 128:i * 128 + rows, :], u[:rows, :], bu)
    return k.finish()


def launch_L1(inp):
    nc = build_L1()
    maps = []
    ident = np.eye(128, dtype=np.float32)
    for c in range(NCORES):
        b, q = c // 4, c % 4
        xs = np.concatenate([inp["x"][b, q * 1024:(q + 1) * 1024], inp["ctx"][b, q * 64:(q + 1) * 64]], 0)
        cc = np.stack([inp["c"][b], inp["c_ctx"]], 0)
        ccT = np.ascontiguousarray(cc.reshape(2, 16, 128).transpose(2, 1, 0))
        maps.append(dict(xs=np.ascontiguousarray(xs), ccT=ccT, gB=bc(inp["mix_norm_g"][0]),
                         wmod=np.ascontiguousarray(inp["w_mod"][0][:, :4096]), bmodB=bc(inp["b_mod"][0][:4096]),
                         w_in=np.ascontiguousarray(inp["s5_w_in"][0]), ident=ident))
    res = run(nc, maps)
    u_lat = np.zeros((2, 4096, D), np.float32)
    u_ctx = np.zeros((2, 256, D), np.float32)
    for c in range(NCORES):
        b, q = c // 4, c % 4
        u_lat[b, q * 1024:(q + 1) * 1024] = res[c]["u"][:1024]
        u_ctx[b, q * 64:(q + 1) * 64] = res[c]["u"][1024:]
    return u_lat, u_ctx


SEQ_T = 4352
SEG = 1088


def build_L2():
    k = KB()
    S = k.S
    V = lambda name, shape, dt=F32: (k.sb(name, shape, dt), Buf(name))
    uTd = [k.din("uT_f", [256, 2 * SEQ_T]), k.din("uT_r", [256, 2 * SEQ_T])]
    yTd = [k.dout("yT_f", [256, 2 * SEQ_T]), k.dout("yT_r", [256, 2 * SEQ_T])]
    pd = {n: k.din(n, [128, 2, 8]) for n in ("are", "aim", "lst")}
    pd4 = {n: k.din(n, [128, 2, 8, 16]) for n in ("bre", "bim", "cre", "cim")}
    ident_d = k.din("ident", [128, 128])
    ident, bident = V("ident_sb", [128, 128])
    k.load(ident[:], ident_d, bident)
    P = {}
    for n, d_ in pd.items():
        P[n] = V(n + "_sb", [128, 16])
        k.load(P[n][0][:], d_.rearrange("p d r -> p (d r)"), P[n][1])
    P4 = {}
    for n, d_ in pd4.items():
        P4[n] = V(n + "_sb", [128, 16, 16])
        k.load(P4[n][0][:], d_.rearrange("p d r h -> p (d r) h"), P4[n][1])

    cnt = [0]

    def tmp(shape=(128, 16)):
        cnt[0] += 1
        return V(f"tmp{cnt[0]}", list(shape))

    def tt(out, a, b, op, eng="dve"):
        S.add(eng, lambda e: e.tensor_tensor(out=out[0][:], in0=a[0][:], in1=b[0][:], op=op), reads=[a[1], b[1]], writes=[out[1]])
        return out

    def ts(out, a, s1, op0, s2=None, op1=None):
        if op1 is None:
            S.add("dve", lambda e: e.tensor_single_scalar(out=out[0][:], in_=a[0][:], scalar=s1, op=op0), reads=[a[1]], writes=[out[1]])
        else:
            S.add("dve", lambda e: e.tensor_scalar(out=out[0][:], in0=a[0][:], scalar1=s1, scalar2=s2, op0=op0, op1=op1),
                  reads=[a[1]], writes=[out[1]])
        return out

    def stt(out, a, s, b, op0, op1):
        S.add("dve", lambda e: e.scalar_tensor_tensor(out=out[0][:], in0=a[0][:], scalar=s, in1=b[0][:], op0=op0, op1=op1),
              reads=[a[1], b[1]], writes=[out[1]])
        return out

    def act(out, a, func):
        S.add("act", lambda e: e.activation(out=out[0][:], in_=a[0][:], func=func), reads=[a[1]], writes=[out[1]])
        return out

    are, aim, lst = P["are"], P["aim"], P["lst"]
    dt = act(tmp(), lst, AF.Exp)
    adt = tt(tmp(), are, dt, ALU.mult)
    mag = act(tmp(), adt, AF.Exp)
    th = tt(tmp(), aim, dt, ALU.mult)
    y = ts(tmp(), th, 1.0 / 32.0, ALU.mult)
    y2 = tt(tmp(), y, y, ALU.mult)
    p = ts(tmp(), y2, 1.0 / 362880.0, ALU.mult)
    for c_ in (-1.0 / 5040.0, 1.0 / 120.0, -1.0 / 6.0):
        p = stt(tmp(), p, c_, y2, ALU.add, ALU.mult)
    s = stt(tmp(), p, 1.0, y, ALU.add, ALU.mult)
    q = ts(tmp(), y2, -1.0 / 3628800.0, ALU.mult)
    for c_ in (1.0 / 40320.0, -1.0 / 720.0, 1.0 / 24.0, -0.5):
        q = stt(tmp(), q, c_, y2, ALU.add, ALU.mult)
    c = ts(tmp(), q, 1.0, ALU.add)
    for _ in range(5):
        s_n = stt(tmp(), s, 2.0, c, ALU.mult, ALU.mult)
        t_ = stt(tmp(), s, -2.0, s, ALU.mult, ALU.mult)
        c = ts(tmp(), t_, 1.0, ALU.add)
        s = s_n
    cth, sth = c, s
    lr = tt(tmp(), mag, cth, ALU.mult)
    li = tt(tmp(), mag, sth, ALU.mult)
    den = tt(tmp(), tt(tmp(), are, are, ALU.mult), tt(tmp(), aim, aim, ALU.mult), ALU.add)
    rden = tmp()
    S.add("dve", lambda e: e.reciprocal(out=rden[0][:], in_=den[0][:]), reads=[den[1]], writes=[rden[1]])
    lm1 = ts(tmp(), lr, -1.0, ALU.add)
    kre = tt(tmp(), tt(tmp(), tt(tmp(), lm1, are, ALU.mult), tt(tmp(), li, aim, ALU.mult), ALU.add), rden, ALU.mult)
    kim = tt(tmp(), tt(tmp(), tt(tmp(), li, are, ALU.mult), tt(tmp(), lm1, aim, ALU.mult), ALU.subtract), rden, ALU.mult)

    def bmul(name, a, b4):
        o = V(name, [128, 16, 16])
        S.add("dve", lambda e: e.tensor_tensor(out=o[0][:], in0=b4[0][:], in1=a[0][:].unsqueeze(2).to_broadcast([128, 16, 16]), op=ALU.mult),
              reads=[a[1], b4[1]], writes=[o[1]])
        return o
    bbr = tt(V("bbr", [128, 16, 16]), bmul("m1", kre, P4["bre"]), bmul("m2", kim, P4["bim"]), ALU.subtract)
    bbi = tt(V("bbi", [128, 16, 16]), bmul("m3", kre, P4["bim"]), bmul("m4", kim, P4["bre"]), ALU.add)
    ncim = ts(V("ncim", [128, 16, 16]), P4["cim"], -1.0, ALU.mult)

    def blockify(name, src, dt_):
        o = V(name, [128, 16, 32], dt_)
        S.add("dve", lambda e: e.memset(o[0][:], 0.0), writes=[o[1]])
        S.add("dve", lambda e: e.tensor_copy(out=o[0][0:64, :, 0:16], in_=src[0][0:64, :, :]), reads=[src[1]], writes=[o[1]])
        S.add("dve", lambda e: e.tensor_copy(out=o[0][64:128, :, 16:32], in_=src[0][64:128, :, :]), reads=[src[1]], writes=[o[1]])
        return o
    CTre = blockify("CTre", P4["cre"], BF16)
    CTim = blockify("CTim", ncim, BF16)
    BLre = blockify("BLre", bbr, F32)
    BLim = blockify("BLim", bbi, F32)
    BbT = [V("BbTre", [32, 16, 128], BF16), V("BbTim", [32, 16, 128], BF16)]
    for ci, BL in enumerate((BLre, BLim)):
        for g4 in range(4):
            pt, bpt = k.bank()

            def fn(e, BL=BL, g4=g4, pt=pt):
                ins = None
                for j in range(4):
                    ins = e.transpose(pt[0:32, j * 128:(j + 1) * 128], BL[0][:, g4 * 4 + j, :], ident[:, :])
                return ins
            S.add("pe", fn, reads=[BL[1], bident], writes=[bpt])
            S.add("act", lambda e, pt=pt, g4=g4, ci=ci: e.copy(out=BbT[ci][0][:, g4 * 4:(g4 + 1) * 4, :],
                                                             in_=pt[0:32, :].rearrange("p (j t) -> p j t", t=128)),
                  reads=[bpt], writes=[BbT[ci][1]])

    CM = [V("cma", [128, 16, 64]), V("cmb", [128, 16, 64])]

    def cmul_bc(o_re, o_im, a_re, a_im, p_re, p_im, L, n):
        W_ = o_re[0].shape[2]
        t1 = (CM[0][0][:, :, 0:n], CM[0][1]); t2 = (CM[1][0][:, :, 0:n], CM[1][1])
        pb_re = lambda: p_re[0][:].unsqueeze(2).to_broadcast([128, 16, n])
        pb_im = lambda: p_im[0][:].unsqueeze(2).to_broadcast([128, 16, n])
        S.add("dve", lambda e: e.tensor_tensor(out=t1[0][:], in0=a_re[0][:, :, 0:n], in1=pb_re(), op=ALU.mult), reads=[a_re[1], p_re[1]], writes=[t1[1]])
        S.add("dve", lambda e: e.tensor_tensor(out=t2[0][:], in0=a_im[0][:, :, 0:n], in1=pb_im(), op=ALU.mult), reads=[a_im[1], p_im[1]], writes=[t2[1]])
        S.add("dve", lambda e: e.tensor_tensor(out=o_re[0][:, :, L:L + n], in0=t1[0][:], in1=t2[0][:], op=ALU.subtract), reads=[t1[1], t2[1]], writes=[o_re[1]])
        S.add("dve", lambda e: e.tensor_tensor(out=t1[0][:], in0=a_re[0][:, :, 0:n], in1=pb_im(), op=ALU.mult), reads=[a_re[1], p_im[1], o_re[1]], writes=[t1[1]])
        S.add("dve", lambda e: e.tensor_tensor(out=t2[0][:], in0=a_im[0][:, :, 0:n], in1=pb_re(), op=ALU.mult), reads=[a_im[1], p_re[1], o_re[1]], writes=[t2[1]])
        S.add("dve", lambda e: e.tensor_tensor(out=o_im[0][:, :, L:L + n], in0=t1[0][:], in1=t2[0][:], op=ALU.add), reads=[t1[1], t2[1]], writes=[o_im[1]])

    def csq(p_re, p_im):
        a = tt(tmp(), p_re, p_re, ALU.mult); b = tt(tmp(), p_im, p_im, ALU.mult)
        n_re = tt(tmp(), a, b, ALU.subtract)
        n_im = stt(tmp(), p_re, 2.0, p_im, ALU.mult, ALU.mult)
        return n_re, n_im

    def power_table(name, p_re, p_im, n):
        T_re = V(name + "re", [128, 16, n]); T_im = V(name + "im", [128, 16, n])
        S.add("dve", lambda e: e.memset(T_re[0][:, :, 0:1], 1.0), writes=[T_re[1]])
        S.add("dve", lambda e: e.memset(T_im[0][:, :, 0:1], 0.0), writes=[T_im[1]])
        L = 1
        while L < n:
            m = min(L, n - L)
            cmul_bc(T_re, T_im, T_re, T_im, p_re, p_im, L, m)
            p_re, p_im = csq(p_re, p_im)
            L *= 2
        return T_re, T_im, p_re, p_im
    E64re, E64im, q_re, q_im = power_table("E64", cth, sth, 64)
    Fre, Fim, _, _ = power_table("F68", q_re, q_im, 68)

    Ere, Eim = V("Ere", [128, 68, 64]), V("Eim", [128, 68, 64])
    Et1, Et2 = V("Et1", [128, 34, 64]), V("Et2", [128, 34, 64])
    Rt = V("Rt", [128, SEG])
    Wsets = [{n: V(n + str(i), [128, SEG]) for n in ("xre", "xim", "A1", "A2", "T1")} for i in range(2)]
    Ssets = [(V(f"sre{i}", [128, SEG], BF16), V(f"sim{i}", [128, SEG], BF16)) for i in range(2)]
    UPF = 2
    usb = [V(f"usb{i}", [32, SEG], BF16) for i in range(UPF + 1)]
    iters = [(d, pr, b, sgi) for d in range(2) for pr in range(8) for b in range(2) for sgi in range(SEQ_T // SEG)]

    def issue_u(n_it):
        d_, pr_, b_, sg_ = iters[n_it]
        u__, bu__ = usb[n_it % (UPF + 1)]
        k.load(u__[:], uTd[d_][pr_ * 32:(pr_ + 1) * 32, b_ * SEQ_T + sg_ * SEG:b_ * SEQ_T + (sg_ + 1) * SEG], bu__, eng="pool")

    yt = [V(f"yt{i}", [32, SEG]) for i in range(2)]
    carries = [V(f"carry{i}", [128, 2]) for i in range(2)]
    it = 0
    for d in range(2):
        for pr in range(8):
            dp = d * 8 + pr
            for hf in range(2):
                Fb = lambda T, hf=hf, dp=dp: T[0][:, dp, hf * 34:(hf + 1) * 34].unsqueeze(2).to_broadcast([128, 34, 64])
                Eb = lambda T, dp=dp: T[0][:, dp, :].unsqueeze(1).to_broadcast([128, 34, 64])
                Eo_re = Ere[0][:, hf * 34:(hf + 1) * 34, :]
                Eo_im = Eim[0][:, hf * 34:(hf + 1) * 34, :]
                S.add("pool", lambda e, Fb=Fb, Eb=Eb: e.tensor_tensor(out=Et1[0][:], in0=Fb(Fre), in1=Eb(E64re), op=ALU.mult), reads=[Fre[1], E64re[1]], writes=[Et1[1]])
                S.add("pool", lambda e, Fb=Fb, Eb=Eb: e.tensor_tensor(out=Et2[0][:], in0=Fb(Fim), in1=Eb(E64im), op=ALU.mult), reads=[Fim[1], E64im[1]], writes=[Et2[1]])
                S.add("pool", lambda e, Eo_re=Eo_re: e.tensor_tensor(out=Eo_re, in0=Et1[0][:], in1=Et2[0][:], op=ALU.subtract), reads=[Et1[1], Et2[1]], writes=[Ere[1]])
                S.add("pool", lambda e, Fb=Fb, Eb=Eb: e.tensor_tensor(out=Et1[0][:], in0=Fb(Fre), in1=Eb(E64im), op=ALU.mult), reads=[Fre[1], E64im[1], Ere[1]], writes=[Et1[1]])
                S.add("pool", lambda e, Fb=Fb, Eb=Eb: e.tensor_tensor(out=Et2[0][:], in0=Fb(Fim), in1=Eb(E64re), op=ALU.mult), reads=[Fim[1], E64re[1], Ere[1]], writes=[Et2[1]])
                S.add("pool", lambda e, Eo_im=Eo_im: e.tensor_tensor(out=Eo_im, in0=Et1[0][:], in1=Et2[0][:], op=ALU.add), reads=[Et1[1], Et2[1]], writes=[Eim[1]])
            S.add("pool", lambda e, dp=dp: e.tensor_copy(out=Rt[0][:], in_=mag[0][:, dp:dp + 1].to_broadcast([128, SEG])), reads=[mag[1]], writes=[Rt[1]])
            Ef_re = Ere[0][:].rearrange("p k t -> p (k t)")
            Ef_im = Eim[0][:].rearrange("p k t -> p (k t)")
            chunks = [(c0, min(512, SEG - c0)) for c0 in range(0, SEG, 512)]

            def tte(eng, out, a_, bap, bbuf, op):
                S.add(eng, lambda e: e.tensor_tensor(out=out[0][:], in0=a_[0][:], in1=bap, op=op), reads=[a_[1], bbuf], writes=[out[1]])

            def stage_A(cx):
                W = cx["W"]; u_, bu_ = cx["u"]
                if cx["it"] == 0:
                    for pf_ in range(UPF):
                        issue_u(pf_)
                if cx["it"] + UPF < len(iters):
                    issue_u(cx["it"] + UPF)
                for ci, (dst, bdst) in enumerate((W["xre"], W["xim"])):
                    for (c0, n) in chunks:
                        pt, bpt = k.bank()
                        k.mm(pt[:, :n], bpt, [(BbT[ci][0][:, dp, :], u_[:, c0:c0 + n])], reads=[BbT[ci][1], bu_])
                        S.add("act", lambda e, pt=pt, dst=dst, c0=c0, n=n: e.copy(out=dst[:, c0:c0 + n], in_=pt[:, :n]), reads=[bpt], writes=[bdst])
                Er, Ei = cx["Er"], cx["Ei"]
                xre, xim, A1, A2, T1 = W["xre"], W["xim"], W["A1"], W["A2"], W["T1"]
                tte("dve", A1, xre, Er, Ere[1], ALU.mult)
                tte("dve", T1, xim, Ei, Eim[1], ALU.mult)
                tt(A1, A1, T1, ALU.add)
                tte("dve", A2, xim, Er, Ere[1], ALU.mult)
                tte("dve", T1, xre, Ei, Eim[1], ALU.mult)
                tt(A2, A2, T1, ALU.subtract)
                pcarry = cx["pcarry"]; carry = cx["carry"]
                for ci, (src, dst) in enumerate(((A1, xre), (A2, xim))):
                    if cx["sgi"] == 0:
                        S.add("dve", lambda e, src=src, dst=dst: e.tensor_tensor_scan(out=dst[0][:], data0=Rt[0][:], data1=src[0][:], initial=0.0,
                                                                                     op0=ALU.mult, op1=ALU.add), reads=[Rt[1], src[1]], writes=[dst[1]])
                    else:
                        S.add("dve", lambda e, src=src, dst=dst, ci=ci, pcarry=pcarry: e.tensor_tensor_scan(out=dst[0][:], data0=Rt[0][:], data1=src[0][:],
                                                                                            initial=pcarry[0][:, ci:ci + 1], op0=ALU.mult, op1=ALU.add),
                              reads=[Rt[1], src[1], pcarry[1]], writes=[dst[1]])
                if cx["sgi"] < SEQ_T // SEG - 1:
                    S.add("act", lambda e, carry=carry, xre=xre: e.copy(out=carry[0][:, 0:1], in_=xre[0][:, SEG - 1:SEG]), reads=[xre[1]], writes=[carry[1]])
                    S.add("act", lambda e, carry=carry, xim=xim: e.copy(out=carry[0][:, 1:2], in_=xim[0][:, SEG - 1:SEG]), reads=[xim[1]], writes=[carry[1]])

            def stage_B(cx):
                W = cx["W"]; sre, sim = cx["S"]; y_, by_ = cx["y"]
                Er, Ei = cx["Er"], cx["Ei"]
                xre, xim, A1, T1 = W["xre"], W["xim"], W["A1"], W["T1"]
                tte("dve", T1, xre, Er, Ere[1], ALU.mult)
                tte("dve", A1, xim, Ei, Eim[1], ALU.mult)
                tt(sre, T1, A1, ALU.subtract)
                tte("dve", T1, xre, Ei, Eim[1], ALU.mult)
                tte("dve", A1, xim, Er, Ere[1], ALU.mult)
                tt(sim, T1, A1, ALU.add)
                for (c0, n) in chunks:
                    pt, bpt = k.bank()
                    k.mm(pt[0:32, :n], bpt, [(CTre[0][:, dp, :], sre[0][:, c0:c0 + n]), (CTim[0][:, dp, :], sim[0][:, c0:c0 + n])],
                         reads=[CTre[1], CTim[1], sre[1], sim[1]])
                    S.add("act", lambda e, pt=pt, y_=y_, c0=c0, n=n: e.copy(out=y_[:, c0:c0 + n], in_=pt[0:32, :n]), reads=[bpt], writes=[by_])
                k.store(yTd[d][pr * 32:(pr + 1) * 32, cx["b"] * SEQ_T + cx["t0"]:cx["b"] * SEQ_T + cx["t0"] + SEG], y_[:], by_)

            cxs = []
            for b in range(2):
                for sgi in range(SEQ_T // SEG):
                    t0 = sgi * SEG
                    cxs.append(dict(it=it, b=b, sgi=sgi, t0=t0, W=Wsets[it % 2], S=Ssets[it % 2], carry=carries[it % 2], pcarry=carries[(it - 1) % 2],
                                    u=usb[it % (UPF + 1)], y=yt[it % 2], Er=Ef_re[:, t0:t0 + SEG], Ei=Ef_im[:, t0:t0 + SEG]))
                    it += 1
            stage_A(cxs[0])
            for i_ in range(len(cxs)):
                if i_ + 1 < len(cxs):
                    stage_A(cxs[i_ + 1])
                stage_B(cxs[i_])
    return k.finish()


def s5_param_layout(inp, c):
    G0 = 16 * c
    out = {}
    for n, key in (("are", "s5_a_re"), ("aim", "s5_a_im")):
        a = inp[key][0][:, G0:G0 + 16, :]
        out[n] = np.ascontiguousarray(a.reshape(2, 8, 2, 64).transpose(2, 3, 0, 1).reshape(128, 2, 8))
    ls = inp["s5_log_step"][0][:, G0:G0 + 16]
    ls = np.broadcast_to(ls.reshape(2, 8, 2, 1), (2, 8, 2, 64))
    out["lst"] = np.ascontiguousarray(ls.transpose(2, 3, 0, 1).reshape(128, 2, 8))
    for n, key in (("bre", "s5_b_re"), ("bim", "s5_b_im")):
        a = inp[key][0][:, G0:G0 + 16]
        out[n] = np.ascontiguousarray(a.reshape(2, 8, 2, 64, 16).transpose(2, 3, 0, 1, 4).reshape(128, 2, 8, 16))
    for n, key in (("cre", "s5_c_re"), ("cim", "s5_c_im")):
        a = inp[key][0][:, G0:G0 + 16]
        out[n] = np.ascontiguousarray(a.reshape(2, 8, 2, 16, 64).transpose(2, 4, 0, 1, 3).reshape(128, 2, 8, 16))
    return out


def launch_L2(inp, u_lat, u_ctx):
    nc = build_L2()
    seq_f = np.concatenate([u_ctx, u_lat], 1)
    seq_r = np.concatenate([u_ctx[:, ::-1], u_lat[:, ::-1]], 1)
    ident = np.eye(128, dtype=np.float32)
    maps = []
    for c in range(NCORES):
        m = s5_param_layout(inp, c)
        m["uT_f"] = np.ascontiguousarray(seq_f[:, :, c * 256:(c + 1) * 256].transpose(2, 0, 1).reshape(256, 2 * SEQ_T))
        m["uT_r"] = np.ascontiguousarray(seq_r[:, :, c * 256:(c + 1) * 256].transpose(2, 0, 1).reshape(256, 2 * SEQ_T))
        m["ident"] = ident
        maps.append(m)
    res = run(nc, maps)
    y_f = np.zeros((2, 4096, D), np.float32)
    y_r = np.zeros((2, 4096, D), np.float32)
    for c in range(NCORES):
        yf = res[c]["yT_f"].reshape(256, 2, SEQ_T)[:, :, 256:]
        yr = res[c]["yT_r"].reshape(256, 2, SEQ_T)[:, :, 256:][:, :, ::-1]
        y_f[:, :, c * 256:(c + 1) * 256] = yf.transpose(1, 2, 0)
        y_r[:, :, c * 256:(c + 1) * 256] = yr.transpose(1, 2, 0)
    return y_f, y_r


def V_(k, name, shape, dt=F32):
    return (k.sb(name, shape, dt), Buf(name))


def gelu_tanh(k, out, bout, xin, bx, tmpa, tmpb, rows=128, accum=None, eng2="pool"):
    S = k.S
    ta, bta = tmpa
    tb, btb = tmpb
    S.add("act", lambda e: e.activation(out=ta, in_=xin, func=AF.Square), reads=[bx], writes=[bta])
    S.add("dve", lambda e: e.tensor_scalar(out=ta, in0=ta, scalar1=0.044715, scalar2=1.0, op0=ALU.mult, op1=ALU.add), reads=[bta], writes=[bta])
    S.add("dve", lambda e: e.tensor_tensor(out=ta, in0=ta, in1=xin, op=ALU.mult), reads=[bta, bx], writes=[bta])
    S.add("act", lambda e: e.activation(out=tb, in_=ta, func=AF.Sigmoid, scale=1.5957691216057308), reads=[bta], writes=[btb])
    if accum is None:
        S.add("dve", lambda e: e.tensor_tensor(out=out, in0=xin, in1=tb, op=ALU.mult), reads=[bx, btb], writes=[bout])
    else:
        acc_ap, bacc = accum
        S.add("dve", lambda e: e.scalar_tensor_tensor(out=out, in0=xin, scalar=1.0, in1=tb, op0=ALU.mult, op1=ALU.mult, accum_out=acc_ap),
              reads=[bx, btb], writes=[bout, bacc])


def transpose16(k, src, bsrc, dst, bdst, ident, bident, rows=128):
    S = k.S
    for half in range(2):
        pt, bpt = k.bank()
        ptb = pt[:].bitcast(BF16)

        def fn(e, half=half, ptb=ptb):
            ins = None
            for j in range(8):
                kc = half * 8 + j
                ins = e.transpose(ptb[:, j * 128:j * 128 + rows], src[:rows, kc * 128:(kc + 1) * 128], ident[:rows, :rows])
            return ins
        S.add("pe", fn, reads=[bsrc, bident], writes=[bpt])
        S.add("act", lambda e, half=half, ptb=ptb: e.copy(out=dst[:, half * 8:(half + 1) * 8, :rows],
                                                         in_=ptb.rearrange("p (j t) -> p j t", t=128)[:, :, :rows]), reads=[bpt], writes=[bdst])


def load_w16(k, w_sb, bw, w_d, nk=16, eng="pool"):
    k.load(w_sb[:, 0:nk, :], w_d.rearrange("(kc p) n -> p kc n", p=128), bw, eng=eng)


def build_L3a(k=None):
    own = k is None
    k = k or KB()
    S = k.S
    yf_d = k.din("yf", [1024, D]); yr_d = k.din("yr", [1024, D]); u_d = k.din("u", [1024, D])
    dB_d = k.din("dB", [128, D]); bg_d = k.din("bgluB", [128, D])
    wg_d = k.din("w_glu", [D, D]); wo_d = k.din("w_out", [D, D]); ident_d = k.din("ident", [128, 128])
    ml_d = k.dout("ml", [1024, D])
    ident, bident = V_(k, "ident_sb", [128, 128], BF16); k.load(ident[:], ident_d, bident, eng="pool")
    dB, bdB = V_(k, "dB_sb", [128, D]); k.load(dB[:], dB_d, bdB)
    bg, bbg = V_(k, "bg_sb", [128, D]); k.load(bg[:], bg_d, bbg)
    wg, bwg = V_(k, "wg", [128, 16, D], BF16); load_w16(k, wg, bwg, wg_d)
    wo, bwo = V_(k, "wo", [128, 16, D], BF16); load_w16(k, wo, bwo, wo_d)
    Y = [V_(k, f"Y{i}", [128, D]) for i in range(2)]
    T, bT = V_(k, "T", [128, D])
    T2, bT2 = V_(k, "T2", [128, D])
    g, bg_ = V_(k, "g", [128, D], BF16)
    gT, bgT = V_(k, "gT", [128, 16, 128], BF16)
    zs, bzs = V_(k, "zs", [128, D])
    v, bv = V_(k, "v", [128, D], BF16)
    vT, bvT = V_(k, "vT", [128, 16, 128], BF16)
    mlo, bmlo = zs, bzs
    for i in range(8):
        y, by = Y[i % 2]
        sl = slice(i * 128, (i + 1) * 128)
        k.load(y[:], yf_d[sl, :], by, eng="sp")
        k.load(T[:], yr_d[sl, :], bT, eng="sp")
        S.add("dve", lambda e, y=y: e.tensor_tensor(out=y[:], in0=y[:], in1=T[:], op=ALU.add), reads=[by, bT], writes=[by])
        k.load(T[:], u_d[sl, :], bT, eng="sp")
        S.add("pool", lambda e: e.tensor_tensor(out=T[:], in0=T[:], in1=dB[:], op=ALU.mult), reads=[bT, bdB], writes=[bT])
        S.add("dve", lambda e, y=y: e.tensor_tensor(out=y[:], in0=y[:], in1=T[:], op=ALU.add), reads=[by, bT], writes=[by])
        gelu_tanh(k, g[:], bg_, y[:], by, (T[:], bT), (T2[:], bT2))
        transpose16(k, g, bg_, gT, bgT, ident, bident)
        for nb in range(4):
            pt, bpt = k.bank()
            cs = slice(nb * 512, (nb + 1) * 512)
            k.mm(pt[:], bpt, [(gT[:, kc, :], wg[:, kc, cs]) for kc in range(16)], reads=[bgT, bwg])
            S.add("dve", lambda e, pt=pt, cs=cs: e.tensor_tensor(out=zs[:, cs], in0=pt[:], in1=bg[:, cs], op=ALU.add), reads=[bpt, bbg], writes=[bzs])
        S.add("act", lambda e: e.activation(out=zs[:], in_=zs[:], func=AF.Sigmoid), reads=[bzs], writes=[bzs])
        S.add("dve", lambda e: e.tensor_tensor(out=v[:], in0=g[:], in1=zs[:], op=ALU.mult), reads=[bg_, bzs], writes=[bv])
        transpose16(k, v, bv, vT, bvT, ident, bident)
        for nb in range(4):
            pt, bpt = k.bank()
            cs = slice(nb * 512, (nb + 1) * 512)
            k.mm(pt[:], bpt, [(vT[:, kc, :], wo[:, kc, cs]) for kc in range(16)], reads=[bvT, bwo])
            S.add("act", lambda e, pt=pt, cs=cs: e.copy(out=mlo[:, cs], in_=pt[:]), reads=[bpt], writes=[bmlo])
        k.store(ml_d[sl, :], mlo[:], bmlo)
    return k.done(own)


def launch_L3a(inp, y_f, y_r, u_lat):
    nc = build_L3a()
    ident = np.eye(128, dtype=np.float32)
    maps = []
    for c in range(NCORES):
        b, q = c // 4, c % 4
        sl = slice(q * 1024, (q + 1) * 1024)
        maps.append(dict(yf=np.ascontiguousarray(y_f[b, sl]), yr=np.ascontiguousarray(y_r[b, sl]), u=np.ascontiguousarray(u_lat[b, sl]),
                         dB=bc(inp["s5_d"][0]), bgluB=bc(inp["s5_b_glu"][0]), w_glu=inp["s5_w_glu"][0], w_out=inp["s5_w_out"][0], ident=ident))
    res = run(nc, maps)
    ml = np.zeros((2, 4096, D), np.float32)
    for c in range(NCORES):
        ml[c // 4, (c % 4) * 1024:(c % 4 + 1) * 1024] = res[c]["ml"]
    return ml


def build_resnorm(k=None):
    own = k is None
    k = k or KB()
    S = k.S
    x_d = k.din("x", [1024, D]); ml_d = k.din("ml", [1024, D]); ccT_d = k.din("ccT", [128, 16, 1])
    wmod_d = k.din("wmod", [D, 3 * D]); bmodB_d = k.din("bmodB", [128, 3 * D]); gF_d = k.din("gF", [128, D])
    x1_d = k.dout("x1", [1024, D]); hl_d = k.dout("hl", [1024, D])
    ccT, bcc = V_(k, "ccT_sb", [128, 16, 1]); k.load(ccT[:], ccT_d, bcc)
    gF, bgF = V_(k, "gF_sb", [128, D]); k.load(gF[:], gF_d, bgF)
    M = [V_(k, f"M{j}", [128, D]) for j in range(3)]
    outs = [lambda c0, n: (M[c0 // D][0][:, c0 % D:c0 % D + n], M[c0 // D][1])]
    build_mod(k, ccT, bcc, wmod_d, bmodB_d, 3 * D, 1, outs, NW=512, nbuf=3)
    A, bA = M[2]
    B, bB = M[1]
    S.add("dve", lambda e: e.scalar_tensor_tensor(out=A[:], in0=A[:], scalar=1.0, in1=gF[:], op0=ALU.add, op1=ALU.mult), reads=[bA, bgF], writes=[bA])
    X = [V_(k, f"X{i}", [128, D]) for i in range(2)]
    Ml = [V_(k, f"Ml{i}", [128, D]) for i in range(2)]
    H = [V_(k, f"H{i}", [128, D]) for i in range(2)]
    junk, bjunk = V_(k, "junk", [128, D], BF16)
    SS = [V_(k, f"ss{i}", [128, 1]) for i in range(2)]
    for i in range(8):
        sl = slice(i * 128, (i + 1) * 128)
        x, bx = X[i % 2]; m, bm = Ml[i % 2]; h, bh = H[i % 2]; ss, bss = SS[i % 2]
        k.load(x[:], x_d[sl, :], bx, eng="sp")
        k.load(m[:], ml_d[sl, :], bm, eng="pool")
        S.add("pool", lambda e, m=m: e.tensor_tensor(out=m[:], in0=m[:], in1=M[0][0][:], op=ALU.mult), reads=[bm, M[0][1]], writes=[bm])
        S.add("dve", lambda e, x=x, m=m: e.tensor_tensor(out=x[:], in0=x[:], in1=m[:], op=ALU.add), reads=[bx, bm], writes=[bx])
        k.store(x1_d[sl, :], x[:], bx)
        S.add("act", lambda e, x=x, ss=ss: e.activation(out=junk[:], in_=x[:], func=AF.Square, accum_out=ss[:, 0:1]), reads=[bx], writes=[bjunk, bss])
        S.add("dve", lambda e, ss=ss: e.tensor_scalar(out=ss[:], in0=ss[:], scalar1=1.0 / D, scalar2=1e-6, op0=ALU.mult, op1=ALU.add), reads=[bss], writes=[bss])
        S.add("act", lambda e, ss=ss: e.sqrt(out=ss[:], in_=ss[:]), reads=[bss], writes=[bss])
        S.add("dve", lambda e, ss=ss: e.reciprocal(out=ss[:], in_=ss[:]), reads=[bss], writes=[bss])
        S.add("dve", lambda e, x=x, h=h, ss=ss: e.scalar_tensor_tensor(out=h[:], in0=x[:], scalar=ss[:, 0:1], in1=A[:], op0=ALU.mult, op1=ALU.mult),
              reads=[bx, bss, bA], writes=[bh])
        S.add("pool", lambda e, h=h: e.tensor_tensor(out=h[:], in0=h[:], in1=B[:], op=ALU.add), reads=[bh, bB], writes=[bh])
        k.store(hl_d[sl, :], h[:], bh)
    return k.done(own)


def launch_resnorm(inp, x, ml, layer, gF, mod_cols):
    nc = build_resnorm()
    cols = np.concatenate([np.arange(j * D, (j + 1) * D) for j in mod_cols])
    wmod = np.ascontiguousarray(inp["w_mod"][layer][:, cols])
    bmodB = bc(inp["b_mod"][layer][cols])
    maps = []
    for c in range(NCORES):
        b, q = c // 4, c % 4
        sl = slice(q * 1024, (q + 1) * 1024)
        ccT = np.ascontiguousarray(inp["c"][b].reshape(1, 16, 128).transpose(2, 1, 0))
        maps.append(dict(x=np.ascontiguousarray(x[b, sl]), ml=np.ascontiguousarray(ml[b, sl]), ccT=ccT, wmod=wmod, bmodB=bmodB, gF=bc(gF)))
    res = run(nc, maps)
    x1 = np.zeros((2, 4096, D), np.float32); hl = np.zeros((2, 4096, D), np.float32)
    for c in range(NCORES):
        x1[c // 4, (c % 4) * 1024:(c % 4 + 1) * 1024] = res[c]["x1"]
        hl[c // 4, (c % 4) * 1024:(c % 4 + 1) * 1024] = res[c]["hl"]
    return x1, hl


FF = 5632


def build_ffn(final, k=None):
    own = k is None
    k = k or KB()
    S = k.S
    hl_d = k.din("hlT", [D, 1152]); x_d = k.din("x", [1024, D]); ccT_d = k.din("ccT", [128, 16, 1])
    wmod_d = k.din("wmod", [D, D]); bmodB_d = k.din("bmodB", [128, D])
    wup_d = k.din("w_up", [D, 2 * FF]); cw_d = k.din("cw", [128, 88, 9]); cb_d = k.din("cb", [128, 88])
    wdn_d = k.din("w_down", [FF, D])
    xo_d = k.dout("xo", [1024, D])
    if final:
        gfin_d = k.din("gfin", [128, D])
        xs_d = k.scratch(k.prefix + "xs_scratch", [1024, D])
        bxs = Buf("xs")
    gate, bgate = V_(k, "gate", [128, D])
    with k.stage(k.prefix + "m_", io=k.io):
        ccT, bcc = V_(k, "ccT_sb", [128, 16, 1]); k.load(ccT[:], ccT_d, bcc)
        build_mod(k, ccT, bcc, wmod_d, bmodB_d, D, 1, [lambda c0, n: (gate[:, c0:c0 + n], bgate)], NW=256, nbuf=3)
    cw, bcw = V_(k, "cw_sb", [128, 88, 9]); k.load(cw[:], cw_d, bcw)
    cb, bcb = V_(k, "cb_sb", [128, 88]); k.load(cb[:], cb_d, bcb)
    hlT, bhlT = V_(k, "hlT_sb", [128, 16, 1152], BF16)
    k.load(hlT[:], hl_d.rearrange("(kc p) n -> p kc n", p=128), bhlT, eng="pool")
    aT, baT = V_(k, "aT", [128, 44, 512], BF16)
    NWD = 256
    wds = [V_(k, f"wd{i}", [128, 44, NWD], BF16) for i in range(2)]
    PF = 2
    wu = [(k.sb(f"wu{i}", [128, 16, 256], BF16), Buf(), Buf()) for i in range(PF + 1)]

    def issue_wu(n_it):
        j_ = n_it % 44
        w_, bg__, bv__ = wu[n_it % (PF + 1)]
        k.load(w_[:, :, 0:128], wup_d[:, j_ * 128:(j_ + 1) * 128].rearrange("(kc p) n -> p kc n", p=128), bg__, eng="pool")
        k.load(w_[:, :, 128:256], wup_d[:, FF + j_ * 128:FF + (j_ + 1) * 128].rearrange("(kc p) n -> p kc n", p=128), bv__, eng="pool")
    zp = [[V_(k, f"zp{i}{c}", [128, 10, 66]) for c in range(2)] for i in range(2)]
    for i in range(2):
        for c in range(2):
            S.add("pool", lambda e, t=zp[i][c][0]: e.memset(t[:], 0.0), writes=[zp[i][c][1]])
    acc = [[V_(k, f"acc{i}{c}", [128, 8, 64]) for c in range(2)] for i in range(2)]
    sgt = [V_(k, f"sg{i}", [128, 8, 64]) for i in range(2)]
    ptmps = [V_(k, f"ptmp{i}", [128, 8, 64]) for i in range(4)]
    tcnt = [0]
    xc = [V_(k, f"xc{i}", [128, NWD]) for i in range(2)]
    Tt = [V_(k, f"Tt{i}", [128, NWD]) for i in range(2)]
    it = 0
    for h in range(2):
        e0 = h * 8 * 64
        for j in range(44):
            if it == 0:
                for pf in range(PF):
                    issue_wu(pf)
            if it + PF < 88:
                issue_wu(it + PF)
            w, bwg_, bwv_ = wu[it % (PF + 1)]
            for c in range(2):
                z, bz = zp[it % 2][c]
                a_, ba_ = acc[it % 2][c]
                ch = j + 44 * c
                for (c0, n, r0) in ((0, 512, 0), (512, 128, 8)):
                    pt, bpt = k.bank()
                    k.mm(pt[:, :n], bpt, [(w[:, kc, c * 128:(c + 1) * 128], hlT[:, kc, e0 + c0:e0 + c0 + n]) for kc in range(16)],
                         reads=[(bwg_, bwv_)[c], bhlT])
                    S.add("act", lambda e, pt=pt, z=z, n=n, r0=r0: e.copy(out=z[:, r0:r0 + n // 64, 1:65],
                                                                          in_=pt[:, :n].rearrange("p (r c) -> p r c", c=64)), reads=[bpt], writes=[bz])
                if c == 0:
                    S.add("dve", lambda e, z=z, a_=a_, ch=ch: e.tensor_scalar(out=a_[:], in0=z[:, 0:8, 0:64], scalar1=cw[:, ch, 0:1], scalar2=cb[:, ch:ch + 1],
                                                                            op0=ALU.mult, op1=ALU.add), reads=[bz, bcw, bcb], writes=[ba_])
                    for t in range(1, 9):
                        di, dj = divmod(t, 3)
                        S.add("dve", lambda e, z=z, a_=a_, ch=ch, t=t, di=di, dj=dj: e.scalar_tensor_tensor(
                            out=a_[:], in0=z[:, di:di + 8, dj:dj + 64], scalar=cw[:, ch, t:t + 1], in1=a_[:], op0=ALU.mult, op1=ALU.add),
                            reads=[bz, bcw, ba_], writes=[ba_])
                else:
                    S.add("act", lambda e, z=z, a_=a_, ch=ch: e.activation(out=a_[:], in_=z[:, 0:8, 0:64], func=AF.Identity, bias=cb[:, ch:ch + 1],
                                                                         scale=cw[:, ch, 0:1]), reads=[bz, bcw, bcb], writes=[ba_])
                    for t in range(1, 9):
                        di, dj = divmod(t, 3)
                        pt_, bpt_ = ptmps[tcnt[0] % 4]
                        tcnt[0] += 1
                        S.add("act", lambda e, z=z, ch=ch, t=t, di=di, dj=dj, pt_=pt_: e.activation(out=pt_[:], in_=z[:, di:di + 8, dj:dj + 64], func=AF.Copy,
                                                                                                 scale=cw[:, ch, t:t + 1]), reads=[bz, bcw], writes=[bpt_])
                        S.add("dve", lambda e, a_=a_, pt_=pt_: e.tensor_tensor(out=a_[:], in0=a_[:], in1=pt_[:], op=ALU.add), reads=[ba_, bpt_], writes=[ba_])
            ag, bag = acc[it % 2][0]
            av, bav = acc[it % 2][1]
            sg, bsg = sgt[it % 2]
            S.add("act", lambda e, ag=ag, sg=sg: e.activation(out=sg[:], in_=ag[:], func=AF.Silu), reads=[bag], writes=[bsg])
            S.add("dve", lambda e, sg=sg, av=av, j=j: e.tensor_tensor(out=aT[:, j, :].rearrange("p (r c) -> p r c", c=64), in0=sg[:], in1=av[:], op=ALU.mult),
                  reads=[bsg, bav], writes=[baT])
            it += 1
        for nb in range(D // NWD):
            cs = slice(nb * NWD, (nb + 1) * NWD)
            wd, bwd = wds[nb % 2]
            k.load(wd[:], wdn_d[:, cs].rearrange("(j p) n -> p j n", p=128), bwd, eng="pool")
            for tt_ in range(4):
                gi = h * 4 + tt_
                rs = slice(gi * 128, (gi + 1) * 128)
                x, bx = xc[(nb * 4 + tt_) % 2]
                T, bT = Tt[(nb * 4 + tt_) % 2]
                k.load(x[:], x_d[rs, cs], bx, eng="sp")
                pt, bpt = k.bank()
                k.mm(pt[:, :NWD], bpt, [(aT[:, j, tt_ * 128:(tt_ + 1) * 128], wd[:, j, :]) for j in range(44)], reads=[baT, bwd])
                S.add("dve", lambda e, pt=pt, T=T, cs=cs: e.tensor_tensor(out=T[:], in0=pt[:, :NWD], in1=gate[:, cs], op=ALU.mult), reads=[bpt, bgate], writes=[bT])
                S.add("pool", lambda e, T=T, x=x: e.tensor_tensor(out=T[:], in0=T[:], in1=x[:], op=ALU.add), reads=[bT, bx], writes=[bT])
                if final:
                    S.add("sp", lambda e, T=T, rs=rs, cs=cs: e.dma_start(out=xs_d[rs, cs], in_=T[:]), reads=[bT], writes=[bxs], dma=True)
                else:
                    k.store(xo_d[rs, cs], T[:], bT)
    if final:
        gfin, bgfin = V_(k, "gfin_sb", [128, D]); k.load(gfin[:], gfin_d, bgfin)
        XR = [(aT[:].rearrange("p j n -> p (j n)").bitcast(F32)[:, 0:D], baT)] * 2
        bjunk = bhlT
        SS = [V_(k, f"ss{i}", [128, 1]) for i in range(2)]
        for i in range(8):
            rs = slice(i * 128, (i + 1) * 128)
            x, bx = XR[i % 2]; ss, bss = SS[i % 2]
            k.load(x[:], xs_d[rs, :], bx, eng="sp", reads=[bxs])
            S.add("act", lambda e, x=x, ss=ss: e.activation(out=hlT[:, 0:2, 0:1024], in_=x[:].rearrange("p (a b) -> p a b", a=2), func=AF.Square,
                                                           accum_out=ss[:, 0:1]), reads=[bx], writes=[bjunk, bss])
            S.add("dve", lambda e, ss=ss: e.tensor_scalar(out=ss[:], in0=ss[:], scalar1=1.0 / D, scalar2=1e-6, op0=ALU.mult, op1=ALU.add), reads=[bss], writes=[bss])
            S.add("act", lambda e, ss=ss: e.sqrt(out=ss[:], in_=ss[:]), reads=[bss], writes=[bss])
            S.add("dve", lambda e, ss=ss: e.reciprocal(out=ss[:], in_=ss[:]), reads=[bss], writes=[bss])
            S.add("dve", lambda e, x=x, ss=ss: e.scalar_tensor_tensor(out=x[:], in0=x[:], scalar=ss[:, 0:1], in1=gfin[:], op0=ALU.mult, op1=ALU.mult),
                  reads=[bx, bss, bgfin], writes=[bx])
            k.store(xo_d[rs, :], x[:], bx)
    return k.done(own)


def launch_ffn(inp, x, hl, layer, final):
    nc = build_ffn(final)
    cols = np.arange(5 * D, 6 * D)
    wmod = np.ascontiguousarray(inp["w_mod"][layer][:, cols]); bmodB = bc(inp["b_mod"][layer][cols])
    cw = np.ascontiguousarray(inp["ffn_conv_w"][layer].reshape(9, 88, 128).transpose(2, 1, 0))
    cb = np.ascontiguousarray(inp["ffn_conv_b"][layer].reshape(88, 128).T)
    maps = []
    for c in range(NCORES):
        b, q = c // 4, c % 4
        ext = np.zeros((1152, D), np.float32)
        lo, hi = q * 1024 - 64, q * 1024 + 1024 + 64
        slo, shi = max(lo, 0), min(hi, 4096)
        ext[slo - lo:shi - lo] = hl[b, slo:shi]
        m = dict(hlT=np.ascontiguousarray(ext.T), x=np.ascontiguousarray(x[b, q * 1024:(q + 1) * 1024]),
                 ccT=np.ascontiguousarray(inp["c"][b].reshape(1, 16, 128).transpose(2, 1, 0)), wmod=wmod, bmodB=bmodB,
                 w_up=inp["ffn_w_up"][layer], cw=cw, cb=cb, w_down=inp["ffn_w_down"][layer])
        if final:
            m["gfin"] = bc(inp["final_norm_g"])
        maps.append(m)
    res = run(nc, maps)
    xo = np.zeros((2, 4096, D), np.float32)
    for c in range(NCORES):
        xo[c // 4, (c % 4) * 1024:(c % 4 + 1) * 1024] = res[c]["xo"]
    return xo


def rms_mod(k, x, bx, A, bA, B, bB, out, bout, junk, ss, t1):
    S = k.S
    S.add("act", lambda e: e.activation(out=junk[0][:], in_=x, func=AF.Square, accum_out=ss[0][:, 0:1]), reads=[bx], writes=[junk[1], ss[1]])
    S.add("dve", lambda e: e.tensor_scalar(out=ss[0][:], in0=ss[0][:], scalar1=1.0 / D, scalar2=1e-6, op0=ALU.mult, op1=ALU.add), reads=[ss[1]], writes=[ss[1]])
    S.add("act", lambda e: e.sqrt(out=ss[0][:], in_=ss[0][:]), reads=[ss[1]], writes=[ss[1]])
    S.add("dve", lambda e: e.reciprocal(out=ss[0][:], in_=ss[0][:]), reads=[ss[1]], writes=[ss[1]])
    S.add("dve", lambda e: e.scalar_tensor_tensor(out=t1[0][:], in0=x, scalar=ss[0][:, 0:1], in1=A, op0=ALU.mult, op1=ALU.mult),
          reads=[bx, ss[1], bA], writes=[t1[1]])
    S.add("dve", lambda e: e.tensor_tensor(out=out, in0=t1[0][:], in1=B, op=ALU.add), reads=[t1[1], bB], writes=[bout])


def build_sgu(k=None):
    own = k is None
    k = k or KB()
    S = k.S
    x_d = k.din("x", [1024, D]); ccT_d = k.din("ccT", [128, 16, 1])
    wmA_d = k.din("wmodA", [D, 2 * D]); bmA_d = k.din("bmodBA", [128, 2 * D]); gM_d = k.din("gMix", [128, D])
    wmB_d = k.din("wmodB", [D, 3 * D]); bmB_d = k.din("bmodBB", [128, 3 * D]); gF_d = k.din("gF", [128, D])
    win_d = k.din("w_in", [D, 8192]); lng_d = k.din("lngT", [128, 32]); lnb_d = k.din("lnbT", [128, 32])
    wsT_d = k.din("wsT", [128, 16, 128]); bsB_d = k.din("bsB", [128, 16, 128]); wo_d = k.din("w_out", [4096, D])
    ident_d = k.din("ident", [128, 128]); ones_d = k.din("ones", [128, 128])
    x3_d = k.dout("x3", [1024, D]); hl_d = k.dout("hl", [1024, D])
    bx3 = Buf("x3d")
    ident, bident = V_(k, "ident_sb", [128, 128], BF16); k.load(ident[:], ident_d, bident, eng="pool")
    ones, bones = V_(k, "ones_sb", [128, 128], BF16); k.load(ones[:], ones_d, bones, eng="pool")
    ccT, bcc = V_(k, "ccT_sb", [128, 16, 1]); k.load(ccT[:], ccT_d, bcc)
    M0 = V_(k, "M0", [128, D]); M1 = V_(k, "M1", [128, D]); G2 = V_(k, "G2", [128, D])
    t1 = V_(k, "t1", [128, D]); junk = V_(k, "junk", [128, D], BF16); ss = V_(k, "ss", [128, 1])
    k.load(t1[0][:], gM_d, t1[1])
    sc = k.sb("sc", [128, 16, 1]); bsc = Buf()
    scb = k.sb("scb", [128, 16, 1, 128], BF16); bscb = Buf()
    S.add("act", lambda e: e.activation(out=sc[:], in_=ccT[:], func=AF.Silu), reads=[bcc], writes=[bsc])
    S.add("dve", lambda e: e.tensor_copy(out=scb[:], in_=sc[:].unsqueeze(3).to_broadcast([128, 16, 1, 128])), reads=[bsc], writes=[bscb])
    NW = 128
    wst = [(k.sb(f"wst{i}", [128, 16, NW], BF16), Buf()) for i in range(2)]
    bmt = [(k.sb(f"bmt{i}", [128, NW]), Buf()) for i in range(2)]

    def mod(wmod_d, bmodB_d, ncols, outf):
        for nb in range(ncols // NW):
            w, bw = wst[nb % 2]
            bm, bbm = bmt[nb % 2]
            k.load(w[:], wmod_d[:, nb * NW:(nb + 1) * NW].rearrange("(kc p) n -> p kc n", p=128), bw, eng="pool")
            k.load(bm[:], bmodB_d[:, nb * NW:(nb + 1) * NW], bbm, eng="sp")
            pt, bpt = k.bank()
            k.mm(pt[:, :NW], bpt, [(scb[:, kc, 0, :], w[:, kc, :]) for kc in range(16)], reads=[bscb, bw])
            dst, bdst = outf(nb * NW, NW)
            S.add("dve", lambda e, pt=pt, dst=dst, bm=bm: e.tensor_tensor(out=dst, in0=pt[:, :NW], in1=bm[:], op=ALU.add), reads=[bpt, bbm], writes=[bdst])
    MA = [M0, M1]
    mod(wmA_d, bmA_d, 2 * D, lambda c0, n: (MA[c0 // D][0][:, c0 % D:c0 % D + n], MA[c0 // D][1]))
    S.add("dve", lambda e: e.scalar_tensor_tensor(out=M1[0][:], in0=M1[0][:], scalar=1.0, in1=t1[0][:], op0=ALU.add, op1=ALU.mult),
          reads=[M1[1], t1[1]], writes=[M1[1]])
    mod(wmB_d[:, 0:D], bmB_d[:, 0:D], D, lambda c0, n: (G2[0][:, c0:c0 + n], G2[1]))
    lng = V_(k, "lng", [128, 32]); k.load(lng[0][:], lng_d, lng[1])
    lnb = V_(k, "lnb", [128, 32]); k.load(lnb[0][:], lnb_d, lnb[1])
    wsT = V_(k, "wsT_sb", [128, 16, 128], BF16); k.load(wsT[0][:], wsT_d, wsT[1], eng="pool")
    bsB = V_(k, "bsB_sb", [128, 16, 128]); k.load(bsB[0][:], bsB_d, bsB[1])
    rsB = V_(k, "rsB", [128, 16, 128])
    for g4 in range(4):
        pt, bpt = k.bank()
        k.mm(pt[:], bpt, [(ones[:], wsT[0][:, g4 * 4:(g4 + 1) * 4, :].rearrange("p g q -> p (g q)"))], reads=[bones, wsT[1]])
        S.add("act", lambda e, pt=pt, g4=g4: e.copy(out=rsB[0][:, g4 * 4:(g4 + 1) * 4, :].rearrange("p g q -> p (g q)"), in_=pt[:]), reads=[bpt], writes=[rsB[1]])
    hlT = V_(k, "hlT", [128, 16, 512], BF16)
    hlb = V_(k, "hlb", [128, D], BF16)
    vn = k.sb("vn", [128, 4, 4096], BF16)
    bvn = [Buf(f"vn{c}") for c in range(32)]
    X = V_(k, "X", [128, D])
    wv = V_(k, "wv", [128, 16, 512], BF16)
    wu = [V_(k, f"wu{i}", [128, 16, 128], BF16) for i in range(2)]
    wo = V_(k, "wo", [128, 32, 256], BF16)
    ga = V_(k, "ga", [128, 512]); gb = V_(k, "gb", [128, 512])
    sums = V_(k, "sums", [128, 4, 8]); sqs = V_(k, "sqs", [128, 4, 8])
    mean = V_(k, "mean", [128, 4]); var = V_(k, "var", [128, 4]); msq = V_(k, "msq", [128, 4])
    uTb = V_(k, "uTb", [128, 512]); svt = V_(k, "svt", [128, 4, 128]); bfull = V_(k, "bfull", [128, 128])
    xc = [V_(k, f"xc{i}", [128, 256]) for i in range(2)]
    Tt = [V_(k, f"Tt{i}", [128, 256]) for i in range(2)]
    for hf in range(2):
        for i in range(4):
            rs = slice(hf * 512 + i * 128, hf * 512 + (i + 1) * 128)
            k.load(X[0][:], x_d[rs, :], X[1], eng="sp")
            rms_mod(k, X[0][:], X[1], M1[0][:], M1[1], M0[0][:], M0[1], hlb[0][:], hlb[1], junk, ss, t1)
            transpose16(k, hlb[0], hlb[1], hlT[0][:, :, i * 128:(i + 1) * 128], hlT[1], ident, bident)
        for cbi in range(8):
            load_w16(k, wv[0], wv[1], win_d[:, 4096 + cbi * 512:4096 + (cbi + 1) * 512])
            for i in range(4):
                pt, bpt = k.bank()
                k.mm(pt[:], bpt, [(hlT[0][:, kc, i * 128:(i + 1) * 128], wv[0][:, kc, :]) for kc in range(16)], reads=[hlT[1], wv[1]])
                blks = bvn[cbi * 4:(cbi + 1) * 4]
                dst = vn[:, i, cbi * 512:(cbi + 1) * 512]
                S_ = k.S
                S_.add("act", lambda e, pt=pt: e.activation(out=ga[0][:], in_=pt[:], func=AF.Square), reads=[bpt], writes=[ga[1]])
                S_.add("dve", lambda e: e.tensor_scalar(out=ga[0][:], in0=ga[0][:], scalar1=0.044715, scalar2=1.0, op0=ALU.mult, op1=ALU.add), reads=[ga[1]], writes=[ga[1]])
                S_.add("dve", lambda e, pt=pt: e.tensor_tensor(out=ga[0][:], in0=ga[0][:], in1=pt[:], op=ALU.mult), reads=[ga[1], bpt], writes=[ga[1]])
                S_.add("act", lambda e: e.activation(out=gb[0][:], in_=ga[0][:], func=AF.Sigmoid, scale=1.5957691216057308), reads=[ga[1]], writes=[gb[1]])
                S_.add("dve", lambda e, pt=pt, dst=dst, i=i, cbi=cbi: e.scalar_tensor_tensor(out=dst, in0=pt[:], scalar=1.0, in1=gb[0][:], op0=ALU.mult, op1=ALU.mult,
                                                                                         accum_out=sums[0][:, i, cbi:cbi + 1]),
                       reads=[bpt, gb[1]], writes=blks + [sums[1]])
                S_.add("act", lambda e, dst=dst, i=i, cbi=cbi: e.activation(out=ga[0][:], in_=dst, func=AF.Square, accum_out=sqs[0][:, i, cbi:cbi + 1]),
                       reads=blks, writes=[ga[1], sqs[1]])
        S.add("dve", lambda e: e.tensor_reduce(out=mean[0][:], in_=sums[0][:], axis=mybir.AxisListType.X, op=ALU.add), reads=[sums[1]], writes=[mean[1]])
        S.add("dve", lambda e: e.tensor_reduce(out=var[0][:], in_=sqs[0][:], axis=mybir.AxisListType.X, op=ALU.add), reads=[sqs[1]], writes=[var[1]])
        S.add("dve", lambda e: e.tensor_single_scalar(out=mean[0][:], in_=mean[0][:], scalar=1.0 / 4096, op=ALU.mult), reads=[mean[1]], writes=[mean[1]])
        S.add("dve", lambda e: e.tensor_tensor(out=msq[0][:], in0=mean[0][:], in1=mean[0][:], op=ALU.mult), reads=[mean[1]], writes=[msq[1]])
        S.add("dve", lambda e: e.scalar_tensor_tensor(out=var[0][:], in0=var[0][:], scalar=1.0 / 4096, in1=msq[0][:], op0=ALU.mult, op1=ALU.subtract),
              reads=[var[1], msq[1]], writes=[var[1]])
        S.add("dve", lambda e: e.tensor_single_scalar(out=var[0][:], in_=var[0][:], scalar=1e-5, op=ALU.add), reads=[var[1]], writes=[var[1]])
        S.add("act", lambda e: e.sqrt(out=var[0][:], in_=var[0][:]), reads=[var[1]], writes=[var[1]])
        S.add("dve", lambda e: e.reciprocal(out=var[0][:], in_=var[0][:]), reads=[var[1]], writes=[var[1]])
        for i in range(4):
            S.add("dve", lambda e, i=i: e.tensor_scalar(out=vn[:, i, :], in0=vn[:, i, :], scalar1=mean[0][:, i:i + 1], scalar2=var[0][:, i:i + 1],
                                                       op0=ALU.subtract, op1=ALU.mult), reads=bvn + [mean[1], var[1]], writes=bvn)
        for cbk in range(32):
            g = cbk // 2
            w, bw = wu[cbk % 2]
            load_w16(k, w, bw, win_d[:, cbk * 128:(cbk + 1) * 128])
            pt, bpt = k.bank()
            k.mm(pt[:], bpt, [(w[:, kc, :], hlT[0][:, kc, :]) for kc in range(16)], reads=[bw, hlT[1]])
            gelu_tanh(k, uTb[0][:], uTb[1], pt[:], bpt, (ga[0][:], ga[1]), (gb[0][:], gb[1]))
            pt2, bpt2 = k.bank()

            def fn(e, pt2=pt2, cbk=cbk, g=g):
                ins = None
                for i in range(4):
                    ins = e.matmul(pt2[:, i * 128:(i + 1) * 128], vn[:, i, cbk * 128:(cbk + 1) * 128], wsT[0][:, g, :], start=True, stop=True)
                return ins
            S.add("pe", fn, reads=[bvn[cbk], wsT[1]], writes=[bpt2])
            S.add("dve", lambda e, cbk=cbk, g=g: e.scalar_tensor_tensor(out=bfull[0][:], in0=rsB[0][:, g, :], scalar=lnb[0][:, cbk:cbk + 1], in1=bsB[0][:, g, :],
                                                                      op0=ALU.mult, op1=ALU.add), reads=[rsB[1], lnb[1], bsB[1]], writes=[bfull[1]])
            S.add("dve", lambda e, pt2=pt2, cbk=cbk: e.scalar_tensor_tensor(out=svt[0][:], in0=pt2[:].rearrange("p (i q) -> p i q", q=128), scalar=lng[0][:, cbk:cbk + 1],
                                                                          in1=bfull[0][:].unsqueeze(1).to_broadcast([128, 4, 128]), op0=ALU.mult, op1=ALU.add),
                  reads=[bpt2, lng[1], bfull[1]], writes=[svt[1]])
            S.add("dve", lambda e, cbk=cbk: e.tensor_tensor(out=vn[:, :, cbk * 128:(cbk + 1) * 128], in0=uTb[0][:].rearrange("p (i q) -> p i q", q=128),
                                                            in1=svt[0][:], op=ALU.mult), reads=[uTb[1], svt[1]], writes=[bvn[cbk]])
        for nb in range(8):
            cs = slice(nb * 256, (nb + 1) * 256)
            load_w16(k, wo[0], wo[1], wo_d[:, cs], nk=32)
            for i in range(4):
                rs = slice(hf * 512 + i * 128, hf * 512 + (i + 1) * 128)
                x, bx = xc[(nb * 4 + i) % 2]
                T, bT = Tt[(nb * 4 + i) % 2]
                k.load(x[:], x_d[rs, cs], bx, eng="sp")
                pt, bpt = k.bank()
                k.mm(pt[:, :256], bpt, [(vn[:, i, cbk * 128:(cbk + 1) * 128], wo[0][:, cbk, :]) for cbk in range(32)], reads=bvn + [wo[1]])
                S.add("dve", lambda e, pt=pt, T=T, cs=cs: e.tensor_tensor(out=T[:], in0=pt[:, :256], in1=G2[0][:, cs], op=ALU.mult), reads=[bpt, G2[1]], writes=[bT])
                S.add("pool", lambda e, T=T, x=x: e.tensor_tensor(out=T[:], in0=T[:], in1=x[:], op=ALU.add), reads=[bT, bx], writes=[bT])
                op = S.add("sp", lambda e, T=T, rs=rs, cs=cs: e.dma_start(out=x3_d[rs, cs], in_=T[:]), reads=[bT], writes=[bx3], dma=True)
                k.finals.append(op)
    k.load(t1[0][:], gF_d, t1[1])
    mod(wmB_d[:, D:3 * D], bmB_d[:, D:3 * D], 2 * D, lambda c0, n: (MA[c0 // D][0][:, c0 % D:c0 % D + n], MA[c0 // D][1]))
    S.add("dve", lambda e: e.scalar_tensor_tensor(out=M1[0][:], in0=M1[0][:], scalar=1.0, in1=t1[0][:], op0=ALU.add, op1=ALU.mult),
          reads=[M1[1], t1[1]], writes=[M1[1]])
    HO = V_(k, "HO", [128, D])
    for i in range(8):
        rs = slice(i * 128, (i + 1) * 128)
        k.load(X[0][:], x3_d[rs, :], X[1], eng="sp", reads=[bx3])
        rms_mod(k, X[0][:], X[1], M1[0][:], M1[1], M0[0][:], M0[1], HO[0][:], HO[1], junk, ss, t1)
        k.store(hl_d[rs, :], HO[0][:], HO[1])
    return k.done(own)


def launch_sgu(inp, x2):
    nc = build_sgu()
    L = 1
    def mcols(js):
        cols = np.concatenate([np.arange(j * D, (j + 1) * D) for j in js])
        return np.ascontiguousarray(inp["w_mod"][L][:, cols]), bc(inp["b_mod"][L][cols])
    wmA, bmA = mcols((0, 1)); wmB, bmB = mcols((2, 3, 4))
    lngT = np.ascontiguousarray(inp["sgu_ln_g"][0].reshape(32, 128).T); lnbT = np.ascontiguousarray(inp["sgu_ln_b"][0].reshape(32, 128).T)
    wsT = np.ascontiguousarray(inp["sgu_w_s"][0].transpose(2, 0, 1))
    bsB = np.ascontiguousarray(np.broadcast_to(inp["sgu_b_s"][0][None], (128, 16, 128)))
    maps = []
    for c in range(NCORES):
        b, q = c // 4, c % 4
        maps.append(dict(x=np.ascontiguousarray(x2[b, q * 1024:(q + 1) * 1024]), ccT=np.ascontiguousarray(inp["c"][b].reshape(1, 16, 128).transpose(2, 1, 0)),
                         wmodA=wmA, bmodBA=bmA, gMix=bc(inp["mix_norm_g"][L]), wmodB=wmB, bmodBB=bmB, gF=bc(inp["ffn_norm_g"][L]),
                         w_in=inp["sgu_w_in"][0], lngT=lngT, lnbT=lnbT, wsT=wsT, bsB=bsB, w_out=inp["sgu_w_out"][0],
                         ident=np.eye(128, dtype=np.float32), ones=np.ones((128, 128), np.float32)))
    res = run(nc, maps)
    x3 = np.zeros((2, 4096, D), np.float32); hl = np.zeros((2, 4096, D), np.float32)
    for c in range(NCORES):
        x3[c // 4, (c % 4) * 1024:(c % 4 + 1) * 1024] = res[c]["x3"]
        hl[c // 4, (c % 4) * 1024:(c % 4 + 1) * 1024] = res[c]["hl"]
    return x3, hl


def build_L3():
    k = KB()
    ml_s = k.scratch("ml_scratch", [1024, D])
    with k.stage("a_", io={"ml": ml_s}):
        build_L3a(k)
    with k.stage("b_", io={"ml": ml_s}):
        build_resnorm(k)
    return k.finish()


def launch_L3(inp, y_f, y_r, u_lat, x, layer, gF, mod_cols):
    nc = build_L3()
    ident = np.eye(128, dtype=np.float32)
    cols = np.concatenate([np.arange(j * D, (j + 1) * D) for j in mod_cols])
    wmod = np.ascontiguousarray(inp["w_mod"][layer][:, cols])
    bmodB = bc(inp["b_mod"][layer][cols])
    maps = []
    for c in range(NCORES):
        b, q = c // 4, c % 4
        sl = slice(q * 1024, (q + 1) * 1024)
        ccT = np.ascontiguousarray(inp["c"][b].reshape(1, 16, 128).transpose(2, 1, 0))
        maps.append(dict(a_yf=np.ascontiguousarray(y_f[b, sl]), a_yr=np.ascontiguousarray(y_r[b, sl]), a_u=np.ascontiguousarray(u_lat[b, sl]),
                         a_dB=bc(inp["s5_d"][0]), a_bgluB=bc(inp["s5_b_glu"][0]), a_w_glu=inp["s5_w_glu"][0], a_w_out=inp["s5_w_out"][0], a_ident=ident,
                         b_x=np.ascontiguousarray(x[b, sl]), b_ccT=ccT, b_wmod=wmod, b_bmodB=bmodB, b_gF=bc(gF)))
    res = run(nc, maps)
    x1 = np.zeros((2, 4096, D), np.float32); hl = np.zeros((2, 4096, D), np.float32)
    for c in range(NCORES):
        x1[c // 4, (c % 4) * 1024:(c % 4 + 1) * 1024] = res[c]["b_x1"]
        hl[c // 4, (c % 4) * 1024:(c % 4 + 1) * 1024] = res[c]["b_hl"]
    return x1, hl


def build_L4():
    k = KB()
    x2_s = k.scratch("x2_scratch", [1024, D])
    with k.stage("f_", io={"xo": x2_s}):
        build_ffn(False, k)
    with k.stage("s_", io={"x": x2_s}):
        build_sgu(k)
    return k.finish()


def ffn_maps(inp, x, hl, layer, final, prefix=""):
    cols = np.arange(5 * D, 6 * D)
    wmod = np.ascontiguousarray(inp["w_mod"][layer][:, cols]); bmodB = bc(inp["b_mod"][layer][cols])
    cw = np.ascontiguousarray(inp["ffn_conv_w"][layer].reshape(9, 88, 128).transpose(2, 1, 0))
    cb = np.ascontiguousarray(inp["ffn_conv_b"][layer].reshape(88, 128).T)
    maps = []
    for c in range(NCORES):
        b, q = c // 4, c % 4
        ext = np.zeros((1152, D), np.float32)
        lo, hi = q * 1024 - 64, q * 1024 + 1024 + 64
        slo, shi = max(lo, 0), min(hi, 4096)
        ext[slo - lo:shi - lo] = hl[b, slo:shi]
        m = dict(hlT=np.ascontiguousarray(ext.T), x=np.ascontiguousarray(x[b, q * 1024:(q + 1) * 1024]),
                 ccT=np.ascontiguousarray(inp["c"][b].reshape(1, 16, 128).transpose(2, 1, 0)), wmod=wmod, bmodB=bmodB,
                 w_up=inp["ffn_w_up"][layer], cw=cw, cb=cb, w_down=inp["ffn_w_down"][layer])
        if final:
            m["gfin"] = bc(inp["final_norm_g"])
        maps.append({prefix + k_: v for k_, v in m.items()})
    return maps


def sgu_maps(inp, x2, prefix="", with_x=True):
    L = 1

    def mcols(js):
        cols = np.concatenate([np.arange(j * D, (j + 1) * D) for j in js])
        return np.ascontiguousarray(inp["w_mod"][L][:, cols]), bc(inp["b_mod"][L][cols])
    wmA, bmA = mcols((0, 1)); wmB, bmB = mcols((2, 3, 4))
    lngT = np.ascontiguousarray(inp["sgu_ln_g"][0].reshape(32, 128).T); lnbT = np.ascontiguousarray(inp["sgu_ln_b"][0].reshape(32, 128).T)
    wsT = np.ascontiguousarray(inp["sgu_w_s"][0].transpose(2, 0, 1))
    bsB = np.ascontiguousarray(np.broadcast_to(inp["sgu_b_s"][0][None], (128, 16, 128)))
    maps = []
    for c in range(NCORES):
        b, q = c // 4, c % 4
        m = dict(ccT=np.ascontiguousarray(inp["c"][b].reshape(1, 16, 128).transpose(2, 1, 0)),
                 wmodA=wmA, bmodBA=bmA, gMix=bc(inp["mix_norm_g"][L]), wmodB=wmB, bmodBB=bmB, gF=bc(inp["ffn_norm_g"][L]),
                 w_in=inp["sgu_w_in"][0], lngT=lngT, lnbT=lnbT, wsT=wsT, bsB=bsB, w_out=inp["sgu_w_out"][0],
                 ident=np.eye(128, dtype=np.float32), ones=np.ones((128, 128), np.float32))
        if with_x:
            m["x"] = np.ascontiguousarray(x2[b, q * 1024:(q + 1) * 1024])
        maps.append({prefix + k_: v for k_, v in m.items()})
    return maps


def launch_L4(inp, x1, hl0):
    nc = build_L4()
    fm = ffn_maps(inp, x1, hl0, 0, False, "f_")
    sm = sgu_maps(inp, None, "s_", with_x=False)
    maps = [dict(**fm[c], **sm[c]) for c in range(NCORES)]
    res = run(nc, maps)
    x3 = np.zeros((2, 4096, D), np.float32); hl = np.zeros((2, 4096, D), np.float32)
    for c in range(NCORES):
        x3[c // 4, (c % 4) * 1024:(c % 4 + 1) * 1024] = res[c]["s_x3"]
        hl[c // 4, (c % 4) * 1024:(c % 4 + 1) * 1024] = res[c]["s_hl"]
    return x3, hl


def kernel(**inputs):
    inp = {k_: np.asarray(v, dtype=np.float32) for k_, v in inputs.items()}
    u_lat, u_ctx = launch_L1(inp)
    y_f, y_r = launch_L2(inp, u_lat, u_ctx)
    x1, hl0 = launch_L3(inp, y_f, y_r, u_lat, inp["x"], 0, inp["ffn_norm_g"][0], (2, 3, 4))
    x3, hl1 = launch_L4(inp, x1, hl0)
    out = launch_ffn(inp, x3, hl1, 1, True)
    return out.astype(np.float32)
```
